# Optimizing a Trainium2 kernel written in Bass

```python
import math
import jax, jax.numpy as jnp
from jax import lax
import numpy as np

D_MODEL = 1024
BATCH = 2
SEQ = 8192
DEPTH = 2

EPS = 1e-6
N_BRANCH = 3
BRANCH_W = 1024
W_A = BRANCH_W
LRU_BLOCKS = 4
LRU_BLOCK = W_A // LRU_BLOCKS
CONV_W = 4
LRU_C = 8.0
W_B = BRANCH_W
RWKV_HEAD = 64
RWKV_HEADS = W_B // RWKV_HEAD
DECAY_LORA = 64
ICLR_LORA = 64
RWKV_GN_EPS = 64e-5
W_C = BRANCH_W
RET_HEADS = 4
RET_HEAD = W_C // RET_HEADS
RET_CHUNK = 128
ROPE_BASE = 10000.0
RET_GN_EPS = 1e-5
B_COLS = 4 * W_B + DECAY_LORA + ICLR_LORA
C_COLS = 4 * W_C
OFF_GA = W_A
OFF_B = 2 * W_A
OFF_C = OFF_B + B_COLS
OFF_M = OFF_C + C_COLS
N_IN = OFF_M + N_BRANCH * D_MODEL

kernel_name = "hybrid_rglru_rwkv7_retention_gated_merge"


def rms_norm(x, w):
    xf = x.astype(jnp.float32)
    y = xf * lax.rsqrt(jnp.mean(xf * xf, axis=-1, keepdims=True) + EPS)
    return (y * w).astype(x.dtype)


def head_norm(y, eps):
    mu = jnp.mean(y, axis=-1, keepdims=True)
    var = jnp.mean(jnp.square(y - mu), axis=-1, keepdims=True)
    return (y - mu) * lax.rsqrt(var + eps)


def rglru_branch(xa, conv_w, conv_b, gate_w, gate_b, lam):
    bsz, s, _ = xa.shape
    xf = xa.astype(jnp.float32)
    xp = jnp.pad(xf, ((0, 0), (CONV_W - 1, 0), (0, 0)))
    conv = conv_b + sum(xp[:, j:j + s] * conv_w[j] for j in range(CONV_W))
    xb = conv.reshape(bsz, s, LRU_BLOCKS, LRU_BLOCK)
    g = jnp.einsum("bsnc,gncd->gbsnd", xb, gate_w).reshape(2, bsz, s, W_A) + gate_b[:, None, None, :]
    r = jax.nn.sigmoid(g[0])
    i = jax.nn.sigmoid(g[1])
    log_a = -LRU_C * r * jax.nn.softplus(-lam)
    a = jnp.exp(log_a)
    mult = jnp.sqrt(jnp.maximum(-jnp.expm1(2.0 * log_a), 0.0))
    mult = jnp.where((jnp.arange(s) == 0)[None, :, None], 1.0, mult)
    b = mult * i * conv

    def combine(left, right):
        a1, b1 = left
        a2, b2 = right
        return a1 * a2, a2 * b1 + b2

    _, h = lax.associative_scan(combine, (a, b), axis=1)
    return h


def rwkv7_branch(pb, mu, w0, w2, a0, a2, k_k, k_a, r_k, lnx_w, lnx_b):
    bsz, s, _ = pb.shape
    pf = pb.astype(jnp.float32)
    prev = jnp.pad(pf, ((0, 0), (1, 0), (0, 0)))[:, :s]
    pf = pf + (prev - pf) * mu
    r, k, v, g, wl, al = jnp.split(pf, [W_B, 2 * W_B, 3 * W_B, 4 * W_B, 4 * W_B + DECAY_LORA], axis=-1)
    w_log = -jax.nn.softplus(-(w0 + jnp.tanh(wl) @ w2)) - 0.5
    decay = jnp.exp(-jnp.exp(w_log))
    a = jax.nn.sigmoid(a0 + al @ a2)
    hd = lambda t: t.reshape(bsz, s, RWKV_HEADS, RWKV_HEAD)
    kk = hd(k * k_k)
    kk = kk / jnp.maximum(jnp.linalg.norm(kk, axis=-1, keepdims=True), 1e-12)
    k = k * (1.0 + (a - 1.0) * k_a)
    rh, kh, vh, wh, ah = hd(r), hd(k), hd(v), hd(decay), hd(a)

    def step(state, inp):
        r_t, w_t, k_t, v_t, kk_t, a_t = inp
        sa = jnp.einsum("bhvk,bhk->bhv", state, -kk_t)
        state = (state * w_t[:, :, None, :]
                 + sa[..., None] * (kk_t * a_t)[:, :, None, :]
                 + v_t[..., None] * k_t[:, :, None, :])
        y = jnp.einsum("bhvk,bhk->bhv", state, r_t)
        return state, y

    tm = lambda t: jnp.moveaxis(t, 1, 0)
    s0 = jnp.zeros((bsz, RWKV_HEADS, RWKV_HEAD, RWKV_HEAD), jnp.float32)
    _, y = lax.scan(step, s0, (tm(rh), tm(wh), tm(kh), tm(vh), tm(kk), tm(ah)))
    y = jnp.moveaxis(y, 0, 1)
    y = head_norm(y, RWKV_GN_EPS).reshape(bsz, s, W_B) * lnx_w + lnx_b
    bonus = (jnp.sum(rh * kh * r_k, axis=-1, keepdims=True) * vh).reshape(bsz, s, W_B)
    return (y + bonus) * jax.nn.silu(g)


def rotary(t, pos):
    half = t.shape[-1] // 2
    inv_freq = 1.0 / (ROPE_BASE ** jnp.linspace(0.0, 1.0, half, dtype=jnp.float32))
    ang = pos[:, None].astype(jnp.float32) * inv_freq
    cos = jnp.cos(ang)[None, :, None, :]
    sin = jnp.sin(ang)[None, :, None, :]
    t1, t2 = t[..., :half], t[..., half:]
    return jnp.concatenate([t1 * cos - t2 * sin, t1 * sin + t2 * cos], axis=-1)


def retention_branch(pc):
    bsz, s, _ = pc.shape
    nc = s // RET_CHUNK
    q, k, v, g = jnp.split(pc.astype(jnp.float32), 4, axis=-1)
    pos = jnp.arange(s)
    hd = lambda t: t.reshape(bsz, s, RET_HEADS, RET_HEAD)
    q = rotary(hd(q), pos)
    k = rotary(hd(k), pos) * (RET_HEAD ** -0.5)
    v = hd(v)
    log_g = jnp.log1p(-jnp.exp2(-5.0 - jnp.arange(RET_HEADS, dtype=jnp.float32)))
    idx = jnp.arange(RET_CHUNK)
    rel = idx[:, None] - idx[None, :]
    mask = jnp.where(rel >= 0, jnp.exp(log_g[:, None, None] * jnp.maximum(rel, 0)), 0.0)
    ch = lambda t: t.reshape(bsz, nc, RET_CHUNK, RET_HEADS, RET_HEAD)
    qc, kc, vc = ch(q), ch(k), ch(v)
    scores = jnp.einsum("bnihd,bnjhd->bnhij", qc, kc) * mask
    inner = jnp.einsum("bnhij,bnjhd->bnihd", scores, vc)
    q_decay = jnp.exp(log_g[None, :] * (idx[:, None] + 1.0))[None, :, :, None]
    k_decay = jnp.exp(log_g[None, :] * (RET_CHUNK - 1.0 - idx[:, None]))[None, :, :, None]
    chunk_decay = jnp.exp(log_g * RET_CHUNK)[None, :, None, None]

    def step(state, inp):
        qn, kn, vn = inp
        cross = jnp.einsum("bihd,bhde->bihe", qn * q_decay, state)
        state = state * chunk_decay + jnp.einsum("bjhd,bjhe->bhde", kn * k_decay, vn)
        return state, cross

    r0 = jnp.zeros((bsz, RET_HEADS, RET_HEAD, RET_HEAD), jnp.float32)
    tm = lambda t: jnp.moveaxis(t, 1, 0)
    _, cross = lax.scan(step, r0, (tm(qc), tm(kc), tm(vc)))
    out = (inner + jnp.moveaxis(cross, 0, 1)).reshape(bsz, s, RET_HEADS, RET_HEAD)
    out = head_norm(out, RET_GN_EPS).reshape(bsz, s, W_C)
    return out * jax.nn.silu(g)


def hybrid_layer(x, norm_w, w_in, b_merge, conv_w, conv_b, lru_gate_w, lru_gate_b, lru_lambda,
                 shift_mu, decay_w0, decay_w2, iclr_a0, iclr_a2, k_k, k_a, r_k, lnx_w, lnx_b,
                 w_branch, w_out):
    bsz, s, _ = x.shape
    h = rms_norm(x, norm_w)
    p = h @ w_in
    xa, ga, pb, pc, pm = jnp.split(p, [OFF_GA, OFF_B, OFF_C, OFF_M], axis=-1)
    y_a = rglru_branch(xa, conv_w, conv_b, lru_gate_w, lru_gate_b, lru_lambda) * jax.nn.silu(ga.astype(jnp.float32))
    y_b = rwkv7_branch(pb, shift_mu, decay_w0, decay_w2, iclr_a0, iclr_a2, k_k, k_a, r_k, lnx_w, lnx_b)
    y_c = retention_branch(pc)
    ys = jnp.stack([y_a, y_b, y_c], axis=0).astype(x.dtype)
    proj = jnp.einsum("nbsw,nwd->bsnd", ys, w_branch)
    gates = jax.nn.sigmoid((pm + b_merge).reshape(bsz, s, N_BRANCH, D_MODEL).astype(jnp.float32))
    merged = jnp.sum(gates * proj, axis=2).astype(x.dtype)
    return x + merged @ w_out


def setup_inputs(seed: int = 0) -> dict:
    key = jax.random.key(seed)
    ks = jax.random.split(key, 22)
    f32 = jnp.float32
    nrm = lambda k, shp: jax.random.normal(k, shp, f32)
    a_c = jax.random.uniform(ks[8], (DEPTH, W_A), f32, 0.9, 0.999)
    a_base = a_c ** (1.0 / LRU_C)
    return {
        "x": nrm(ks[0], (BATCH, SEQ, D_MODEL)),
        "norm_w": 1.0 + 0.02 * nrm(ks[1], (DEPTH, D_MODEL)),
        "w_in": nrm(ks[2], (DEPTH, D_MODEL, N_IN)) * D_MODEL ** -0.5,
        "b_merge": 0.02 * nrm(ks[3], (DEPTH, N_BRANCH * D_MODEL)),
        "conv_w": nrm(ks[4], (DEPTH, CONV_W, W_A)) * CONV_W ** -0.5,
        "conv_b": 0.02 * nrm(ks[5], (DEPTH, W_A)),
        "lru_gate_w": nrm(ks[6], (DEPTH, 2, LRU_BLOCKS, LRU_BLOCK, LRU_BLOCK)) * LRU_BLOCK ** -0.5,
        "lru_gate_b": 0.02 * nrm(ks[7], (DEPTH, 2, W_A)),
        "lru_lambda": jnp.log(a_base) - jnp.log1p(-a_base),
        "shift_mu": jax.random.uniform(ks[9], (DEPTH, B_COLS), f32),
        "decay_w0": jax.random.uniform(ks[10], (DEPTH, W_B), f32, -6.0, -1.0),
        "decay_w2": nrm(ks[11], (DEPTH, DECAY_LORA, W_B)) * 0.5 * DECAY_LORA ** -0.5,
        "iclr_a0": 0.1 * nrm(ks[12], (DEPTH, W_B)),
        "iclr_a2": nrm(ks[13], (DEPTH, ICLR_LORA, W_B)) * ICLR_LORA ** -0.5,
        "k_k": 0.85 + 0.05 * nrm(ks[14], (DEPTH, W_B)),
        "k_a": 1.0 + 0.05 * nrm(ks[15], (DEPTH, W_B)),
        "r_k": 0.1 * nrm(ks[16], (DEPTH, RWKV_HEADS, RWKV_HEAD)),
        "lnx_w": 1.0 + 0.02 * nrm(ks[17], (DEPTH, W_B)),
        "lnx_b": 0.02 * nrm(ks[18], (DEPTH, W_B)),
        "w_branch": nrm(ks[19], (DEPTH, N_BRANCH, BRANCH_W, D_MODEL)) * BRANCH_W ** -0.5,
        "w_out": nrm(ks[20], (DEPTH, D_MODEL, D_MODEL)) * D_MODEL ** -0.5,
        "final_norm_w": 1.0 + 0.02 * nrm(ks[21], (D_MODEL,)),
    }


def reference(x, norm_w, w_in, b_merge, conv_w, conv_b, lru_gate_w, lru_gate_b, lru_lambda,
              shift_mu, decay_w0, decay_w2, iclr_a0, iclr_a2, k_k, k_a, r_k, lnx_w, lnx_b,
              w_branch, w_out, final_norm_w):
    for l in range(DEPTH):
        x = hybrid_layer(x, norm_w[l], w_in[l], b_merge[l], conv_w[l], conv_b[l], lru_gate_w[l],
                         lru_gate_b[l], lru_lambda[l], shift_mu[l], decay_w0[l], decay_w2[l],
                         iclr_a0[l], iclr_a2[l], k_k[l], k_a[l], r_k[l], lnx_w[l], lnx_b[l],
                         w_branch[l], w_out[l])
    return rms_norm(x, final_norm_w)
```

```python
import numpy as np
import ml_dtypes
import concourse.bass as bass
import concourse.mybir as mybir
from concourse.bass_utils import run_bass_kernel_spmd

F32 = mybir.dt.float32
BF16 = mybir.dt.bfloat16
AF = mybir.ActivationFunctionType
ALU = mybir.AluOpType

D = 1024
DEPTH = 2
EPS = 1e-6
OFF_GA = 1024
OFF_B = 2048
B_COLS = 4 * 1024 + 128
OFF_C = OFF_B + B_COLS
OFF_M = OFF_C + 4096
N_IN = OFF_M + 3072
TB = 512
NCOLA = 2688

ENGS = ("pe", "dve", "act", "pool", "sp")
NDMASEM = 8


class Prog:
    def __init__(self, nc, same_engine_sync=True):
        self.nc = nc
        self.ops = {e: [] for e in ENGS}
        self.csem = {e: nc.alloc_semaphore(name=f"c_{e}") for e in ENGS if e != "sp"}
        self.ccnt = {e: 0 for e in ENGS}
        self.dsem = {e: [nc.alloc_semaphore(name=f"d_{e}{i}") for i in range(NDMASEM)]
                     for e in ("sp", "act", "pool")}
        self.dval = {e: [0] * NDMASEM for e in ("sp", "act", "pool")}
        self.dk = {e: 0 for e in ("sp", "act", "pool")}
        self.waited = {}
        self.lastw = {}
        self.readers = {}
        self.ses = same_engine_sync
        self.nops = 0

    def _need(self, e, tok, out):
        semh, val, src = tok
        if src == e and (e == "pe" or not self.ses):
            return
        if self.waited.get((e, id(semh)), 0) >= val:
            return
        cur = out.get(id(semh))
        if cur is None or cur[1] < val:
            out[id(semh)] = (semh, val)

    def _deps(self, e, reads, writes, par=False):
        out = {}
        for k in reads:
            for t in self.lastw.get(k, ()):
                self._need(e, t, out)
        for k in writes:
            if not par:
                for t in self.lastw.get(k, ()):
                    self._need(e, t, out)
            for t in self.readers.get(k, ()):
                self._need(e, t, out)
        for semh, val in out.values():
            self.ops[e].append(("wait", semh, val))
            self.waited[(e, id(semh))] = val

    def _commit(self, tok, reads, writes, par=False):
        for k in reads:
            self.readers.setdefault(k, []).append(tok)
        for k in writes:
            if par:
                self.lastw.setdefault(k, []).append(tok)
            else:
                self.lastw[k] = [tok]
                self.readers[k] = []

    def op(self, e, meth, *args, reads=(), writes=(), **kw):
        fn = (lambda g: getattr(g, meth)(*args, **kw)) if isinstance(meth, str) else meth
        xr = [k for k in reads if k.startswith("bank") or k == "hps"]
        if xr:
            writes = list(writes) + xr
            reads = [k for k in reads if k not in xr]
        self._deps(e, reads, writes)
        self.ccnt[e] += 1
        tok = (self.csem[e], self.ccnt[e], e)
        self.ops[e].append(("op", fn, self.csem[e], 1))
        self._commit(tok, reads, writes)
        self.nops += 1
        return tok

    def dma(self, e, meth, *args, reads=(), writes=(), par=False, **kw):
        fn = (lambda g: getattr(g, meth)(*args, **kw)) if isinstance(meth, str) else meth
        i = self.dk[e] % NDMASEM
        self.dk[e] += 1
        semh = self.dsem[e][i]
        if self.dval[e][i] > 0:
            k = (e, id(semh))
            if self.waited.get(k, 0) < self.dval[e][i]:
                self.ops[e].append(("wait", semh, self.dval[e][i]))
                self.waited[k] = self.dval[e][i]
        self._deps(e, reads, writes, par)
        self.dval[e][i] += 16
        tok = (semh, self.dval[e][i], "dma_" + e)
        self.ops[e].append(("op", fn, semh, 16))
        self._commit(tok, reads, writes, par)
        self.nops += 1
        return tok

    def wait_tok(self, e, tok):
        semh, val, _ = tok
        k = (e, id(semh))
        if self.waited.get(k, 0) < val:
            self.ops[e].append(("wait", semh, val))
            self.waited[k] = val

    def barrier(self):
        for e in ENGS:
            for e2 in self.csem:
                if e2 != e and self.ccnt[e2] > 0:
                    self.wait_tok(e, (self.csem[e2], self.ccnt[e2], e2))
            for q in self.dsem:
                for i in range(NDMASEM):
                    if self.dval[q][i] > 0:
                        self.wait_tok(e, (self.dsem[q][i], self.dval[q][i], "dma_" + q))

    def finish(self, final_toks):
        for t in final_toks:
            self.wait_tok("sp", t)
        nc = self.nc
        ops = self.ops

        def replay(engine, lst):
            for it in lst:
                if it[0] == "wait":
                    engine.wait_ge(it[1], it[2])
                else:
                    it[1](engine).then_inc(it[2], it[3])

        with nc.Block() as block:
            @block.tensor
            def _(e):
                replay(e, ops["pe"])

            @block.vector
            def _(e):
                replay(e, ops["dve"])

            @block.scalar
            def _(e):
                replay(e, ops["act"])

            @block.gpsimd
            def _(e):
                replay(e, ops["pool"])

            @block.sync
            def _(e):
                replay(e, ops["sp"])


class Cx:
    def __init__(self):
        self.nc = bass.Bass("TRN2", target_bir_lowering=False)
        self.P = Prog(self.nc)
        self.banks = [self.nc.alloc_psum_tensor(f"bank{i}", [128, 512], F32) for i in range(8)]
        self.lo = (int(self.nc.sbuf_base) + 63) // 64 * 64
        self.hi = int(self.nc.sbuf_top)
        self.ptr = self.lo
        self.n = 0

    def reset(self):
        self.ptr = self.lo

    def sb(self, shape, dt=F32, name=None):
        self.n += 1
        nm = f"s{self.n}_" + (name or "t")
        esz = 2 if dt == BF16 else 4
        nbytes = esz
        for v in shape[1:]:
            nbytes *= v
        nbytes = (nbytes + 63) // 64 * 64
        assert self.ptr + nbytes <= self.hi, f"SBUF arena overflow allocating {nm} {shape}: {self.ptr + nbytes - self.hi} over"
        t = self.nc.alloc_sbuf_tensor_at(nm, list(shape), dt, offset=self.ptr)
        self.ptr += nbytes
        return t, nm

    def dram(self, n, s, d=F32, k="ExternalInput"):
        return self.nc.dram_tensor(n, list(s), d, kind=k).ap()

PV = {}
_c = 0
for _nm, _n in [("cw0", 2), ("cw1", 2), ("cw2", 2), ("cw3", 2), ("cb", 2), ("gbr", 2), ("gbi", 2),
                ("lam", 2), ("mu_r", 2), ("mu_k", 2), ("mu_v", 2), ("mu_g", 2), ("mu_wl", 1),
                ("w0", 2), ("a0", 2), ("k_k", 2), ("k_a", 2), ("r_k", 2), ("lnw", 2), ("lnb", 2),
                ("kdec", 1), ("cdec", 1)]:
    PV[_nm] = _c
    _c += _n
NPV = _c
DV = {"omu_r": 0, "omu_k": 2, "omu_v": 4, "omu_g": 6, "omu_wl": 8, "lrs": 9}
NDV = 11

CST = {"ident": 0, "b64": 128, "b64m": 256, "m1": 384, "m2": 640, "rmask": 768, "qdec": 896, "ones": 1408}
NCST = 1408 + 64


def emit_A(cx, S, d, flags=(1, 1, 1)):
    DO_A, DO_B, DO_C = flags
    rstop = 99
    NTB = S // TB
    nc = cx.nc
    P = cx.P
    sb = cx.sb
    x_d, gw_d, w2p_d, a2p_d, pv_d, nwb_d, cst_d, cos_d, sin_d, ys_d = (d[k] for k in
        ("x", "gw", "w2p", "a2p", "pv", "nwb", "cst", "cosT", "sinT", "ysT"))
    out_toks = []

    def dump(*a, **k):
        return

    banks = cx.banks
    big_i = [0]

    def _nb():
        i = big_i[0] % 7
        big_i[0] += 1
        return i

    def big():
        i = _nb()
        return banks[i], f"bank{i}"

    def quarter():
        i = _nb()
        return banks[i][:, 0:128], f"bank{i}"

    def half():
        i = _nb()
        return banks[i][:, 0:256], [f"bank{i}"]
    hps = banks[7]

    wA, k_wA = sb([128, 8, NCOLA], BF16, "wA")
    gw, k_gw = sb([128, 2, 2, 256], BF16, "gw")
    w2p, k_w2p = sb([128, 256], BF16, "w2p")
    a2p, k_a2p = sb([128, 256], BF16, "a2p")
    pv, k_pv = sb([128, NPV], F32, "pv")
    dv, k_dv = sb([128, NDV], F32, "dv")
    nwb, k_nwb = sb([128, D], F32, "nwb")
    cst, k_cst = sb([128, NCST], F32, "cst")
    cstb, k_cstb = sb([128, 640 + 128], BF16, "cstb")

    for c0, piece in d["wA_cols"]:
        npc = piece.shape[1]
        pv_ = piece.rearrange("(kc p) n -> p kc n", p=128)
        for kc in range(0, 8, 4):
            P.dma("pool", "dma_start", out=wA[:, kc:kc + 4, c0:c0 + npc], in_=pv_[:, kc:kc + 4, :], writes=[k_wA], par=True)
    P.dma("pool", "dma_start", out=gw[:], in_=gw_d.rearrange("n (cc p) d -> p n cc d", p=128), writes=[k_gw])
    P.dma("pool", "dma_start", out=w2p[:], in_=w2p_d, writes=[k_w2p])
    P.dma("pool", "dma_start", out=a2p[:], in_=a2p_d, writes=[k_a2p])
    P.dma("sp", "dma_start", out=pv[:], in_=pv_d, writes=[k_pv])
    P.dma("sp", "dma_start", out=nwb[:], in_=nwb_d, writes=[k_nwb])
    P.dma("sp", "dma_start", out=cst[:], in_=cst_d, writes=[k_cst])
    P.op("dve", "tensor_copy", out=cstb[:, 0:128], in_=cst[:, 0:128], reads=[k_cst], writes=[k_cstb])
    identb = cstb[:, 0:128]
    ident_f = cst[:, CST["ident"]:CST["ident"] + 128]
    b64 = cst[:, CST["b64"]:CST["b64"] + 128]
    b64m = cst[:, CST["b64m"]:CST["b64m"] + 128]
    m1 = cst[:, CST["m1"]:CST["m1"] + 256]
    m2 = cst[:, CST["m2"]:CST["m2"] + 128]
    rmask = cst[:, CST["rmask"]:CST["rmask"] + 128]
    qdec = cst[:, CST["qdec"]:CST["qdec"] + 512]
    ones64 = cst[:, CST["ones"]:CST["ones"] + 64]

    def pvc(nm, j=0):
        c = PV[nm] + j
        return pv[:, c:c + 1]

    def dvc(nm, j=0):
        c = DV[nm] + j
        return dv[:, c:c + 1]

    P.op("dve", "tensor_scalar", out=dv[:, 0:9], in0=pv[:, PV["mu_r"]:PV["mu_r"] + 9], scalar1=-1.0, scalar2=1.0,
                                          op0=ALU.mult, op1=ALU.add, reads=[k_pv], writes=[k_dv])
    tl, k_tl = sb([128, 2], F32, "tl")
    P.op("act", "activation", out=tl[:], in_=pv[:, PV["lam"]:PV["lam"] + 2], func=AF.Exp, scale=-1.0, reads=[k_pv], writes=[k_tl])
    P.op("act", "activation", out=tl[:], in_=tl[:], func=AF.Ln, bias=1.0, reads=[k_tl], writes=[k_tl])
    P.op("dve", "tensor_scalar", out=dv[:, 9:11], in0=tl[:], scalar1=-8.0, scalar2=None, op0=ALU.mult, reads=[k_tl, k_dv], writes=[k_dv])

    xt = [sb([128, D], F32, f"xt{i}") for i in range(2)]
    junk, k_junk = sb([128, D], BF16, "junk")
    ss = [sb([128, 4], F32, f"ss{i}") for i in range(2)]
    xs = [sb([128, D], BF16, f"xs{i}") for i in range(2)]
    hT = [sb([128, 8, TB], BF16, f"hT{i}") for i in range(1)]
    hps_b = hps[:].bitcast(BF16)
    pool = [sb([128, TB], F32, f"pl{i}") for i in range(20)]

    xa = [sb([128, TB + 3], F32, f"xa{i}") for i in range(2)]
    cv = [pool[0], pool[1]]
    lr = [pool[2], pool[3]]
    li = [pool[4], pool[5]]
    la, k_la = pool[6]
    lt, k_lt = pool[7]
    lh, k_lh = pool[8]
    sga = [pool[9], pool[10]]
    cvb = [sb([128, TB], BF16, f"cvb{i}") for i in range(2)]
    hcar = [sb([128, 1], F32, f"hcar{i}") for i in range(2)]
    ysA = [sb([128, TB], BF16, f"ysA{i}") for i in range(2)]
    for i in range(2):
        P.op("pool", "memset", xa[i][0][:, 0:3], 0.0, writes=[xa[i][1]])
        P.op("pool", "memset", hcar[i][0][:], 0.0, writes=[hcar[i][1]])

    pB = {nm: [sb([128, TB + 1], F32, f"pB{nm}{i}") for i in range(2)] for nm in "rkvg"}
    pWL, k_pWL = sb([128, TB + 1], F32, "pWL")
    P.op("pool", "memset", pWL[:, 0:1], 0.0, writes=[k_pWL])
    for nm in "rkvg":
        for i in range(2):
            P.op("pool", "memset", pB[nm][i][0][:, 0:1], 0.0, writes=[pB[nm][i][1]])
    sWL, k_sWL = pool[19]
    tnh, k_tnh = sb([128, TB], BF16, "tnh")
    sWLb, k_sWLb = sb([128, TB], BF16, "sWLb")
    sS = {nm: pool[i] for i, nm in enumerate("rkvg")}
    f = {nm: pool[4 + i] for i, nm in enumerate(
        ["logw", "cum", "Wt", "iW", "Wp", "kk", "sq", "rn", "kmod", "kka", "iWC", "bon", "sgg", "yT", "aa"])}
    BD_AR = [sb([128, 8, 2, 128], BF16, f"BD_AR{i}") for i in range(1)] * 2
    BDn = {nm: [sb([128, 8, 128], BF16, f"BD_{nm}{i}") for i in range(1)] * 2 for nm in ["B", "K", "Bh", "Kh", "V"]}
    P.op("pool", "memset", BD_AR[0][0][:], 0.0, writes=[BD_AR[0][1]])
    for nm in BDn:
        P.op("pool", "memset", BDn[nm][0][0][:], 0.0, writes=[BDn[nm][0][1]])
    S0f = [sb([128, 128], F32, f"S0f{i}") for i in range(2)]
    S0b = [sb([128, 128], BF16, f"S0b{i}") for i in range(2)]
    for i in range(2):
        P.op("pool", "memset", S0f[i][0][:], 0.0, writes=[S0f[i][1]])
        P.op("pool", "memset", S0b[i][0][:], 0.0, writes=[S0b[i][1]])
    NM1 = [sb([128, 256], BF16, f"NM1_{i}") for i in range(2)]
    NM2 = [sb([128, 256], BF16, f"NM2_{i}") for i in range(2)]
    Lm = [sb([128, 128], BF16, f"Lm{i}") for i in range(4)]
    Nm = [sb([128, 128], BF16, f"Nm{i}") for i in range(4)]
    Pm = [sb([128, 128], BF16, f"Pm{i}") for i in range(4)]
    TK = [sb([128, 3, 128], BF16, f"TK{i}") for i in range(2)]
    Zb = [sb([128, 128], BF16, f"Zb{i}") for i in range(2)]
    Ub = [sb([128, 128], BF16, f"Ub{i}") for i in range(2)]
    ysB = [sb([128, TB], BF16, f"ysB{i}") for i in range(2)]

    qf = [pool[0], pool[1]]
    kf = [pool[2], pool[3]]
    cosb, k_cos = pool[4]
    sinb, k_sin = pool[5]
    rt = [pool[6 + i] for i in range(4)]
    qT = [sb([128, TB], BF16, f"qT{i}") for i in range(2)]
    kT = [sb([128, TB], BF16, f"kT{i}") for i in range(2)]
    qd = [sb([128, TB], BF16, f"qd{i}") for i in range(2)]
    v_tok = [sb([128, 256], BF16, f"vtok{i}") for i in range(4)]
    sg_tok = [sb([128, 256], F32, f"sgtok{i}") for i in range(4)]
    STm = [sb([128, 128], BF16, f"ST{i}") for i in range(2)]
    kdt = [sb([128, 256], BF16, f"kdt{i}") for i in range(2)]
    stf = [sb([128, 256], F32, f"stf{i}") for i in range(2)]
    stb = [sb([128, 256], BF16, f"stb{i}") for i in range(2)]
    for i in range(2):
        P.op("pool", "memset", stf[i][0][:], 0.0, writes=[stf[i][1]])
        P.op("pool", "memset", stb[i][0][:], 0.0, writes=[stb[i][1]])
    st6 = [sb([128, 6], F32, f"st6_{i}") for i in range(2)]
    mv = [sb([128, 4], F32, f"mv{i}") for i in range(2)]
    yn = [sb([128, 256], F32, f"yn{i}") for i in range(2)]
    ycb = [sb([128, 256], BF16, f"ycb{i}") for i in range(2)]
    ysC = [sb([128, TB], BF16, f"ysC{i}") for i in range(2)]

    cp_i = [0]

    def copy_eng():
        cp_i[0] += 1
        return "act" if cp_i[0] % 2 else "dve"

    def evac(e, out, in_, reads, writes, scale=None):
        if e == "act":
            if scale is None:
                P.op("act", "activation", out=out, in_=in_, func=AF.Copy, reads=reads, writes=writes)
            else:
                P.op("act", "activation", out=out, in_=in_, func=AF.Copy, scale=scale, reads=reads, writes=writes)
        else:
            if scale is None:
                P.op(e, "tensor_copy", out=out, in_=in_, reads=reads, writes=writes)
            else:
                P.op(e, "tensor_scalar", out=out, in0=in_, scalar1=scale, scalar2=None, op0=ALU.mult, reads=reads, writes=writes)

    for tb in range(NTB):
        t0 = tb * TB
        hTt, k_hT = hT[0]
        for tt in range(4):
            xb_, k_x = xt[tt % 2]
            ssb, k_ss = ss[tt % 2]
            xsb, k_xs = xs[tt % 2]
            r0 = t0 + tt * 128
            P.dma("sp", "dma_start", out=xb_[:], in_=x_d[r0:r0 + 128, :], writes=[k_x])
            P.op("act", "activation", out=junk[:], in_=xb_[:], func=AF.Square, accum_out=ssb[:, 0:1],
                 reads=[k_x], writes=[k_junk, k_ss])
            P.op("dve", "tensor_scalar", out=ssb[:, 1:2], in0=ssb[:, 0:1], scalar1=1.0 / D, scalar2=EPS, op0=ALU.mult, op1=ALU.add,
                 reads=[k_ss], writes=[k_ss])
            P.op("act", "activation", out=ssb[:, 2:3], in_=ssb[:, 1:2], func=AF.Sqrt, reads=[k_ss], writes=[k_ss])
            P.op("dve", "reciprocal", out=ssb[:, 3:4], in_=ssb[:, 2:3], reads=[k_ss], writes=[k_ss])
            P.op("dve", "scalar_tensor_tensor", out=xsb[:], in0=xb_[:], scalar=ssb[:, 3:4], in1=nwb[:],
                                                                                 op0=ALU.mult, op1=ALU.mult,
                 reads=[k_x, k_ss, k_nwb], writes=[k_xs])
            for kc in range(8):
                P.op("pe", "transpose", hps_b[:, kc * 128:(kc + 1) * 128], xsb[:, kc * 128:(kc + 1) * 128], identb,
                     reads=[k_xs, k_cstb], writes=["hps"])
            P.op("act", "activation", out=hTt[:, :, tt * 128:(tt + 1) * 128],
                                                             in_=hps_b.rearrange("p (k t) -> p k t", t=128), func=AF.Copy,
                 reads=["hps"], writes=[k_hT])

        def proj_fm(cb):
            bk, k_bk = big()
            for kc in range(8):
                P.op("pe", "matmul", bk[:], lhsT=wA[:, kc, cb * 128:(cb + 1) * 128], rhs=hTt[:, kc, :],
                                                           start=(kc == 0), stop=(kc == 7),
                     reads=[k_wA, k_hT], writes=[k_bk])
            return bk, k_bk

        if tb == 0:
            dump("hT", hTt[:, 0, :], k_hT)
            dump("xs", xs[1][0][:, 0:TB], xs[1][1])
            dump("ss", ss[1][0][:, 0:4], ss[1][1], 4)
        if DO_A:
            for pt in range(2):
                bk, k_bk = proj_fm(pt)
                evac("act", xa[pt][0][:, 3:TB + 3], bk[:], [k_bk], [xa[pt][1]])
                if tb == 0 and pt == 0:
                    dump("xa", xa[0][0][:, 3:TB + 3], xa[0][1])
                bk, k_bk = proj_fm(2 + pt)
                P.op("act", "activation", out=sga[pt][0][:], in_=bk[:], func=AF.Silu, reads=[k_bk], writes=[sga[pt][1]])
        if DO_B:
            for j, nm in enumerate("rkvg"):
                for pt in range(2):
                    bk, k_bk = proj_fm(4 + 2 * j + pt)
                    evac(copy_eng(), pB[nm][pt][0][:, 1:TB + 1], bk[:], [k_bk], [pB[nm][pt][1]])
            bk, k_bk = proj_fm(12)
            evac(copy_eng(), pWL[:, 1:TB + 1], bk[:], [k_bk], [k_pWL])
        if DO_A:
            for pt in range(2):
                xat, k_xa = xa[pt]
                cvt, k_cv = cv[pt]
                P.op("dve", "tensor_scalar", out=cvt[:], in0=xat[:, 3:TB + 3], scalar1=pvc("cw3", pt), scalar2=pvc("cb", pt),
                                                                             op0=ALU.mult, op1=ALU.add, reads=[k_xa, k_pv], writes=[k_cv])
                for j in range(3):
                    P.op("dve", "scalar_tensor_tensor", out=cvt[:], in0=xat[:, j:j + TB], scalar=pvc(f"cw{j}", pt), in1=cvt[:],
                                                                                             op0=ALU.mult, op1=ALU.add, reads=[k_xa, k_cv, k_pv], writes=[k_cv])
                P.op("pool", "tensor_copy", out=xat[:, 0:3], in_=xat[:, TB:TB + 3], reads=[k_xa], writes=[k_xa])
                P.op("pool", "tensor_copy", out=cvb[pt][0][:], in_=cvt[:], reads=[k_cv], writes=[cvb[pt][1]])
            for n in range(2):
                for dblk in range(2):
                    bk, k_bk = big()
                    for cc in range(2):
                        P.op("pe", "matmul", bk[:], lhsT=gw[:, n, cc, dblk * 128:(dblk + 1) * 128], rhs=cvb[cc][0][:],
                                                                                   start=(cc == 0), stop=(cc == 1),
                             reads=[k_gw, cvb[cc][1]], writes=[k_bk])
                    dst = (lr if n == 0 else li)[dblk]
                    P.op("act", "activation", out=dst[0][:], in_=bk[:], func=AF.Sigmoid,
                                                                                      bias=pvc("gbr" if n == 0 else "gbi", dblk),
                         reads=[k_bk, k_pv], writes=[dst[1]])
            for pt in range(2):
                cvt, k_cv = cv[pt]
                P.op("act", "activation", out=la[:], in_=lr[pt][0][:], func=AF.Exp, scale=dvc("lrs", pt), reads=[lr[pt][1], k_dv], writes=[k_la])
                P.op("dve", "scalar_tensor_tensor", out=lt[:], in0=la[:], scalar=-1.0, in1=la[:], op0=ALU.mult, op1=ALU.mult, reads=[k_la], writes=[k_lt])
                P.op("dve", "tensor_scalar", out=lt[:], in0=lt[:], scalar1=1.0, scalar2=0.0, op0=ALU.add, op1=ALU.max, reads=[k_lt], writes=[k_lt])
                P.op("act", "activation", out=lt[:], in_=lt[:], func=AF.Sqrt, reads=[k_lt], writes=[k_lt])
                if tb == 0:
                    P.op("dve", "memset", lt[:, 0:1], 1.0, reads=[k_lt], writes=[k_lt])
                P.op("dve", "tensor_tensor", out=lt[:], in0=lt[:], in1=li[pt][0][:], op=ALU.mult, reads=[k_lt, li[pt][1]], writes=[k_lt])
                P.op("dve", "tensor_tensor", out=lt[:], in0=lt[:], in1=cvt[:], op=ALU.mult, reads=[k_lt, k_cv], writes=[k_lt])
                P.op("dve", "tensor_tensor_scan", out=lh[:], data0=la[:], data1=lt[:], initial=hcar[pt][0][:, 0:1], op0=ALU.mult, op1=ALU.add,
                     reads=[k_la, k_lt, hcar[pt][1]], writes=[k_lh])
                P.op("dve", "tensor_copy", out=hcar[pt][0][:], in_=lh[:, TB - 1:TB], reads=[k_lh], writes=[hcar[pt][1]])
                if tb == 0 and pt == 0:
                    dump("cv", cv[0][0][:], cv[0][1]); dump("lr", lr[0][0][:], lr[0][1]); dump("li", li[0][0][:], li[0][1])
                    dump("la", la[:], k_la); dump("lt", lt[:], k_lt); dump("lh", lh[:], k_lh); dump("sga", sga[0][0][:], sga[0][1])
                P.op("dve", "tensor_tensor", out=ysA[pt][0][:], in0=lh[:], in1=sga[pt][0][:], op=ALU.mult, reads=[k_lh, sga[pt][1]], writes=[ysA[pt][1]])
                out_toks.append(P.dma("sp", "dma_start", out=ys_d[0, pt * 128:(pt + 1) * 128, t0:t0 + TB], in_=ysA[pt][0][:],
                                      reads=[ysA[pt][1]]))

        if DO_B:
            def shift(buf, k_buf, dst, k_dst, mu, omu):
                P.op("dve", "tensor_scalar", out=dst[:], in0=buf[:, 0:TB], scalar1=mu, scalar2=None, op0=ALU.mult,
                     reads=[k_buf, k_pv], writes=[k_dst])
                P.op("dve", "scalar_tensor_tensor", out=dst[:], in0=buf[:, 1:TB + 1], scalar=omu, in1=dst[:], op0=ALU.mult, op1=ALU.add,
                     reads=[k_buf, k_dst, k_dv], writes=[k_dst])
                P.op("pool", "tensor_copy", out=buf[:, 0:1], in_=buf[:, TB:TB + 1], reads=[k_buf], writes=[k_buf])

            shift(pWL, k_pWL, sWL, k_sWL, pvc("mu_wl"), dvc("omu_wl"))
            P.op("act", "activation", out=tnh[:], in_=sWL[:], func=AF.Tanh, reads=[k_sWL], writes=[k_tnh])
            P.op("pool", "tensor_copy", out=sWLb[:], in_=sWL[:], reads=[k_sWL], writes=[k_sWLb])
            for pt in range(2):
                F_ = lambda nm: f[nm][0]
                K_ = lambda nm: f[nm][1]
                for nm in "rkvg":
                    shift(pB[nm][pt][0], pB[nm][pt][1], sS[nm][0], sS[nm][1], pvc("mu_" + nm, pt), dvc("omu_" + nm, pt))
                s_r, s_k, s_v, s_g = (sS[nm][0] for nm in "rkvg")
                ksr, ksk, ksv, ksg = (sS[nm][1] for nm in "rkvg")
                bk, k_bk = big()
                P.op("pe", "matmul", bk[:], lhsT=w2p[:, pt * 128:(pt + 1) * 128], rhs=tnh[:], start=True, stop=True,
                     reads=[k_w2p, k_tnh], writes=[k_bk])
                P.op("act", "activation", out=F_("logw")[:], in_=bk[:], func=AF.Sigmoid, bias=pvc("w0", pt), reads=[k_bk, k_pv], writes=[K_("logw")])
                bk, k_bk = big()
                P.op("pe", "matmul", bk[:], lhsT=a2p[:, pt * 128:(pt + 1) * 128], rhs=sWLb[:], start=True, stop=True,
                     reads=[k_a2p, k_sWLb], writes=[k_bk])
                P.op("act", "activation", out=F_("aa")[:], in_=bk[:], func=AF.Sigmoid, bias=pvc("a0", pt), reads=[k_bk, k_pv], writes=[K_("aa")])
                P.op("dve", "tensor_scalar", out=F_("logw")[:], in0=F_("logw")[:], scalar1=-0.6065306597126334, scalar2=None, op0=ALU.mult,
                     reads=[K_("logw")], writes=[K_("logw")])
                for ch in range(8):
                    P.op("dve", "tensor_tensor_scan", out=F_("cum")[:, ch * 64:(ch + 1) * 64], data0=ones64, data1=F_("logw")[:, ch * 64:(ch + 1) * 64],
                                                                     initial=0.0, op0=ALU.mult, op1=ALU.add, reads=[K_("logw"), k_cst], writes=[K_("cum")])
                P.op("act", "activation", out=F_("Wt")[:], in_=F_("cum")[:], func=AF.Exp, reads=[K_("cum")], writes=[K_("Wt")])
                P.op("act", "activation", out=F_("iW")[:], in_=F_("cum")[:], func=AF.Exp, scale=-1.0, reads=[K_("cum")], writes=[K_("iW")])
                P.op("dve", "tensor_tensor", out=F_("Wp")[:], in0=F_("cum")[:], in1=F_("logw")[:], op=ALU.subtract, reads=[K_("cum"), K_("logw")], writes=[K_("Wp")])
                P.op("act", "activation", out=F_("Wp")[:], in_=F_("Wp")[:], func=AF.Exp, reads=[K_("Wp")], writes=[K_("Wp")])
                P.op("dve", "tensor_scalar", out=F_("kk")[:], in0=s_k[:], scalar1=pvc("k_k", pt), scalar2=None, op0=ALU.mult,
                     reads=[ksk, k_pv], writes=[K_("kk")])
                P.op("pool", "tensor_tensor", out=F_("sq")[:], in0=F_("kk")[:], in1=F_("kk")[:], op=ALU.mult, reads=[K_("kk")], writes=[K_("sq")])
                bk, k_bk = big()
                P.op("pe", "matmul", bk[:], lhsT=b64, rhs=F_("sq")[:], start=True, stop=True, reads=[k_cst, K_("sq")], writes=[k_bk])
                P.op("act", "activation", out=F_("rn")[:], in_=bk[:], func=AF.Sqrt, reads=[k_bk], writes=[K_("rn")])
                P.op("dve", "tensor_scalar", out=F_("rn")[:], in0=F_("rn")[:], scalar1=1e-12, scalar2=None, op0=ALU.max, reads=[K_("rn")], writes=[K_("rn")])
                P.op("dve", "reciprocal", out=F_("rn")[:], in_=F_("rn")[:], reads=[K_("rn")], writes=[K_("rn")])
                P.op("dve", "tensor_tensor", out=F_("kk")[:], in0=F_("kk")[:], in1=F_("rn")[:], op=ALU.mult, reads=[K_("kk"), K_("rn")], writes=[K_("kk")])
                P.op("dve", "tensor_scalar", out=F_("kmod")[:], in0=F_("aa")[:], scalar1=-1.0, scalar2=pvc("k_a", pt), op0=ALU.add, op1=ALU.mult,
                     reads=[K_("aa"), k_pv], writes=[K_("kmod")])
                P.op("dve", "scalar_tensor_tensor", out=F_("kmod")[:], in0=F_("kmod")[:], scalar=1.0, in1=s_k[:], op0=ALU.add, op1=ALU.mult,
                     reads=[K_("kmod"), ksk], writes=[K_("kmod")])
                P.op("pool", "tensor_tensor", out=F_("kka")[:], in0=F_("kk")[:], in1=F_("aa")[:], op=ALU.mult, reads=[K_("kk"), K_("aa")], writes=[K_("kka")])
                P.op("dve", "tensor_tensor", out=F_("iWC")[:].rearrange("p (c t) -> p c t", t=64), in0=F_("iW")[:].rearrange("p (c t) -> p c t", t=64),
                                                      in1=F_("Wt")[:].rearrange("p (c t) -> p c t", t=64)[:, :, 63:64].broadcast_to([128, 8, 64]), op=ALU.mult,
                     reads=[K_("iW"), K_("Wt")], writes=[K_("iWC")])
                bdar, k_bdar = BD_AR[pt]
                v3 = lambda t, hh: t[hh * 64:(hh + 1) * 64, :].rearrange("p (c t) -> p c t", t=64)
                for hh in range(2):
                    hs_ = slice(hh * 64, (hh + 1) * 64)
                    P.op("dve", "scalar_tensor_tensor", out=bdar[hs_, :, 0, hh * 64:(hh + 1) * 64], in0=v3(F_("kk"), hh), scalar=-1.0,
                                                                                 in1=v3(F_("Wp"), hh), op0=ALU.mult, op1=ALU.mult,
                         reads=[K_("kk"), K_("Wp")], writes=[k_bdar])
                    P.op("pool", "tensor_tensor", out=bdar[hs_, :, 1, hh * 64:(hh + 1) * 64], in0=v3(s_r, hh), in1=v3(F_("Wt"), hh), op=ALU.mult,
                         reads=[ksr, K_("Wt")], writes=[k_bdar])
                    for nm, a_, b_, eng in [("B", "kka", "iW", "dve"), ("K", "kmod", "iW", "pool"), ("Bh", "kka", "iWC", "dve"), ("Kh", "kmod", "iWC", "pool")]:
                        P.op(eng, "tensor_tensor", out=BDn[nm][pt][0][hs_, :, hh * 64:(hh + 1) * 64], in0=v3(F_(a_), hh),
                                                                                                in1=v3(F_(b_), hh), op=ALU.mult,
                             reads=[K_(a_), K_(b_)], writes=[BDn[nm][pt][1]])
                    P.op("act", "activation", out=BDn["V"][pt][0][hs_, :, hh * 64:(hh + 1) * 64], in_=v3(s_v, hh), func=AF.Copy,
                         reads=[ksv], writes=[BDn["V"][pt][1]])
                P.op("dve", "scalar_tensor_tensor", out=F_("sq")[:], in0=s_r[:], scalar=pvc("r_k", pt), in1=F_("kmod")[:], op0=ALU.mult, op1=ALU.mult,
                     reads=[ksr, K_("kmod"), k_pv], writes=[K_("sq")])
                bk, k_bk = big()
                P.op("pe", "matmul", bk[:], lhsT=b64, rhs=F_("sq")[:], start=True, stop=True, reads=[k_cst, K_("sq")], writes=[k_bk])
                P.op("dve", "tensor_tensor", out=F_("bon")[:], in0=bk[:], in1=s_v[:], op=ALU.mult, reads=[k_bk, ksv], writes=[K_("bon")])
                P.op("act", "activation", out=F_("sgg")[:], in_=s_g[:], func=AF.Silu, reads=[ksg], writes=[K_("sgg")])

                s0f, k_s0f = S0f[pt]
                s0b, k_s0b = S0b[pt]
                bB, k_bB = BDn["B"][pt]
                bK, k_bK = BDn["K"][pt]
                bBh, k_bBh = BDn["Bh"][pt]
                bKh, k_bKh = BDn["Kh"][pt]
                bV, k_bV = BDn["V"][pt]
                for ch in range(8):
                    nm1, k_nm1 = NM1[ch % 2]
                    nm2, k_nm2 = NM2[ch % 2]
                    ps, k_ps = half()
                    P.op("pe", "matmul", ps, lhsT=bB[:, ch, :], rhs=bdar[:, ch, :, :].rearrange("p a b -> p (a b)"), start=True, stop=True,
                         reads=[k_bB, k_bdar], writes=k_ps)
                    P.op("dve", "tensor_tensor", out=nm1[:], in0=ps, in1=m1, op=ALU.mult, reads=k_ps + [k_cst], writes=[k_nm1])
                    ps, k_ps = half()
                    P.op("pe", "matmul", ps, lhsT=bK[:, ch, :], rhs=bdar[:, ch, :, :].rearrange("p a b -> p (a b)"), start=True, stop=True,
                         reads=[k_bK, k_bdar], writes=k_ps)
                    P.op("dve", "tensor_tensor", out=nm2[:], in0=ps, in1=m1, op=ALU.mult, reads=k_ps + [k_cst], writes=[k_nm2])
                    lcur, k_lcur = Lm[0]
                    ps, k_ps = quarter()
                    P.op("pe", "matmul", ps, lhsT=bdar[:, ch, 0, :], rhs=bB[:, ch, :], start=True, stop=True, reads=[k_bdar, k_bB], writes=[k_ps])
                    P.op("dve", "tensor_tensor", out=lcur[:], in0=ps, in1=m2, op=ALU.mult, reads=[k_ps, k_cst], writes=[k_lcur])
                    ncur, k_ncur = nm1[:, 0:128], k_nm1
                    pcur, k_pcur = Pm[0]
                    P.op("pool", "tensor_tensor", out=pcur[:], in0=nm1[:, 0:128], in1=ident_f, op=ALU.add, reads=[k_nm1, k_cst], writes=[k_pcur])
                    for j in range(1, 6):
                        lnew, k_lnew = Lm[j % 4]
                        ps, k_ps = quarter()
                        P.op("pe", "matmul", ps, lhsT=ncur, rhs=lcur[:], start=True, stop=True, reads=[k_ncur, k_lcur], writes=[k_ps])
                        evac("act", lnew[:], ps, [k_ps], [k_lnew])
                        if j <= 4:
                            nnew, k_nnew = Nm[j % 4]
                            ps2, k_ps2 = quarter()
                            P.op("pe", "matmul", ps2, lhsT=lcur[:], rhs=ncur, start=True, stop=True, reads=[k_ncur, k_lcur], writes=[k_ps2])
                            evac("act", nnew[:], ps2, [k_ps2], [k_nnew])
                        pnew, k_pnew = Pm[j % 4]
                        ps3, k_ps3 = quarter()
                        P.op("pe", "matmul", ps3, lhsT=lnew[:], rhs=pcur[:], start=True, stop=True, reads=[k_lnew, k_pcur], writes=[k_ps3])
                        P.op("dve", "tensor_tensor", out=pnew[:], in0=ps3, in1=pcur[:], op=ALU.add, reads=[k_ps3, k_pcur], writes=[k_pnew])
                        lcur, k_lcur = lnew, k_lnew
                        if j <= 4:
                            ncur, k_ncur = nnew[:], k_nnew
                        pcur, k_pcur = pnew, k_pnew
                    tk, k_tk = TK[ch % 2]
                    psh, k_psh = half()
                    psh_b = psh.bitcast(BF16)
                    for i3, (src, ks) in enumerate([(bV, k_bV), (bBh, k_bBh), (bKh, k_bKh)]):
                        P.op("pe", "transpose", psh_b[:, i3 * 128:(i3 + 1) * 128], src[:, ch, :], identb,
                             reads=[ks, k_cstb], writes=k_psh)
                    P.op("act", "activation", out=tk[:].rearrange("p a b -> p (a b)"), in_=psh_b[:, 0:384], func=AF.Copy, reads=k_psh, writes=[k_tk])
                    zb, k_zb = Zb[ch % 2]
                    ub, k_ub = Ub[ch % 2]
                    ps, k_ps = quarter()
                    P.op("pe", "matmul", ps, lhsT=bdar[:, ch, 0, :], rhs=s0b[:], start=True, stop=False, reads=[k_bdar, k_s0b], writes=[k_ps])
                    P.op("pe", "matmul", ps, lhsT=nm2[:, 0:128], rhs=tk[:, 0, :], start=False, stop=True, reads=[k_nm2, k_tk], writes=[k_ps])
                    evac("dve", zb[:], ps, [k_ps], [k_zb])
                    ps, k_ps = quarter()
                    P.op("pe", "matmul", ps, lhsT=pcur[:], rhs=zb[:], start=True, stop=True, reads=[k_pcur, k_zb], writes=[k_ps])
                    evac("act", ub[:], ps, [k_ps], [k_ub])
                    ps, k_ps = quarter()
                    P.op("pe", "matmul", ps, lhsT=s0b[:], rhs=bdar[:, ch, 1, :], start=True, stop=False, reads=[k_bdar, k_s0b], writes=[k_ps])
                    P.op("pe", "matmul", ps, lhsT=ub[:], rhs=nm1[:, 128:256], start=False, stop=False, reads=[k_ub, k_nm1], writes=[k_ps])
                    P.op("pe", "matmul", ps, lhsT=tk[:, 0, :], rhs=nm2[:, 128:256], start=False, stop=True, reads=[k_tk, k_nm2], writes=[k_ps])
                    P.op("dve", "tensor_copy", out=F_("yT")[0:64, ch * 64:(ch + 1) * 64], in_=ps[0:64, 0:64], reads=[k_ps], writes=[K_("yT")])
                    P.op("act", "activation", out=F_("yT")[64:128, ch * 64:(ch + 1) * 64], in_=ps[64:128, 64:128], func=AF.Copy, reads=[k_ps], writes=[K_("yT")])
                    ps, k_ps = quarter()
                    P.op("pe", "matmul", ps, lhsT=tk[:, 1, :], rhs=ub[:], start=True, stop=False, reads=[k_tk, k_ub], writes=[k_ps])
                    P.op("pe", "matmul", ps, lhsT=tk[:, 2, :], rhs=tk[:, 0, :], start=False, stop=True, reads=[k_tk], writes=[k_ps])
                    P.op("dve", "scalar_tensor_tensor", out=s0f[:], in0=s0f[:], scalar=F_("Wt")[:, ch * 64 + 63:ch * 64 + 64], in1=ps, op0=ALU.mult, op1=ALU.add,
                         reads=[k_s0f, K_("Wt"), k_ps], writes=[k_s0f])
                    P.op("act", "activation", out=s0b[:], in_=s0f[:], func=AF.Copy, reads=[k_s0f], writes=[k_s0b])
                bk, k_bk = big()
                P.op("pe", "matmul", bk[:], lhsT=b64m, rhs=F_("yT")[:], start=True, stop=True, reads=[k_cst, K_("yT")], writes=[k_bk])
                P.op("dve", "tensor_tensor", out=F_("yT")[:], in0=F_("yT")[:], in1=bk[:], op=ALU.subtract, reads=[k_bk, K_("yT")], writes=[K_("yT")])
                P.op("pool", "tensor_tensor", out=F_("sq")[:], in0=F_("yT")[:], in1=F_("yT")[:], op=ALU.mult, reads=[K_("yT")], writes=[K_("sq")])
                bk, k_bk = big()
                P.op("pe", "matmul", bk[:], lhsT=b64m, rhs=F_("sq")[:], start=True, stop=True, reads=[k_cst, K_("sq")], writes=[k_bk])
                P.op("dve", "tensor_scalar", out=F_("rn")[:], in0=bk[:], scalar1=64e-5, scalar2=None, op0=ALU.add, reads=[k_bk], writes=[K_("rn")])
                P.op("act", "activation", out=F_("rn")[:], in_=F_("rn")[:], func=AF.Sqrt, reads=[K_("rn")], writes=[K_("rn")])
                P.op("dve", "reciprocal", out=F_("rn")[:], in_=F_("rn")[:], reads=[K_("rn")], writes=[K_("rn")])
                P.op("dve", "tensor_tensor", out=F_("yT")[:], in0=F_("yT")[:], in1=F_("rn")[:], op=ALU.mult, reads=[K_("yT"), K_("rn")], writes=[K_("yT")])
                P.op("dve", "tensor_scalar", out=F_("yT")[:], in0=F_("yT")[:], scalar1=pvc("lnw", pt), scalar2=pvc("lnb", pt), op0=ALU.mult, op1=ALU.add,
                     reads=[K_("yT"), k_pv], writes=[K_("yT")])
                P.op("dve", "tensor_tensor", out=F_("yT")[:], in0=F_("yT")[:], in1=F_("bon")[:], op=ALU.add, reads=[K_("yT"), K_("bon")], writes=[K_("yT")])
                P.op("dve", "tensor_tensor", out=ysB[pt][0][:], in0=F_("yT")[:], in1=F_("sgg")[:], op=ALU.mult, reads=[K_("yT"), K_("sgg")], writes=[ysB[pt][1]])
                out_toks.append(P.dma("sp", "dma_start", out=ys_d[1, pt * 128:(pt + 1) * 128, t0:t0 + TB], in_=ysB[pt][0][:],
                                      reads=[ysB[pt][1]]))

        if DO_C:
            for pt in range(2):
                bk, k_bk = proj_fm(13 + pt)
                evac("act", qf[pt][0][:], bk[:], [k_bk], [qf[pt][1]], scale=1.0 / 16.0)
                bk, k_bk = proj_fm(15 + pt)
                evac("dve", kf[pt][0][:], bk[:], [k_bk], [kf[pt][1]])
            for tt in range(4):
                bk, k_bk = big()
                for kc in range(8):
                    P.op("pe", "matmul", bk[:], lhsT=hTt[:, kc, tt * 128:(tt + 1) * 128], rhs=wA[:, kc, 2176:2688],
                                                                      start=(kc == 0), stop=(kc == 7),
                         reads=[k_wA, k_hT], writes=[k_bk])
                evac("dve", v_tok[tt][0][:], bk[:, 0:256], [k_bk], [v_tok[tt][1]])
                P.op("act", "activation", out=sg_tok[tt][0][:], in_=bk[:, 256:512], func=AF.Silu,
                     reads=[k_bk], writes=[sg_tok[tt][1]])

        if DO_C and rstop >= 2:
            P.dma("sp", "dma_start", out=cosb[:], in_=cos_d[:, t0:t0 + TB], writes=[k_cos])
            P.dma("sp", "dma_start", out=sinb[:], in_=sin_d[:, t0:t0 + TB], writes=[k_sin])
            for src, dst in ((qf, qT), (kf, kT)):
                s1, k1 = src[0]
                s2, k2 = src[1]
                P.op("dve", "tensor_tensor", out=rt[0][0][:], in0=s1[:], in1=cosb[:], op=ALU.mult, reads=[k1, k_cos], writes=[rt[0][1]])
                P.op("pool", "tensor_tensor", out=rt[1][0][:], in0=s2[:], in1=sinb[:], op=ALU.mult, reads=[k2, k_sin], writes=[rt[1][1]])
                P.op("dve", "tensor_tensor", out=dst[0][0][:], in0=rt[0][0][:], in1=rt[1][0][:], op=ALU.subtract, reads=[rt[0][1], rt[1][1]], writes=[dst[0][1]])
                P.op("pool", "tensor_tensor", out=rt[2][0][:], in0=s1[:], in1=sinb[:], op=ALU.mult, reads=[k1, k_sin], writes=[rt[2][1]])
                P.op("dve", "tensor_tensor", out=rt[3][0][:], in0=s2[:], in1=cosb[:], op=ALU.mult, reads=[k2, k_cos], writes=[rt[3][1]])
                P.op("dve", "tensor_tensor", out=dst[1][0][:], in0=rt[2][0][:], in1=rt[3][0][:], op=ALU.add, reads=[rt[2][1], rt[3][1]], writes=[dst[1][1]])
            for pt in range(2):
                P.op("pool", "tensor_tensor", out=qd[pt][0][:], in0=qT[pt][0][:], in1=qdec, op=ALU.mult, reads=[qT[pt][1], k_cst], writes=[qd[pt][1]])
            for c in range(4 if rstop >= 4 else 0):
                cs = slice(c * 128, (c + 1) * 128)
                stm, k_stm = STm[c % 2]
                kd, k_kd = kdt[c % 2]
                ps, k_ps = quarter()
                for pt in range(2):
                    P.op("pe", "matmul", ps, lhsT=kT[pt][0][:, cs], rhs=qT[pt][0][:, cs], start=(pt == 0), stop=(pt == 1),
                         reads=[kT[pt][1], qT[pt][1]], writes=[k_ps])
                P.op("dve", "tensor_tensor", out=stm[:], in0=ps, in1=rmask, op=ALU.mult, reads=[k_ps, k_cst], writes=[k_stm])
                if rstop < 5:
                    continue
                psq, k_psq = quarter()
                psq_b = psq.bitcast(BF16)
                for pt in range(2):
                    P.op("pe", "transpose", psq_b[:, pt * 128:(pt + 1) * 128], kT[pt][0][:, cs], identb,
                         reads=[kT[pt][1], k_cstb], writes=[k_psq])
                P.op("dve", "tensor_scalar", out=kd[:], in0=psq_b, scalar1=pvc("kdec"), scalar2=None, op0=ALU.mult, reads=[k_psq, k_pv], writes=[k_kd])
                if rstop < 6:
                    continue
                pso, k_pso = half()
                P.op("pe", "matmul", pso, lhsT=stm[:], rhs=v_tok[c][0][:], start=True, stop=False, reads=[k_stm, v_tok[c][1]], writes=k_pso)
                for pt in range(2):
                    P.op("pe", "matmul", pso, lhsT=qd[pt][0][:, cs], rhs=stb[pt][0][:], start=False, stop=(pt == 1),
                         reads=[qd[pt][1], stb[pt][1]], writes=k_pso)
                for dc in range(2 if rstop >= 7 else 0):
                    pss, k_pss = half()
                    P.op("pe", "matmul", pss, lhsT=kd[:, dc * 128:(dc + 1) * 128], rhs=v_tok[c][0][:], start=True, stop=True,
                         reads=[k_kd, v_tok[c][1]], writes=k_pss)
                    P.op("dve", "scalar_tensor_tensor", out=stf[dc][0][:], in0=stf[dc][0][:], scalar=pvc("cdec"), in1=pss, op0=ALU.mult, op1=ALU.add,
                         reads=[stf[dc][1], k_pv] + k_pss, writes=[stf[dc][1]])
                    P.op("act", "activation", out=stb[dc][0][:], in_=stf[dc][0][:], func=AF.Copy, reads=[stf[dc][1]], writes=[stb[dc][1]])
                if rstop < 8:
                    continue
                s6, k_s6 = st6[c % 2]
                mvt, k_mv = mv[c % 2]
                ynt, k_yn = yn[c % 2]
                ycbt, k_ycb = ycb[c % 2]
                P.op("dve", "tensor_reduce", out=mvt[:, 0:1], in_=pso, axis=mybir.AxisListType.X, op=ALU.add, reads=k_pso, writes=[k_mv])
                P.op("dve", "tensor_scalar", out=mvt[:, 1:2], in0=mvt[:, 0:1], scalar1=-1.0 / 256.0, scalar2=None, op0=ALU.mult, reads=[k_mv], writes=[k_mv])
                P.op("act", "activation", out=ynt[:], in_=pso, func=AF.Identity, bias=mvt[:, 1:2], reads=k_pso + [k_mv], writes=[k_yn])
                P.op("act", "activation", out=junk[:, 0:256], in_=ynt[:], func=AF.Square, accum_out=s6[:, 0:1], reads=[k_yn], writes=[k_junk, k_s6])
                P.op("dve", "tensor_scalar", out=s6[:, 1:2], in0=s6[:, 0:1], scalar1=1.0 / 256.0, scalar2=1e-5, op0=ALU.mult, op1=ALU.add, reads=[k_s6], writes=[k_s6])
                P.op("act", "activation", out=s6[:, 2:3], in_=s6[:, 1:2], func=AF.Sqrt, reads=[k_s6], writes=[k_s6])
                P.op("dve", "reciprocal", out=s6[:, 3:4], in_=s6[:, 2:3], reads=[k_s6], writes=[k_s6])
                P.op("dve", "scalar_tensor_tensor", out=ycbt[:], in0=ynt[:], scalar=s6[:, 3:4], in1=sg_tok[c][0][:], op0=ALU.mult, op1=ALU.mult,
                     reads=[k_yn, k_s6, sg_tok[c][1]], writes=[k_ycb])
                if rstop < 9:
                    continue
                pst, k_pst = quarter()
                pst_b = pst.bitcast(BF16)
                for pt in range(2):
                    P.op("pe", "transpose", pst_b[:, pt * 128:(pt + 1) * 128], ycbt[:, pt * 128:(pt + 1) * 128], identb,
                         reads=[k_ycb, k_cstb], writes=[k_pst])
                for pt in range(2):
                    evac(copy_eng(), ysC[pt][0][:, cs], pst_b[:, pt * 128:(pt + 1) * 128], [k_pst], [ysC[pt][1]])
            for pt in range(2 if rstop >= 10 else 0):
                out_toks.append(P.dma("sp", "dma_start", out=ys_d[2, pt * 128:(pt + 1) * 128, t0:t0 + TB], in_=ysC[pt][0][:],
                                      reads=[ysC[pt][1]]))

    return out_toks


def _consts(g, S):
    cst = np.zeros((128, NCST), np.float32)
    cst[:, CST["ident"]:CST["ident"] + 128] = np.eye(128, dtype=np.float32)
    blk = np.kron(np.eye(2, dtype=np.float32), np.ones((64, 64), np.float32))
    cst[:, CST["b64"]:CST["b64"] + 128] = blk
    cst[:, CST["b64m"]:CST["b64m"] + 128] = blk / 64.0
    su = np.triu(np.ones((64, 64), np.float32), 1)
    ui = np.triu(np.ones((64, 64), np.float32), 0)
    cst[:, CST["m1"]:CST["m1"] + 128] = np.kron(np.eye(2, dtype=np.float32), su)
    cst[:, CST["m1"] + 128:CST["m1"] + 256] = np.kron(np.eye(2, dtype=np.float32), ui)
    cst[:, CST["m2"]:CST["m2"] + 128] = np.kron(np.eye(2, dtype=np.float32), su.T)
    gam = np.float32(1.0) - np.exp2(np.float32(-5.0 - g)).astype(np.float32)
    lg = np.log1p(-np.exp2(np.float32(-5.0 - g))).astype(np.float64)
    idx = np.arange(128)
    rel = idx[None, :] - idx[:, None]
    cst[:, CST["rmask"]:CST["rmask"] + 128] = np.where(rel >= 0, np.exp(lg * np.maximum(rel, 0)), 0.0).astype(np.float32)
    qrow = np.exp(lg * (idx + 1.0)).astype(np.float32)
    cst[:, CST["qdec"]:CST["qdec"] + 512] = np.tile(qrow[None, :], (128, 4))
    cst[:, CST["ones"]:CST["ones"] + 64] = 1.0
    kdec = np.exp(lg * (127.0 - idx)).astype(np.float32)
    cdec = np.float32(np.exp(lg * 128.0))
    half = 128
    inv_freq = (1.0 / (10000.0 ** np.linspace(0.0, 1.0, half, dtype=np.float32))).astype(np.float32)
    ang = np.arange(S, dtype=np.float32)[None, :] * inv_freq[:, None]
    return cst, kdec, cdec, np.cos(ang).astype(np.float32), np.sin(ang).astype(np.float32)


def _prep_A(inp, l, b, g, S, light=False):
    w_in = inp["w_in"][l]
    sl = lambda off: np.arange(off + g * 256, off + (g + 1) * 256)
    cols = np.concatenate([sl(0), sl(OFF_GA), sl(OFF_B), sl(OFF_B + 1024), sl(OFF_B + 2048), sl(OFF_B + 3072),
                           np.arange(OFF_B + 4096, OFF_B + 4224), sl(OFF_C), sl(OFF_C + 1024), sl(OFF_C + 2048), sl(OFF_C + 3072)])
    assert cols.shape[0] == NCOLA
    cst, kdec, cdec, cosT, sinT = _consts(g, S)
    pv = np.zeros((128, NPV), np.float32)

    def put(nm, vec256):
        pv[:, PV[nm]:PV[nm] + 2] = np.asarray(vec256, np.float32).reshape(2, 128).T
    ch = slice(g * 256, (g + 1) * 256)
    for j in range(4):
        put(f"cw{j}", inp["conv_w"][l][j, ch])
    put("cb", inp["conv_b"][l][ch])
    put("gbr", inp["lru_gate_b"][l][0, ch])
    put("gbi", inp["lru_gate_b"][l][1, ch])
    put("lam", inp["lru_lambda"][l][ch])
    mu = inp["shift_mu"][l]
    for j, nm in enumerate("rkvg"):
        put("mu_" + nm, mu[j * 1024 + g * 256: j * 1024 + (g + 1) * 256])
    pv[:, PV["mu_wl"]] = mu[4096:4224]
    put("w0", inp["decay_w0"][l][ch])
    put("a0", inp["iclr_a0"][l][ch])
    put("k_k", inp["k_k"][l][ch])
    put("k_a", inp["k_a"][l][ch])
    put("r_k", inp["r_k"][l].reshape(-1)[ch])
    put("lnw", inp["lnx_w"][l][ch])
    put("lnb", inp["lnx_b"][l][ch])
    pv[:, PV["kdec"]] = kdec
    pv[:, PV["cdec"]] = cdec
    z64 = np.zeros((64, 256), np.float32)
    return {
        "x": None if light else np.ascontiguousarray(inp["x"][b, :S]),
        "wA": None if light else np.ascontiguousarray(w_in[:, cols]),
        "gw": np.ascontiguousarray(inp["lru_gate_w"][l][:, g]),
        "w2p": np.concatenate([inp["decay_w2"][l][:, ch], z64], 0),
        "a2p": np.concatenate([z64, inp["iclr_a2"][l][:, ch]], 0),
        "pv": pv,
        "nwb": np.ascontiguousarray(np.broadcast_to(inp["norm_w"][l][None, :], (128, D))),
        "cst": cst, "cosT": cosT, "sinT": sinT,
    }


def emit_B(cx, NT, last, d):
    nc = cx.nc
    P = cx.P
    sb = cx.sb
    x_d, ys_d, nwb_d, fnwb_d, wM_d, bm_d, wBr_d, wO_d, id_d, sel_d, out_d = (d[k] for k in
        ("x", "ys", "nwb", "fnwb", "wM", "bm", "wBr", "wO", "ident", "sel", "out"))
    banks = cx.banks
    big_i = [0]

    def big():
        i = big_i[0] % 6
        big_i[0] += 1
        return banks[i], f"bank{i}"
    hps_b = banks[6][:].bitcast(BF16)
    mps_b = banks[7][:].bitcast(BF16)

    wM, k_wM = sb([128, 8, 3 * D], BF16, "wM")
    wBr, k_wBr = sb([128, 3, 8, D], BF16, "wBr")
    wO, k_wO = sb([128, 8, D], BF16, "wO")
    nwb, k_nwb = sb([128, D], F32, "nwb")
    fnwb, k_fnwb = sb([128, D], F32, "fnwb")
    bm, k_bm = sb([6, 512], F32, "bm")
    sel, k_sel = sb([6, 6, 128], F32, "sel")
    idf, k_idf = sb([128, 128], F32, "idf")
    idb, k_idb = sb([128, 128], BF16, "idb")
    wM_v = wM_d.rearrange("(kc p) n -> p kc n", p=128)
    for kc in range(8):
        P.dma("pool", "dma_start", out=wM[:, kc, :], in_=wM_v[:, kc, :], writes=[k_wM], par=True)
    for n in range(3):
        wv = wBr_d[n].rearrange("(kc p) d -> p kc d", p=128)
        for kc in range(0, 8, 4):
            P.dma("pool", "dma_start", out=wBr[:, n, kc:kc + 4, :], in_=wv[:, kc:kc + 4, :], writes=[k_wBr], par=True)
    wv = wO_d.rearrange("(kc p) d -> p kc d", p=128)
    for kc in range(0, 8, 4):
        P.dma("pool", "dma_start", out=wO[:, kc:kc + 4, :], in_=wv[:, kc:kc + 4, :], writes=[k_wO], par=True)
    P.dma("sp", "dma_start", out=nwb[:], in_=nwb_d, writes=[k_nwb])
    P.dma("sp", "dma_start", out=fnwb[:], in_=fnwb_d, writes=[k_fnwb])
    P.dma("sp", "dma_start", out=bm[:], in_=bm_d, writes=[k_bm])
    P.dma("sp", "dma_start", out=idf[:], in_=id_d, writes=[k_idf])
    P.op("dve", "tensor_copy", out=idb[:], in_=idf[:], reads=[k_idf], writes=[k_idb])
    P.dma("sp", "dma_start", out=sel[:], in_=sel_d, writes=[k_sel])

    YB = 256
    ysb = [sb([128, 3, 8, YB], BF16, f"ysb{i}") for i in range(2)]
    xt = [sb([128, D], F32, f"xt{i}") for i in range(2)]
    junk, k_junk = sb([128, D], BF16, "junk")
    ss = [sb([128, 8], F32, f"ss{i}") for i in range(2)]
    xs = [sb([128, D], BF16, f"xs{i}") for i in range(2)]
    hT = [sb([128, 8, 128], BF16, f"hT{i}") for i in range(2)]
    gt = [sb([128, 512], F32, f"gt{i}") for i in range(2)]
    tmp = [sb([128, 512], F32, f"tmp{i}") for i in range(2)]
    mg = [sb([128, D], F32, f"mg{i}") for i in range(1)] * 2
    mgb = [sb([128, D], BF16, f"mgb{i}") for i in range(1)] * 2
    mT = [sb([128, 8, 128], BF16, f"mT{i}") for i in range(2)]
    xn = [sb([128, D], F32, f"xn{i}") for i in range(1)] * 2
    out_toks = []

    def rms(src, k_src, ssb, k_ss, wb, k_wb, dst, k_dst):
        P.op("act", "activation", out=junk[:], in_=src[:], func=AF.Square, accum_out=ssb[:, 0:1], reads=[k_src], writes=[k_junk, k_ss])
        P.op("dve", "tensor_scalar", out=ssb[:, 1:2], in0=ssb[:, 0:1], scalar1=1.0 / D, scalar2=EPS, op0=ALU.mult, op1=ALU.add, reads=[k_ss], writes=[k_ss])
        P.op("act", "activation", out=ssb[:, 2:3], in_=ssb[:, 1:2], func=AF.Sqrt, reads=[k_ss], writes=[k_ss])
        P.op("dve", "reciprocal", out=ssb[:, 3:4], in_=ssb[:, 2:3], reads=[k_ss], writes=[k_ss])
        P.op("dve", "scalar_tensor_tensor", out=dst[:], in0=src[:], scalar=ssb[:, 3:4], in1=wb[:], op0=ALU.mult, op1=ALU.mult,
             reads=[k_src, k_ss, k_wb], writes=[k_dst])

    for tb in range(NT // YB):
        yb, k_yb = ysb[tb % 2]
        for n in range(3):
            yv = ys_d[n].rearrange("(wc p) t -> p wc t", p=128)
            P.dma("sp", "dma_start", out=yb[:, n, :, :], in_=yv[:, :, tb * YB:(tb + 1) * YB], writes=[k_yb])
        for tt in range(YB // 128):
            ti = tb * (YB // 128) + tt
            r0 = ti * 128
            xb_, k_x = xt[ti % 2]
            ssb, k_ss = ss[ti % 2]
            xsb, k_xs = xs[ti % 2]
            hTt, k_hT = hT[ti % 2]
            mgt, k_mg = mg[ti % 2]
            mgbt, k_mgb = mgb[ti % 2]
            mTt, k_mT = mT[ti % 2]
            xnt, k_xn = xn[ti % 2]
            xot, k_xo = xb_, k_x
            P.dma("sp", "dma_start", out=xb_[:], in_=x_d[r0:r0 + 128, :], writes=[k_x])
            rms(xb_, k_x, ssb, k_ss, nwb, k_nwb, xsb, k_xs)
            for kc in range(8):
                P.op("pe", "transpose", hps_b[:, kc * 128:(kc + 1) * 128], xsb[:, kc * 128:(kc + 1) * 128], idb[:], reads=[k_xs, k_idb], writes=["hps"])
            P.op("act", "activation", out=hTt[:].rearrange("p k t -> p (k t)"), in_=hps_b, func=AF.Copy, reads=["hps"], writes=[k_hT])
            for n in range(3):
                for hf in range(2):
                    c0 = n * D + hf * 512
                    g_, k_g = gt[(n * 2 + hf) % 2]
                    t_, k_t = tmp[(n * 2 + hf) % 2]
                    bk, k_bk = big()
                    P.op("pe", "matmul", bk[:], lhsT=sel[:, n * 2 + hf, :], rhs=bm[:], start=True, stop=False, reads=[k_sel, k_bm], writes=[k_bk])
                    for kc in range(8):
                        P.op("pe", "matmul", bk[:], lhsT=hTt[:, kc, :], rhs=wM[:, kc, c0:c0 + 512], start=False, stop=(kc == 7), reads=[k_hT, k_wM], writes=[k_bk])
                    P.op("act", "activation", out=g_[:], in_=bk[:], func=AF.Sigmoid, reads=[k_bk], writes=[k_g])
                    bk, k_bk = big()
                    for wc in range(8):
                        P.op("pe", "matmul", bk[:], lhsT=yb[:, n, wc, tt * 128:(tt + 1) * 128], rhs=wBr[:, n, wc, hf * 512:(hf + 1) * 512],
                             start=(wc == 0), stop=(wc == 7), reads=[k_yb, k_wBr], writes=[k_bk])
                    msl = mgt[:, hf * 512:(hf + 1) * 512]
                    if n == 0:
                        P.op("dve", "tensor_tensor", out=msl, in0=g_[:], in1=bk[:], op=ALU.mult, reads=[k_g, k_bk], writes=[k_mg])
                    else:
                        P.op("dve", "tensor_tensor", out=t_[:], in0=g_[:], in1=bk[:], op=ALU.mult, reads=[k_g, k_bk], writes=[k_t])
                        P.op("pool", "tensor_tensor", out=msl, in0=msl, in1=t_[:], op=ALU.add, reads=[k_mg, k_t], writes=[k_mg])
            P.op("pool", "tensor_copy", out=mgbt[:], in_=mgt[:], reads=[k_mg], writes=[k_mgb])
            for kc in range(8):
                P.op("pe", "transpose", mps_b[:, kc * 128:(kc + 1) * 128], mgbt[:, kc * 128:(kc + 1) * 128], idb[:], reads=[k_mgb, k_idb], writes=["mps"])
            P.op("act", "activation", out=mTt[:].rearrange("p k t -> p (k t)"), in_=mps_b, func=AF.Copy, reads=["mps"], writes=[k_mT])
            for hf in range(2):
                bk, k_bk = big()
                for kc in range(8):
                    P.op("pe", "matmul", bk[:], lhsT=mTt[:, kc, :], rhs=wO[:, kc, hf * 512:(hf + 1) * 512], start=(kc == 0), stop=(kc == 7),
                         reads=[k_mT, k_wO], writes=[k_bk])
                P.op("dve", "tensor_tensor", out=xnt[:, hf * 512:(hf + 1) * 512], in0=xb_[:, hf * 512:(hf + 1) * 512], in1=bk[:], op=ALU.add,
                     reads=[k_x, k_bk], writes=[k_xn])
            if last:
                rms(xnt, k_xn, ssb, k_ss, fnwb, k_fnwb, xot, k_xo)
                out_toks.append(P.dma("sp", "dma_start", out=out_d[r0:r0 + 128, :], in_=xot[:], reads=[k_xo]))
            else:
                out_toks.append(P.dma("sp", "dma_start", out=out_d[r0:r0 + 128, :], in_=xnt[:], reads=[k_xn]))
    return out_toks


def _prep_B(inp, l, xcur, ys_full, b, j, NT):
    return {
        "x": np.ascontiguousarray(xcur[b, j * NT:(j + 1) * NT]),
        "ys": np.ascontiguousarray(ys_full[b][:, :, j * NT:(j + 1) * NT]),
        "nwb": np.ascontiguousarray(np.broadcast_to(inp["norm_w"][l][None, :], (128, D))),
        "fnwb": np.ascontiguousarray(np.broadcast_to(inp["final_norm_w"][None, :], (128, D))),
        "wM": np.ascontiguousarray(inp["w_in"][l][:, OFF_M:]),
        "bm": np.ascontiguousarray(inp["b_merge"][l].reshape(6, 512)),
        "sel": np.ascontiguousarray(np.broadcast_to(np.eye(6, dtype=np.float32)[:, :, None], (6, 6, 128))),
        "wBr": np.ascontiguousarray(inp["w_branch"][l]),
        "wO": np.ascontiguousarray(inp["w_out"][l]),
        "ident": np.eye(128, dtype=np.float32),
    }


def build_A(S, flags=(1, 1, 1)):
    cx = Cx()
    d = {"x": cx.dram("x", [S, D]), "wA_cols": [(0, cx.dram("wA", [D, NCOLA]))], "gw": cx.dram("gw", [2, 256, 256]),
         "w2p": cx.dram("w2p", [128, 256]), "a2p": cx.dram("a2p", [128, 256]), "pv": cx.dram("pv", [128, NPV]),
         "nwb": cx.dram("nwb", [128, D]), "cst": cx.dram("cst", [128, NCST]), "cosT": cx.dram("cosT", [128, S]),
         "sinT": cx.dram("sinT", [128, S]), "ysT": cx.dram("ysT", [3, 256, S], BF16, "ExternalOutput")}
    toks = emit_A(cx, S, d, flags)
    cx.P.finish(toks)
    return cx.nc


def build_B(NT, last):
    cx = Cx()
    d = {"x": cx.dram("x", [NT, D]), "ys": cx.dram("ys", [3, D, NT], BF16), "nwb": cx.dram("nwb", [128, D]),
         "fnwb": cx.dram("fnwb", [128, D]), "wM": cx.dram("wM", [D, 3 * D]), "bm": cx.dram("bm", [6, 512]),
         "wBr": cx.dram("wBr", [3, D, D]), "wO": cx.dram("wO", [D, D]), "ident": cx.dram("ident", [128, 128]),
         "sel": cx.dram("sel", [6, 6, 128]), "out": cx.dram("out", [NT, D], F32, "ExternalOutput")}
    toks = emit_B(cx, NT, last, d)
    cx.P.finish(toks)
    return cx.nc


_COLS = [(0, 0), (256, OFF_GA), (512, OFF_B), (768, OFF_B + 1024), (1024, OFF_B + 2048), (1280, OFF_B + 3072),
         (1664, OFF_C), (1920, OFF_C + 1024), (2176, OFF_C + 2048), (2432, OFF_C + 3072)]


def build_fused(S, groups=(0, 1, 2, 3)):
    cx = Cx()
    nc, P = cx.nc, cx.P
    x_in = cx.dram("x", [S, D])
    w_in = cx.dram("w_in", [DEPTH, D, N_IN])
    gw = cx.dram("gw", [DEPTH, 4, 2, 256, 256])
    w2p = cx.dram("w2p", [DEPTH, 4, 128, 256])
    a2p = cx.dram("a2p", [DEPTH, 4, 128, 256])
    pv = cx.dram("pv", [DEPTH, 4, 128, NPV])
    nwb = cx.dram("nwb", [DEPTH, 128, D])
    fnwb = cx.dram("fnwb", [128, D])
    cst = cx.dram("cst", [4, 128, NCST])
    cosT = cx.dram("cosT", [128, S])
    sinT = cx.dram("sinT", [128, S])
    bm = cx.dram("bm", [DEPTH, 6, 512])
    sel = cx.dram("sel", [6, 6, 128])
    wBr = cx.dram("wBr", [DEPTH, 3, D, D])
    wO = cx.dram("wO", [DEPTH, D, D])
    ident = cx.dram("ident", [128, 128])
    out = cx.dram("out", [S, D], F32, "ExternalOutput")
    ys_scr = nc.dram_tensor("ys_scr", [3, D, S], BF16, kind="Internal").ap()
    x_scr = nc.dram_tensor("x_scr", [S, D], F32, kind="Internal").ap()
    toks = []
    for l in range(DEPTH):
        x_src = x_in if l == 0 else x_scr
        for g in groups:
            cx.reset()
            cols = [(c0, w_in[l][:, off + g * 256: off + (g + 1) * 256]) for c0, off in _COLS]
            cols.append((1536, w_in[l][:, OFF_B + 4096: OFF_B + 4224]))
            d = {"x": x_src, "wA_cols": cols, "gw": gw[l, g], "w2p": w2p[l, g], "a2p": a2p[l, g], "pv": pv[l, g],
                 "nwb": nwb[l], "cst": cst[g], "cosT": cosT, "sinT": sinT, "ysT": ys_scr[:, g * 256:(g + 1) * 256, :]}
            for t in emit_A(cx, S, d):
                P.wait_tok("sp", t)
            P.barrier()
        cx.reset()
        last = (l == DEPTH - 1)
        d = {"x": x_src, "ys": ys_scr, "nwb": nwb[l], "fnwb": fnwb, "wM": w_in[l][:, OFF_M:], "bm": bm[l], "wBr": wBr[l],
             "wO": wO[l], "ident": ident, "sel": sel, "out": out if last else x_scr}
        toks = emit_B(cx, S, last, d)
        for t in toks:
            P.wait_tok("sp", t)
        P.barrier()
    P.finish(toks)
    return nc


def _prep_fused(inp, b, S):
    pvs = np.zeros((DEPTH, 4, 128, NPV), np.float32)
    csts = np.zeros((4, 128, NCST), np.float32)
    w2 = np.zeros((DEPTH, 4, 128, 256), np.float32)
    a2 = np.zeros((DEPTH, 4, 128, 256), np.float32)
    gws = np.zeros((DEPTH, 4, 2, 256, 256), np.float32)
    cosT = sinT = None
    for l in range(DEPTH):
        for g in range(4):
            m = _prep_A(inp, l, b, g, S, light=True)
            pvs[l, g] = m["pv"]
            w2[l, g] = m["w2p"]
            a2[l, g] = m["a2p"]
            gws[l, g] = m["gw"]
            csts[g] = m["cst"]
            cosT, sinT = m["cosT"], m["sinT"]
    return {
        "x": np.ascontiguousarray(inp["x"][b, :S]), "w_in": np.ascontiguousarray(inp["w_in"]), "gw": gws, "w2p": w2, "a2p": a2,
        "pv": pvs, "nwb": np.ascontiguousarray(np.broadcast_to(inp["norm_w"][:, None, :], (DEPTH, 128, D))),
        "fnwb": np.ascontiguousarray(np.broadcast_to(inp["final_norm_w"][None, :], (128, D))),
        "cst": csts, "cosT": cosT, "sinT": sinT, "bm": np.ascontiguousarray(inp["b_merge"].reshape(DEPTH, 6, 512)),
        "sel": np.ascontiguousarray(np.broadcast_to(np.eye(6, dtype=np.float32)[:, :, None], (6, 6, 128))),
        "wBr": np.ascontiguousarray(inp["w_branch"]), "wO": np.ascontiguousarray(inp["w_out"]),
        "ident": np.eye(128, dtype=np.float32),
    }


def _forward_unfused(inp, S):
    NT = S // 4
    xcur = np.ascontiguousarray(inp["x"][:, :S]).astype(np.float32)
    for l in range(DEPTH):
        ncA = build_A(S)
        inp_l = dict(inp)
        inp_l["x"] = xcur
        in_maps = [_prep_A(inp_l, l, c // 4, c % 4, S) for c in range(8)]
        res = run_bass_kernel_spmd(ncA, in_maps, core_ids=list(range(8)))
        ys_full = [np.concatenate([res.results[b * 4 + g]["ysT"] for g in range(4)], axis=1) for b in range(2)]
        ncB = build_B(NT, last=(l == DEPTH - 1))
        in_maps = [_prep_B(inp, l, xcur, ys_full, c // 4, c % 4, NT) for c in range(8)]
        res = run_bass_kernel_spmd(ncB, in_maps, core_ids=list(range(8)))
        xcur = np.stack([np.concatenate([res.results[b * 4 + j]["out"] for j in range(4)], axis=0) for b in range(2)], axis=0)
    return xcur


def _forward_fused(inp, S):
    nc = build_fused(S)
    maps = [_prep_fused(inp, b, S) for b in range(2)]
    in_maps = [maps[c // 4] for c in range(8)]
    res = run_bass_kernel_spmd(nc, in_maps, core_ids=list(range(8)))
    return np.stack([res.results[0]["out"], res.results[4]["out"]], axis=0)


def kernel(**inputs):
    inp = {k: np.asarray(v) for k, v in inputs.items()}
    return _forward_fused(inp, inp["x"].shape[1]).astype(np.float32)
```

```python
import numpy as np
import ml_dtypes
import concourse.bass as bass
import concourse.mybir as mybir
from concourse.bass_utils import run_bass_kernel_spmd

F32 = mybir.dt.float32
BF16 = mybir.dt.bfloat16
AF = mybir.ActivationFunctionType
ALU = mybir.AluOpType

D = 1024
DEPTH = 2
EPS = 1e-6
OFF_GA = 1024
OFF_B = 2048
B_COLS = 4 * 1024 + 128
OFF_C = OFF_B + B_COLS
OFF_M = OFF_C + 4096
N_IN = OFF_M + 3072
TB = 512
NCOLA = 2688

ENGS = ("pe", "dve", "act", "pool", "sp")
NDMASEM = 8


class Prog:
    def __init__(self, nc, same_engine_sync=True):
        self.nc = nc
        self.ops = {e: [] for e in ENGS}
        self.csem = {e: nc.alloc_semaphore(name=f"c_{e}") for e in ENGS if e != "sp"}
        self.ccnt = {e: 0 for e in ENGS}
        self.dsem = {e: [nc.alloc_semaphore(name=f"d_{e}{i}") for i in range(NDMASEM)]
                     for e in ("sp", "act", "pool")}
        self.dval = {e: [0] * NDMASEM for e in ("sp", "act", "pool")}
        self.dk = {e: 0 for e in ("sp", "act", "pool")}
        self.waited = {}
        self.lastw = {}
        self.readers = {}
        self.ses = same_engine_sync
        self.nops = 0

    def _need(self, e, tok, out):
        semh, val, src = tok
        if src == e and (e == "pe" or not self.ses):
            return
        if self.waited.get((e, id(semh)), 0) >= val:
            return
        cur = out.get(id(semh))
        if cur is None or cur[1] < val:
            out[id(semh)] = (semh, val)

    def _deps(self, e, reads, writes, par=False):
        out = {}
        for k in reads:
            for t in self.lastw.get(k, ()):
                self._need(e, t, out)
        for k in writes:
            if not par:
                for t in self.lastw.get(k, ()):
                    self._need(e, t, out)
            for t in self.readers.get(k, ()):
                self._need(e, t, out)
        for semh, val in out.values():
            self.ops[e].append(("wait", semh, val))
            self.waited[(e, id(semh))] = val

    def _commit(self, tok, reads, writes, par=False):
        for k in reads:
            self.readers.setdefault(k, []).append(tok)
        for k in writes:
            if par:
                self.lastw.setdefault(k, []).append(tok)
            else:
                self.lastw[k] = [tok]
                self.readers[k] = []

    def op(self, e, meth, *args, reads=(), writes=(), **kw):
        fn = (lambda g: getattr(g, meth)(*args, **kw)) if isinstance(meth, str) else meth
        xr = [k for k in reads if k.startswith("bank") or k == "hps"]
        if xr:
            writes = list(writes) + xr
            reads = [k for k in reads if k not in xr]
        self._deps(e, reads, writes)
        self.ccnt[e] += 1
        tok = (self.csem[e], self.ccnt[e], e)
        self.ops[e].append(("op", fn, self.csem[e], 1))
        self._commit(tok, reads, writes)
        self.nops += 1
        return tok

    def dma(self, e, meth, *args, reads=(), writes=(), par=False, **kw):
        fn = (lambda g: getattr(g, meth)(*args, **kw)) if isinstance(meth, str) else meth
        i = self.dk[e] % NDMASEM
        self.dk[e] += 1
        semh = self.dsem[e][i]
        if self.dval[e][i] > 0:
            k = (e, id(semh))
            if self.waited.get(k, 0) < self.dval[e][i]:
                self.ops[e].append(("wait", semh, self.dval[e][i]))
                self.waited[k] = self.dval[e][i]
        self._deps(e, reads, writes, par)
        self.dval[e][i] += 16
        tok = (semh, self.dval[e][i], "dma_" + e)
        self.ops[e].append(("op", fn, semh, 16))
        self._commit(tok, reads, writes, par)
        self.nops += 1
        return tok

    def wait_tok(self, e, tok):
        semh, val, _ = tok
        k = (e, id(semh))
        if self.waited.get(k, 0) < val:
            self.ops[e].append(("wait", semh, val))
            self.waited[k] = val

    def barrier(self):
        for e in ENGS:
            for e2 in self.csem:
                if e2 != e and self.ccnt[e2] > 0:
                    self.wait_tok(e, (self.csem[e2], self.ccnt[e2], e2))
            for q in self.dsem:
                for i in range(NDMASEM):
                    if self.dval[q][i] > 0:
                        self.wait_tok(e, (self.dsem[q][i], self.dval[q][i], "dma_" + q))

    def finish(self, final_toks):
        for t in final_toks:
            self.wait_tok("sp", t)
        nc = self.nc
        ops = self.ops

        def replay(engine, lst):
            for it in lst:
                if it[0] == "wait":
                    engine.wait_ge(it[1], it[2])
                else:
                    it[1](engine).then_inc(it[2], it[3])

        with nc.Block() as block:
            @block.tensor
            def _(e):
                replay(e, ops["pe"])

            @block.vector
            def _(e):
                replay(e, ops["dve"])

            @block.scalar
            def _(e):
                replay(e, ops["act"])

            @block.gpsimd
            def _(e):
                replay(e, ops["pool"])

            @block.sync
            def _(e):
                replay(e, ops["sp"])


class Cx:
    def __init__(self):
        self.nc = bass.Bass("TRN2", target_bir_lowering=False)
        self.P = Prog(self.nc)
        self.banks = [self.nc.alloc_psum_tensor(f"bank{i}", [128, 512], F32) for i in range(8)]
        self.lo = (int(self.nc.sbuf_base) + 63) // 64 * 64
        self.hi = int(self.nc.sbuf_top)
        self.ptr = self.lo
        self.n = 0

    def reset(self):
        self.ptr = self.lo

    def sb(self, shape, dt=F32, name=None):
        self.n += 1
        nm = f"s{self.n}_" + (name or "t")
        esz = 2 if dt == BF16 else 4
        nbytes = esz
        for v in shape[1:]:
            nbytes *= v
        nbytes = (nbytes + 63) // 64 * 64
        assert self.ptr + nbytes <= self.hi, f"SBUF arena overflow allocating {nm} {shape}: {self.ptr + nbytes - self.hi} over"
        t = self.nc.alloc_sbuf_tensor_at(nm, list(shape), dt, offset=self.ptr)
        self.ptr += nbytes
        return t, nm

    def dram(self, n, s, d=F32, k="ExternalInput"):
        return self.nc.dram_tensor(n, list(s), d, kind=k).ap()

PV = {}
_c = 0
for _nm, _n in [("cw0", 2), ("cw1", 2), ("cw2", 2), ("cw3", 2), ("cb", 2), ("gbr", 2), ("gbi", 2),
                ("lam", 2), ("mu_r", 2), ("mu_k", 2), ("mu_v", 2), ("mu_g", 2), ("mu_wl", 1),
                ("w0", 2), ("a0", 2), ("k_k", 2), ("k_a", 2), ("r_k", 2), ("lnw", 2), ("lnb", 2),
                ("kdec", 1), ("cdec", 1)]:
    PV[_nm] = _c
    _c += _n
NPV = _c
DV = {"omu_r": 0, "omu_k": 2, "omu_v": 4, "omu_g": 6, "omu_wl": 8, "lrs": 9}
NDV = 11

CST = {"ident": 0, "b64": 128, "b64m": 256, "m1": 384, "m2": 640, "rmask": 768, "qdec": 896, "ones": 1408}
NCST = 1408 + 64


def emit_A(cx, S, d, flags=(1, 1, 1)):
    DO_A, DO_B, DO_C = flags
    rstop = 99
    TB_ = 256
    NTT = TB_ // 128
    NCH = TB_ // 64
    NTB = S // TB_
    nc = cx.nc
    P = cx.P
    sb = cx.sb
    x_d, gw_d, w2p_d, a2p_d, pv_d, nwb_d, cst_d, cos_d, sin_d, ys_d = (d[k] for k in
        ("x", "gw", "w2p", "a2p", "pv", "nwb", "cst", "cosT", "sinT", "ysT"))
    out_toks = []

    def dump(*a, **k):
        return

    banks = cx.banks
    cur_stream = [0]
    bank_sets = {0: [0, 1], 1: [3, 4], 2: [5, 6]}
    bank_ctr = {0: 0, 1: 0, 2: 0}

    def _nb():
        st = cur_stream[0]
        bs = bank_sets[st]
        i = bs[bank_ctr[st] % len(bs)]
        bank_ctr[st] += 1
        return i

    def big():
        i = _nb()
        return banks[i], f"bank{i}"

    def quarter():
        i = _nb()
        return banks[i][:, 0:128], f"bank{i}"

    def half():
        i = _nb()
        return banks[i][:, 0:256], [f"bank{i}"]
    hps = banks[7]

    wA, k_wA = sb([128, 8, NCOLA], BF16, "wA")
    gw, k_gw = sb([128, 2, 2, 256], BF16, "gw")
    w2p, k_w2p = sb([128, 256], BF16, "w2p")
    a2p, k_a2p = sb([128, 256], BF16, "a2p")
    pv, k_pv = sb([128, NPV], F32, "pv")
    dv, k_dv = sb([128, NDV], F32, "dv")
    nwb, k_nwb = sb([128, D], F32, "nwb")
    cst, k_cst = sb([128, NCST], F32, "cst")
    cstb, k_cstb = sb([128, 640 + 128], BF16, "cstb")

    for c0, piece in d["wA_cols"]:
        npc = piece.shape[1]
        pv_ = piece.rearrange("(kc p) n -> p kc n", p=128)
        for kc in range(0, 8, 4):
            P.dma("pool", "dma_start", out=wA[:, kc:kc + 4, c0:c0 + npc], in_=pv_[:, kc:kc + 4, :], writes=[k_wA], par=True)
    P.dma("pool", "dma_start", out=gw[:], in_=gw_d.rearrange("n (cc p) d -> p n cc d", p=128), writes=[k_gw])
    P.dma("pool", "dma_start", out=w2p[:], in_=w2p_d, writes=[k_w2p])
    P.dma("pool", "dma_start", out=a2p[:], in_=a2p_d, writes=[k_a2p])
    P.dma("sp", "dma_start", out=pv[:], in_=pv_d, writes=[k_pv])
    P.dma("sp", "dma_start", out=nwb[:], in_=nwb_d, writes=[k_nwb])
    P.dma("sp", "dma_start", out=cst[:], in_=cst_d, writes=[k_cst])
    P.op("dve", "tensor_copy", out=cstb[:, 0:128], in_=cst[:, 0:128], reads=[k_cst], writes=[k_cstb])
    identb = cstb[:, 0:128]
    ident_f = cst[:, CST["ident"]:CST["ident"] + 128]
    b64 = cst[:, CST["b64"]:CST["b64"] + 128]
    b64m = cst[:, CST["b64m"]:CST["b64m"] + 128]
    m1 = cst[:, CST["m1"]:CST["m1"] + 256]
    m2 = cst[:, CST["m2"]:CST["m2"] + 128]
    rmask = cst[:, CST["rmask"]:CST["rmask"] + 128]
    qdec = cst[:, CST["qdec"]:CST["qdec"] + 512]
    ones64 = cst[:, CST["ones"]:CST["ones"] + 64]

    def pvc(nm, j=0):
        c = PV[nm] + j
        return pv[:, c:c + 1]

    def dvc(nm, j=0):
        c = DV[nm] + j
        return dv[:, c:c + 1]

    P.op("dve", "tensor_scalar", out=dv[:, 0:9], in0=pv[:, PV["mu_r"]:PV["mu_r"] + 9], scalar1=-1.0, scalar2=1.0,
                                          op0=ALU.mult, op1=ALU.add, reads=[k_pv], writes=[k_dv])
    tl, k_tl = sb([128, 2], F32, "tl")
    P.op("act", "activation", out=tl[:], in_=pv[:, PV["lam"]:PV["lam"] + 2], func=AF.Exp, scale=-1.0, reads=[k_pv], writes=[k_tl])
    P.op("act", "activation", out=tl[:], in_=tl[:], func=AF.Ln, bias=1.0, reads=[k_tl], writes=[k_tl])
    P.op("dve", "tensor_scalar", out=dv[:, 9:11], in0=tl[:], scalar1=-8.0, scalar2=None, op0=ALU.mult, reads=[k_tl, k_dv], writes=[k_dv])

    xt = [sb([128, D], F32, f"xt{i}") for i in range(2)]
    junk, k_junk = sb([128, D], BF16, "junk")
    ss = [sb([128, 4], F32, f"ss{i}") for i in range(2)]
    xs = [sb([128, D], BF16, f"xs{i}") for i in range(1)] * 2
    hT = [sb([128, 8, TB_], BF16, f"hT{i}") for i in range(1)]
    hps_b = hps[:].bitcast(BF16)
    pool = [sb([128, TB_], F32, f"pl{i}") for i in range(20)]
    poolB = [sb([128, TB_], F32, f"plB{i}") for i in range(19)]
    poolC = [sb([128, TB_], F32, f"plC{i}") for i in range(11)]

    xa = [sb([128, TB_ + 3], F32, f"xa{i}") for i in range(2)]
    cv = [poolC[0], poolC[1]]
    lr = [poolC[2], poolC[3]]
    li = [poolC[4], poolC[5]]
    la, k_la = poolC[6]
    lt, k_lt = poolC[7]
    lh, k_lh = poolC[8]
    sga = [poolC[9], poolC[10]]
    cvb = [sb([128, TB_], BF16, f"cvb{i}") for i in range(2)]
    hcar = [sb([128, 1], F32, f"hcar{i}") for i in range(2)]
    ysA = [sb([128, TB_], BF16, f"ysA{i}") for i in range(2)]
    for i in range(2):
        P.op("pool", "memset", xa[i][0][:, 0:3], 0.0, writes=[xa[i][1]])
        P.op("pool", "memset", hcar[i][0][:], 0.0, writes=[hcar[i][1]])

    pB = {nm: [sb([128, TB_ + 1], F32, f"pB{nm}{i}") for i in range(2)] for nm in "rkvg"}
    pWL, k_pWL = sb([128, TB_ + 1], F32, "pWL")
    P.op("pool", "memset", pWL[:, 0:1], 0.0, writes=[k_pWL])
    for nm in "rkvg":
        for i in range(2):
            P.op("pool", "memset", pB[nm][i][0][:, 0:1], 0.0, writes=[pB[nm][i][1]])
    sWL, k_sWL = pool[19]
    tnh, k_tnh = sb([128, TB_], BF16, "tnh")
    sWLb, k_sWLb = sb([128, TB_], BF16, "sWLb")
    sS_pt = [{nm: pl[i] for i, nm in enumerate("rkvg")} for pl in (pool, poolB)]
    f_pt = [{nm: pl[4 + i] for i, nm in enumerate(
        ["logw", "cum", "Wt", "iW", "Wp", "kk", "sq", "rn", "kmod", "kka", "iWC", "bon", "sgg", "yT", "aa"])} for pl in (pool, poolB)]
    BD_AR = [sb([128, NCH, 2, 128], BF16, f"BD_AR{i}") for i in range(2)]
    BDn = {nm: [sb([128, NCH, 128], BF16, f"BD_{nm}{i}") for i in range(2)] for nm in ["B", "K", "Bh", "Kh", "V"]}
    for i in range(2):
        P.op("pool", "memset", BD_AR[i][0][:], 0.0, writes=[BD_AR[i][1]])
        for nm in BDn:
            P.op("pool", "memset", BDn[nm][i][0][:], 0.0, writes=[BDn[nm][i][1]])
    S0f = [sb([128, 128], F32, f"S0f{i}") for i in range(2)]
    S0b = [sb([128, 128], BF16, f"S0b{i}") for i in range(2)]
    for i in range(2):
        P.op("pool", "memset", S0f[i][0][:], 0.0, writes=[S0f[i][1]])
        P.op("pool", "memset", S0b[i][0][:], 0.0, writes=[S0b[i][1]])
    NM1a_l = [sb([128, NCH, 256], BF16, f"NM1a{i}") for i in range(2)]
    NM2a_l = [sb([128, NCH, 256], BF16, f"NM2a{i}") for i in range(2)]
    LNP_l = [[sb([128, NCH, 128], BF16, f"LNP{j}_{i}") for i in range(6)] for j in range(2)]
    TKa_l = [sb([128, NCH, 3, 128], BF16, f"TKa{i}") for i in range(2)]
    wc8_l = [sb([128, NCH], F32, f"wc8_{i}") for i in range(2)]
    Zb_l = [[sb([128, 128], BF16, f"Zb{j}_{i}") for i in range(2)] for j in range(2)]
    Ub_l = [[sb([128, 128], BF16, f"Ub{j}_{i}") for i in range(2)] for j in range(2)]
    ysB = [sb([128, TB_], BF16, f"ysB{i}") for i in range(2)]

    qf = [poolC[0], poolC[1]]
    kf = [poolC[2], poolC[3]]
    cosb, k_cos = poolC[4]
    sinb, k_sin = poolC[5]
    rt = [poolC[6 + i] for i in range(4)]
    qT = [sb([128, TB_], BF16, f"qT{i}") for i in range(2)]
    kT = [sb([128, TB_], BF16, f"kT{i}") for i in range(2)]
    qd = [sb([128, TB_], BF16, f"qd{i}") for i in range(2)]
    v_tok = [sb([128, 256], BF16, f"vtok{i}") for i in range(4)]
    sg_tok = [sb([128, 256], F32, f"sgtok{i}") for i in range(4)]
    STm = [sb([128, 128], BF16, f"ST{i}") for i in range(2)]
    kdt = [sb([128, 256], BF16, f"kdt{i}") for i in range(2)]
    stf = [sb([128, 256], F32, f"stf{i}") for i in range(2)]
    stb = [sb([128, 256], BF16, f"stb{i}") for i in range(2)]
    for i in range(2):
        P.op("pool", "memset", stf[i][0][:], 0.0, writes=[stf[i][1]])
        P.op("pool", "memset", stb[i][0][:], 0.0, writes=[stb[i][1]])
    st6 = [sb([128, 6], F32, f"st6_{i}") for i in range(2)]
    mv = [sb([128, 4], F32, f"mv{i}") for i in range(2)]
    yn = [sb([128, 256], F32, f"yn{i}") for i in range(1)] * 2
    ycb = [sb([128, 256], BF16, f"ycb{i}") for i in range(2)]
    ysC = [sb([128, TB_], BF16, f"ysC{i}") for i in range(2)]

    cp_i = [0]

    def copy_eng():
        cp_i[0] += 1
        return "act" if cp_i[0] % 2 else "dve"

    def evac(e, out, in_, reads, writes, scale=None):
        if e == "act":
            if scale is None:
                P.op("act", "activation", out=out, in_=in_, func=AF.Copy, reads=reads, writes=writes)
            else:
                P.op("act", "activation", out=out, in_=in_, func=AF.Copy, scale=scale, reads=reads, writes=writes)
        else:
            if scale is None:
                P.op(e, "tensor_copy", out=out, in_=in_, reads=reads, writes=writes)
            else:
                P.op(e, "tensor_scalar", out=out, in0=in_, scalar1=scale, scalar2=None, op0=ALU.mult, reads=reads, writes=writes)

    for tb in range(NTB):
        t0 = tb * TB_
        hTt, k_hT = hT[0]
        for tt in range(NTT):
            xb_, k_x = xt[tt % 2]
            ssb, k_ss = ss[tt % 2]
            xsb, k_xs = xs[tt % 2]
            r0 = t0 + tt * 128
            P.dma("sp", "dma_start", out=xb_[:], in_=x_d[r0:r0 + 128, :], writes=[k_x])
            P.op("act", "activation", out=junk[:], in_=xb_[:], func=AF.Square, accum_out=ssb[:, 0:1],
                 reads=[k_x], writes=[k_junk, k_ss])
            P.op("dve", "tensor_scalar", out=ssb[:, 1:2], in0=ssb[:, 0:1], scalar1=1.0 / D, scalar2=EPS, op0=ALU.mult, op1=ALU.add,
                 reads=[k_ss], writes=[k_ss])
            P.op("act", "activation", out=ssb[:, 2:3], in_=ssb[:, 1:2], func=AF.Sqrt, reads=[k_ss], writes=[k_ss])
            P.op("dve", "reciprocal", out=ssb[:, 3:4], in_=ssb[:, 2:3], reads=[k_ss], writes=[k_ss])
            P.op("dve", "scalar_tensor_tensor", out=xsb[:], in0=xb_[:], scalar=ssb[:, 3:4], in1=nwb[:],
                                                                                 op0=ALU.mult, op1=ALU.mult,
                 reads=[k_x, k_ss, k_nwb], writes=[k_xs])
            for kc in range(8):
                P.op("pe", "transpose", hps_b[:, kc * 128:(kc + 1) * 128], xsb[:, kc * 128:(kc + 1) * 128], identb,
                     reads=[k_xs, k_cstb], writes=["hps"])
            P.op("act", "activation", out=hTt[:, :, tt * 128:(tt + 1) * 128],
                                                             in_=hps_b.rearrange("p (k t) -> p k t", t=128), func=AF.Copy,
                 reads=["hps"], writes=[k_hT])

        def proj_fm(cb):
            bk, k_bk = big()
            for kc in range(8):
                P.op("pe", "matmul", bk[:, 0:TB_], lhsT=wA[:, kc, cb * 128:(cb + 1) * 128], rhs=hTt[:, kc, :],
                                                           start=(kc == 0), stop=(kc == 7),
                     reads=[k_wA, k_hT], writes=[k_bk])
            return bk, k_bk

        if tb == 0:
            dump("hT", hTt[:, 0, :], k_hT)
            dump("xs", xs[1][0][:, 0:TB_], xs[1][1])
            dump("ss", ss[1][0][:, 0:4], ss[1][1], 4)
        def lru_gen():
            for pt in range(2):
                bk, k_bk = proj_fm(pt)
                evac("act", xa[pt][0][:, 3:TB_ + 3], bk[:, 0:TB_], [k_bk], [xa[pt][1]])
                yield
                if tb == 0 and pt == 0:
                    dump("xa", xa[0][0][:, 3:TB_ + 3], xa[0][1])
                bk, k_bk = proj_fm(2 + pt)
                P.op("act", "activation", out=sga[pt][0][:], in_=bk[:, 0:TB_], func=AF.Silu, reads=[k_bk], writes=[sga[pt][1]])
                yield
            for pt in range(2):
                xat, k_xa = xa[pt]
                cvt, k_cv = cv[pt]
                P.op("dve", "tensor_scalar", out=cvt[:], in0=xat[:, 3:TB_ + 3], scalar1=pvc("cw3", pt), scalar2=pvc("cb", pt),
                                                                             op0=ALU.mult, op1=ALU.add, reads=[k_xa, k_pv], writes=[k_cv])
                yield
                for j in range(3):
                    P.op("dve", "scalar_tensor_tensor", out=cvt[:], in0=xat[:, j:j + TB_], scalar=pvc(f"cw{j}", pt), in1=cvt[:],
                                                                                             op0=ALU.mult, op1=ALU.add, reads=[k_xa, k_cv, k_pv], writes=[k_cv])
                    yield
                P.op("pool", "tensor_copy", out=xat[:, 0:3], in_=xat[:, TB_:TB_ + 3], reads=[k_xa], writes=[k_xa])
                yield
                P.op("pool", "tensor_copy", out=cvb[pt][0][:], in_=cvt[:], reads=[k_cv], writes=[cvb[pt][1]])
                yield
            for n in range(2):
                for dblk in range(2):
                    bk, k_bk = big()
                    for cc in range(2):
                        P.op("pe", "matmul", bk[:, 0:TB_], lhsT=gw[:, n, cc, dblk * 128:(dblk + 1) * 128], rhs=cvb[cc][0][:],
                                                                                   start=(cc == 0), stop=(cc == 1),
                             reads=[k_gw, cvb[cc][1]], writes=[k_bk])
                    dst = (lr if n == 0 else li)[dblk]
                    P.op("act", "activation", out=dst[0][:], in_=bk[:, 0:TB_], func=AF.Sigmoid,
                                                                                      bias=pvc("gbr" if n == 0 else "gbi", dblk),
                         reads=[k_bk, k_pv], writes=[dst[1]])
                    yield
            for pt in range(2):
                cvt, k_cv = cv[pt]
                P.op("act", "activation", out=la[:], in_=lr[pt][0][:], func=AF.Exp, scale=dvc("lrs", pt), reads=[lr[pt][1], k_dv], writes=[k_la])
                yield
                P.op("dve", "scalar_tensor_tensor", out=lt[:], in0=la[:], scalar=-1.0, in1=la[:], op0=ALU.mult, op1=ALU.mult, reads=[k_la], writes=[k_lt])
                yield
                P.op("dve", "tensor_scalar", out=lt[:], in0=lt[:], scalar1=1.0, scalar2=0.0, op0=ALU.add, op1=ALU.max, reads=[k_lt], writes=[k_lt])
                yield
                P.op("act", "activation", out=lt[:], in_=lt[:], func=AF.Sqrt, reads=[k_lt], writes=[k_lt])
                yield
                if tb == 0:
                    P.op("dve", "memset", lt[:, 0:1], 1.0, reads=[k_lt], writes=[k_lt])
                    yield
                P.op("dve", "tensor_tensor", out=lt[:], in0=lt[:], in1=li[pt][0][:], op=ALU.mult, reads=[k_lt, li[pt][1]], writes=[k_lt])
                yield
                P.op("dve", "tensor_tensor", out=lt[:], in0=lt[:], in1=cvt[:], op=ALU.mult, reads=[k_lt, k_cv], writes=[k_lt])
                yield
                P.op("dve", "tensor_tensor_scan", out=lh[:], data0=la[:], data1=lt[:], initial=hcar[pt][0][:, 0:1], op0=ALU.mult, op1=ALU.add,
                     reads=[k_la, k_lt, hcar[pt][1]], writes=[k_lh])
                yield
                P.op("dve", "tensor_copy", out=hcar[pt][0][:], in_=lh[:, TB_ - 1:TB_], reads=[k_lh], writes=[hcar[pt][1]])
                yield
                if tb == 0 and pt == 0:
                    dump("cv", cv[0][0][:], cv[0][1]); dump("lr", lr[0][0][:], lr[0][1]); dump("li", li[0][0][:], li[0][1])
                    dump("la", la[:], k_la); dump("lt", lt[:], k_lt); dump("lh", lh[:], k_lh); dump("sga", sga[0][0][:], sga[0][1])
                P.op("dve", "tensor_tensor", out=ysA[pt][0][:], in0=lh[:], in1=sga[pt][0][:], op=ALU.mult, reads=[k_lh, sga[pt][1]], writes=[ysA[pt][1]])
                yield
                out_toks.append(P.dma("sp", "dma_start", out=ys_d[0, pt * 128:(pt + 1) * 128, t0:t0 + TB_], in_=ysA[pt][0][:],
                                      reads=[ysA[pt][1]]))

            return
            yield

        ret_done = [True]

        def ret_gen():
            if DO_C:
                for pt in range(2):
                    bk, k_bk = proj_fm(13 + pt)
                    evac("act", qf[pt][0][:], bk[:, 0:TB_], [k_bk], [qf[pt][1]], scale=1.0 / 16.0)
                    yield
                    bk, k_bk = proj_fm(15 + pt)
                    evac("dve", kf[pt][0][:], bk[:, 0:TB_], [k_bk], [kf[pt][1]])
                    yield
                for tt in range(NTT):
                    bk, k_bk = big()
                    for kc in range(8):
                        P.op("pe", "matmul", bk[:, 0:512], lhsT=hTt[:, kc, tt * 128:(tt + 1) * 128], rhs=wA[:, kc, 2176:2688],
                                                                          start=(kc == 0), stop=(kc == 7),
                             reads=[k_wA, k_hT], writes=[k_bk])
                    evac("dve", v_tok[tt][0][:], bk[:, 0:256], [k_bk], [v_tok[tt][1]])
                    yield
                    P.op("act", "activation", out=sg_tok[tt][0][:], in_=bk[:, 256:512], func=AF.Silu,
                         reads=[k_bk], writes=[sg_tok[tt][1]])
                    yield

            if DO_C and rstop >= 2:
                P.dma("sp", "dma_start", out=cosb[:], in_=cos_d[:, t0:t0 + TB_], writes=[k_cos])
                P.dma("sp", "dma_start", out=sinb[:], in_=sin_d[:, t0:t0 + TB_], writes=[k_sin])
                for src, dst in ((qf, qT), (kf, kT)):
                    s1, k1 = src[0]
                    s2, k2 = src[1]
                    P.op("dve", "tensor_tensor", out=rt[0][0][:], in0=s1[:], in1=cosb[:], op=ALU.mult, reads=[k1, k_cos], writes=[rt[0][1]])
                    yield
                    P.op("pool", "tensor_tensor", out=rt[1][0][:], in0=s2[:], in1=sinb[:], op=ALU.mult, reads=[k2, k_sin], writes=[rt[1][1]])
                    yield
                    P.op("dve", "tensor_tensor", out=dst[0][0][:], in0=rt[0][0][:], in1=rt[1][0][:], op=ALU.subtract, reads=[rt[0][1], rt[1][1]], writes=[dst[0][1]])
                    yield
                    P.op("pool", "tensor_tensor", out=rt[2][0][:], in0=s1[:], in1=sinb[:], op=ALU.mult, reads=[k1, k_sin], writes=[rt[2][1]])
                    yield
                    P.op("dve", "tensor_tensor", out=rt[3][0][:], in0=s2[:], in1=cosb[:], op=ALU.mult, reads=[k2, k_cos], writes=[rt[3][1]])
                    yield
                    P.op("dve", "tensor_tensor", out=dst[1][0][:], in0=rt[2][0][:], in1=rt[3][0][:], op=ALU.add, reads=[rt[2][1], rt[3][1]], writes=[dst[1][1]])
                    yield
                for pt in range(2):
                    P.op("pool", "tensor_tensor", out=qd[pt][0][:], in0=qT[pt][0][:], in1=qdec[:, 0:TB_], op=ALU.mult, reads=[qT[pt][1], k_cst], writes=[qd[pt][1]])
                    yield
                for c in range(NTT):
                    cs = slice(c * 128, (c + 1) * 128)
                    stm, k_stm = STm[c % 2]
                    kd, k_kd = kdt[c % 2]
                    ps, k_ps = quarter()
                    for pt in range(2):
                        P.op("pe", "matmul", ps, lhsT=kT[pt][0][:, cs], rhs=qT[pt][0][:, cs], start=(pt == 0), stop=(pt == 1),
                             reads=[kT[pt][1], qT[pt][1]], writes=[k_ps])
                    P.op("dve", "tensor_tensor", out=stm[:], in0=ps, in1=rmask, op=ALU.mult, reads=[k_ps, k_cst], writes=[k_stm])
                    yield
                    if rstop < 5:
                        continue
                    psq, k_psq = quarter()
                    psq_b = psq.bitcast(BF16)
                    for pt in range(2):
                        P.op("pe", "transpose", psq_b[:, pt * 128:(pt + 1) * 128], kT[pt][0][:, cs], identb,
                             reads=[kT[pt][1], k_cstb], writes=[k_psq])
                    P.op("dve", "tensor_scalar", out=kd[:], in0=psq_b, scalar1=pvc("kdec"), scalar2=None, op0=ALU.mult, reads=[k_psq, k_pv], writes=[k_kd])
                    yield
                    if rstop < 6:
                        continue
                    pso, k_pso = banks[2][:, 0:256], ["bank2"]
                    P.op("pe", "matmul", pso, lhsT=stm[:], rhs=v_tok[c][0][:], start=True, stop=False, reads=[k_stm, v_tok[c][1]], writes=k_pso)
                    for pt in range(2):
                        P.op("pe", "matmul", pso, lhsT=qd[pt][0][:, cs], rhs=stb[pt][0][:], start=False, stop=(pt == 1),
                             reads=[qd[pt][1], stb[pt][1]], writes=k_pso)
                    for dc in range(2 if rstop >= 7 else 0):
                        pss, k_pss = half()
                        P.op("pe", "matmul", pss, lhsT=kd[:, dc * 128:(dc + 1) * 128], rhs=v_tok[c][0][:], start=True, stop=True,
                             reads=[k_kd, v_tok[c][1]], writes=k_pss)
                        P.op("dve", "scalar_tensor_tensor", out=stf[dc][0][:], in0=stf[dc][0][:], scalar=pvc("cdec"), in1=pss, op0=ALU.mult, op1=ALU.add,
                             reads=[stf[dc][1], k_pv] + k_pss, writes=[stf[dc][1]])
                        yield
                        P.op("act", "activation", out=stb[dc][0][:], in_=stf[dc][0][:], func=AF.Copy, reads=[stf[dc][1]], writes=[stb[dc][1]])
                        yield
                    if rstop < 8:
                        continue
                    s6, k_s6 = st6[c % 2]
                    mvt, k_mv = mv[c % 2]
                    ynt, k_yn = yn[c % 2]
                    ycbt, k_ycb = ycb[c % 2]
                    P.op("dve", "tensor_reduce", out=mvt[:, 0:1], in_=pso, axis=mybir.AxisListType.X, op=ALU.add, reads=k_pso, writes=[k_mv])
                    yield
                    P.op("dve", "tensor_scalar", out=mvt[:, 1:2], in0=mvt[:, 0:1], scalar1=-1.0 / 256.0, scalar2=None, op0=ALU.mult, reads=[k_mv], writes=[k_mv])
                    yield
                    P.op("act", "activation", out=ynt[:], in_=pso, func=AF.Identity, bias=mvt[:, 1:2], reads=k_pso + [k_mv], writes=[k_yn])
                    yield
                    P.op("act", "activation", out=junk[:, 0:256], in_=ynt[:], func=AF.Square, accum_out=s6[:, 0:1], reads=[k_yn], writes=[k_junk, k_s6])
                    yield
                    P.op("dve", "tensor_scalar", out=s6[:, 1:2], in0=s6[:, 0:1], scalar1=1.0 / 256.0, scalar2=1e-5, op0=ALU.mult, op1=ALU.add, reads=[k_s6], writes=[k_s6])
                    yield
                    P.op("act", "activation", out=s6[:, 2:3], in_=s6[:, 1:2], func=AF.Sqrt, reads=[k_s6], writes=[k_s6])
                    yield
                    P.op("dve", "reciprocal", out=s6[:, 3:4], in_=s6[:, 2:3], reads=[k_s6], writes=[k_s6])
                    yield
                    P.op("dve", "scalar_tensor_tensor", out=ycbt[:], in0=ynt[:], scalar=s6[:, 3:4], in1=sg_tok[c][0][:], op0=ALU.mult, op1=ALU.mult,
                         reads=[k_yn, k_s6, sg_tok[c][1]], writes=[k_ycb])
                    yield
                    if rstop < 9:
                        continue
                    pst, k_pst = quarter()
                    pst_b = pst.bitcast(BF16)
                    for pt in range(2):
                        P.op("pe", "transpose", pst_b[:, pt * 128:(pt + 1) * 128], ycbt[:, pt * 128:(pt + 1) * 128], identb,
                             reads=[k_ycb, k_cstb], writes=[k_pst])
                    for pt in range(2):
                        evac(copy_eng(), ysC[pt][0][:, cs], pst_b[:, pt * 128:(pt + 1) * 128], [k_pst], [ysC[pt][1]])
                        yield
                for pt in range(2 if rstop >= 10 else 0):
                    out_toks.append(P.dma("sp", "dma_start", out=ys_d[2, pt * 128:(pt + 1) * 128, t0:t0 + TB_], in_=ysC[pt][0][:],
                                          reads=[ysC[pt][1]]))
            return
            yield

        if DO_B:
            bk, k_bk = proj_fm(12)
            evac(copy_eng(), pWL[:, 1:TB_ + 1], bk[:, 0:TB_], [k_bk], [k_pWL])
            def shift(buf, k_buf, dst, k_dst, mu, omu):
                P.op("act", "activation", out=dst[:], in_=buf[:, 0:TB_], func=AF.Copy, scale=mu,
                     reads=[k_buf, k_pv], writes=[k_dst])
                P.op("dve", "scalar_tensor_tensor", out=dst[:], in0=buf[:, 1:TB_ + 1], scalar=omu, in1=dst[:], op0=ALU.mult, op1=ALU.add,
                     reads=[k_buf, k_dst, k_dv], writes=[k_dst])
                P.op("pool", "tensor_copy", out=buf[:, 0:1], in_=buf[:, TB_:TB_ + 1], reads=[k_buf], writes=[k_buf])

            shift(pWL, k_pWL, sWL, k_sWL, pvc("mu_wl"), dvc("omu_wl"))
            P.op("act", "activation", out=tnh[:], in_=sWL[:], func=AF.Tanh, reads=[k_sWL], writes=[k_tnh])
            P.op("pool", "tensor_copy", out=sWLb[:], in_=sWL[:], reads=[k_sWL], writes=[k_sWLb])

        def rwkv_gen(pt):
            sS = sS_pt[pt]
            f = f_pt[pt]
            NM1a = NM1a_l[pt]; NM2a = NM2a_l[pt]; LNP = LNP_l[pt]; TKa = TKa_l[pt]; Zb = Zb_l[pt]; Ub = Ub_l[pt]
            wc8, k_wc8 = wc8_l[pt]
            for j, nm in enumerate("rkvg"):
                bk, k_bk = proj_fm(4 + 2 * j + pt)
                evac(copy_eng(), pB[nm][pt][0][:, 1:TB_ + 1], bk[:, 0:TB_], [k_bk], [pB[nm][pt][1]])
                yield
            F_ = lambda nm: f[nm][0]
            K_ = lambda nm: f[nm][1]
            for nm in "rkvg":
                shift(pB[nm][pt][0], pB[nm][pt][1], sS[nm][0], sS[nm][1], pvc("mu_" + nm, pt), dvc("omu_" + nm, pt))
            s_r, s_k, s_v, s_g = (sS[nm][0] for nm in "rkvg")
            ksr, ksk, ksv, ksg = (sS[nm][1] for nm in "rkvg")
            bk, k_bk = big()
            P.op("pe", "matmul", bk[:, 0:TB_], lhsT=w2p[:, pt * 128:(pt + 1) * 128], rhs=tnh[:], start=True, stop=True,
                 reads=[k_w2p, k_tnh], writes=[k_bk])
            P.op("act", "activation", out=F_("logw")[:], in_=bk[:, 0:TB_], func=AF.Sigmoid, bias=pvc("w0", pt), reads=[k_bk, k_pv], writes=[K_("logw")])
            yield
            bk, k_bk = big()
            P.op("pe", "matmul", bk[:, 0:TB_], lhsT=a2p[:, pt * 128:(pt + 1) * 128], rhs=sWLb[:], start=True, stop=True,
                 reads=[k_a2p, k_sWLb], writes=[k_bk])
            P.op("act", "activation", out=F_("aa")[:], in_=bk[:, 0:TB_], func=AF.Sigmoid, bias=pvc("a0", pt), reads=[k_bk, k_pv], writes=[K_("aa")])
            yield
            P.op("act", "activation", out=F_("logw")[:], in_=F_("logw")[:], func=AF.Copy, scale=-0.6065306597126334,
                 reads=[K_("logw")], writes=[K_("logw")])
            yield
            for ch in range(NCH):
                P.op("dve", "tensor_tensor_scan", out=F_("cum")[:, ch * 64:(ch + 1) * 64], data0=ones64, data1=F_("logw")[:, ch * 64:(ch + 1) * 64],
                                                                 initial=0.0, op0=ALU.mult, op1=ALU.add, reads=[K_("logw"), k_cst], writes=[K_("cum")])
                yield
            P.op("act", "activation", out=F_("Wt")[:], in_=F_("cum")[:], func=AF.Exp, reads=[K_("cum")], writes=[K_("Wt")])
            yield
            P.op("act", "activation", out=F_("iW")[:], in_=F_("cum")[:], func=AF.Exp, scale=-1.0, reads=[K_("cum")], writes=[K_("iW")])
            yield
            P.op("pool", "tensor_tensor", out=F_("Wp")[:], in0=F_("cum")[:], in1=F_("logw")[:], op=ALU.subtract, reads=[K_("cum"), K_("logw")], writes=[K_("Wp")])
            yield
            P.op("act", "activation", out=F_("Wp")[:], in_=F_("Wp")[:], func=AF.Exp, reads=[K_("Wp")], writes=[K_("Wp")])
            yield
            P.op("act", "activation", out=F_("kk")[:], in_=s_k[:], func=AF.Copy, scale=pvc("k_k", pt),
                 reads=[ksk, k_pv], writes=[K_("kk")])
            yield
            P.op("pool", "tensor_tensor", out=F_("sq")[:], in0=F_("kk")[:], in1=F_("kk")[:], op=ALU.mult, reads=[K_("kk")], writes=[K_("sq")])
            yield
            bk, k_bk = big()
            P.op("pe", "matmul", bk[:, 0:TB_], lhsT=b64, rhs=F_("sq")[:], start=True, stop=True, reads=[k_cst, K_("sq")], writes=[k_bk])
            P.op("act", "activation", out=F_("rn")[:], in_=bk[:, 0:TB_], func=AF.Sqrt, reads=[k_bk], writes=[K_("rn")])
            yield
            P.op("dve", "tensor_scalar", out=F_("rn")[:], in0=F_("rn")[:], scalar1=1e-12, scalar2=None, op0=ALU.max, reads=[K_("rn")], writes=[K_("rn")])
            yield
            P.op("dve", "reciprocal", out=F_("rn")[:], in_=F_("rn")[:], reads=[K_("rn")], writes=[K_("rn")])
            yield
            P.op("dve", "tensor_tensor", out=F_("kk")[:], in0=F_("kk")[:], in1=F_("rn")[:], op=ALU.mult, reads=[K_("kk"), K_("rn")], writes=[K_("kk")])
            yield
            P.op("pool", "tensor_scalar", out=F_("kmod")[:], in0=F_("aa")[:], scalar1=-1.0, scalar2=pvc("k_a", pt), op0=ALU.add, op1=ALU.mult,
                 reads=[K_("aa"), k_pv], writes=[K_("kmod")])
            yield
            P.op("dve", "scalar_tensor_tensor", out=F_("kmod")[:], in0=F_("kmod")[:], scalar=1.0, in1=s_k[:], op0=ALU.add, op1=ALU.mult,
                 reads=[K_("kmod"), ksk], writes=[K_("kmod")])
            yield
            P.op("pool", "tensor_tensor", out=F_("kka")[:], in0=F_("kk")[:], in1=F_("aa")[:], op=ALU.mult, reads=[K_("kk"), K_("aa")], writes=[K_("kka")])
            yield
            P.op("dve", "tensor_tensor", out=F_("iWC")[:].rearrange("p (c t) -> p c t", t=64), in0=F_("iW")[:].rearrange("p (c t) -> p c t", t=64),
                                                  in1=F_("Wt")[:].rearrange("p (c t) -> p c t", t=64)[:, :, 63:64].broadcast_to([128, NCH, 64]), op=ALU.mult,
                 reads=[K_("iW"), K_("Wt")], writes=[K_("iWC")])
            yield
            bdar, k_bdar = BD_AR[pt]
            v3 = lambda t, hh: t[hh * 64:(hh + 1) * 64, :].rearrange("p (c t) -> p c t", t=64)
            for hh in range(2):
                hs_ = slice(hh * 64, (hh + 1) * 64)
                P.op("dve", "scalar_tensor_tensor", out=bdar[hs_, :, 0, hh * 64:(hh + 1) * 64], in0=v3(F_("kk"), hh), scalar=-1.0,
                                                                             in1=v3(F_("Wp"), hh), op0=ALU.mult, op1=ALU.mult,
                     reads=[K_("kk"), K_("Wp")], writes=[k_bdar])
                yield
                P.op("pool", "tensor_tensor", out=bdar[hs_, :, 1, hh * 64:(hh + 1) * 64], in0=v3(s_r, hh), in1=v3(F_("Wt"), hh), op=ALU.mult,
                     reads=[ksr, K_("Wt")], writes=[k_bdar])
                yield
                for nm, a_, b_, eng in [("B", "kka", "iW", "dve"), ("K", "kmod", "iW", "pool"), ("Bh", "kka", "iWC", "dve"), ("Kh", "kmod", "iWC", "pool")]:
                    P.op(eng, "tensor_tensor", out=BDn[nm][pt][0][hs_, :, hh * 64:(hh + 1) * 64], in0=v3(F_(a_), hh),
                                                                                            in1=v3(F_(b_), hh), op=ALU.mult,
                         reads=[K_(a_), K_(b_)], writes=[BDn[nm][pt][1]])
                P.op("act", "activation", out=BDn["V"][pt][0][hs_, :, hh * 64:(hh + 1) * 64], in_=v3(s_v, hh), func=AF.Copy,
                     reads=[ksv], writes=[BDn["V"][pt][1]])
                yield
            P.op("dve", "scalar_tensor_tensor", out=F_("sq")[:], in0=s_r[:], scalar=pvc("r_k", pt), in1=F_("kmod")[:], op0=ALU.mult, op1=ALU.mult,
                 reads=[ksr, K_("kmod"), k_pv], writes=[K_("sq")])
            yield
            bk, k_bk = big()
            P.op("pe", "matmul", bk[:, 0:TB_], lhsT=b64, rhs=F_("sq")[:], start=True, stop=True, reads=[k_cst, K_("sq")], writes=[k_bk])
            P.op("dve", "tensor_tensor", out=F_("bon")[:], in0=bk[:, 0:TB_], in1=s_v[:], op=ALU.mult, reads=[k_bk, ksv], writes=[K_("bon")])
            yield
            P.op("act", "activation", out=F_("sgg")[:], in_=s_g[:], func=AF.Silu, reads=[ksg], writes=[K_("sgg")])
            yield

            s0f, k_s0f = S0f[pt]
            s0b, k_s0b = S0b[pt]
            bB, k_bB = BDn["B"][pt]
            bK, k_bK = BDn["K"][pt]
            bBh, k_bBh = BDn["Bh"][pt]
            bKh, k_bKh = BDn["Kh"][pt]
            bV, k_bV = BDn["V"][pt]
            P.op("dve", "tensor_copy", out=wc8[:], in_=F_("Wt")[:].rearrange("p (c t) -> p c t", t=64)[:, :, 63], reads=[K_("Wt")], writes=[k_wc8])
            yield
            nm1a, k_nm1a = NM1a
            nm2a, k_nm2a = NM2a
            tka, k_tka = TKa
            m1b = m1.unsqueeze(1).broadcast_to([128, 2, 256])
            m2b = m2.unsqueeze(1).broadcast_to([128, 4, 128])
            idb4 = ident_f.unsqueeze(1).broadcast_to([128, NCH, 128])
            for lhs_t, k_lhs, dst, k_dst in ((bB, k_bB, nm1a, k_nm1a), (bK, k_bK, nm2a, k_nm2a)):
                for c2 in range(NCH // 2):
                    bk, k_bk = big()
                    for u in range(2):
                        ch = 2 * c2 + u
                        P.op("pe", "matmul", bk[:, u * 256:(u + 1) * 256], lhsT=lhs_t[:, ch, :], rhs=bdar[:, ch, :, :].rearrange("p a b -> p (a b)"),
                             start=True, stop=True, reads=[k_lhs, k_bdar], writes=[k_bk])
                    P.op("dve", "tensor_tensor", out=dst[:, 2 * c2:2 * c2 + 2, :], in0=bk[:].rearrange("p (a b) -> p a b", b=256), in1=m1b, op=ALU.mult,
                         reads=[k_bk, k_cst], writes=[k_dst])
                    yield
            lcur, k_lcur = LNP[0]
            for c4 in range(NCH // 4):
                bk, k_bk = big()
                for u in range(4):
                    ch = 4 * c4 + u
                    P.op("pe", "matmul", bk[:, u * 128:(u + 1) * 128], lhsT=bdar[:, ch, 0, :], rhs=bB[:, ch, :], start=True, stop=True,
                         reads=[k_bdar, k_bB], writes=[k_bk])
                P.op("dve", "tensor_tensor", out=lcur[:, 4 * c4:4 * c4 + 4, :], in0=bk[:].rearrange("p (a b) -> p a b", b=128), in1=m2b, op=ALU.mult,
                     reads=[k_bk, k_cst], writes=[k_lcur])
                yield
            pcur, k_pcur = LNP[4]
            P.op("pool", "tensor_tensor", out=pcur[:], in0=nm1a[:, :, 0:128], in1=idb4, op=ALU.add, reads=[k_nm1a, k_cst], writes=[k_pcur])
            yield
            ncur_f = lambda ch: nm1a[:, ch, 0:128]
            k_ncur = k_nm1a
            for j in range(1, 6):
                lnew, k_lnew = LNP[j % 2]
                nnew, k_nnew = LNP[2 + j % 2]
                pnew, k_pnew = LNP[4 + j % 2]
                lb = []
                for c4 in range(NCH // 4):
                    bk, k_bk = big()
                    for u in range(4):
                        ch = 4 * c4 + u
                        P.op("pe", "matmul", bk[:, u * 128:(u + 1) * 128], lhsT=ncur_f(ch), rhs=lcur[:, ch, :], start=True, stop=True,
                             reads=[k_ncur, k_lcur], writes=[k_bk])
                    lb.append((bk, k_bk))
                nb = []
                if j <= 4:
                    for c4 in range(NCH // 4):
                        bk, k_bk = big()
                        for u in range(4):
                            ch = 4 * c4 + u
                            P.op("pe", "matmul", bk[:, u * 128:(u + 1) * 128], lhsT=lcur[:, ch, :], rhs=ncur_f(ch), start=True, stop=True,
                                 reads=[k_ncur, k_lcur], writes=[k_bk])
                        nb.append((bk, k_bk))
                for c4, (bk, k_bk) in enumerate(lb):
                    evac("act", lnew[:, 4 * c4:4 * c4 + 4, :], bk[:].rearrange("p (a b) -> p a b", b=128), [k_bk], [k_lnew])
                    yield
                for c4, (bk, k_bk) in enumerate(nb):
                    evac("dve", nnew[:, 4 * c4:4 * c4 + 4, :], bk[:].rearrange("p (a b) -> p a b", b=128), [k_bk], [k_nnew])
                    yield
                for c4 in range(NCH // 4):
                    bk, k_bk = big()
                    for u in range(4):
                        ch = 4 * c4 + u
                        P.op("pe", "matmul", bk[:, u * 128:(u + 1) * 128], lhsT=lnew[:, ch, :], rhs=pcur[:, ch, :], start=True, stop=True,
                             reads=[k_lnew, k_pcur], writes=[k_bk])
                    P.op("dve", "tensor_tensor", out=pnew[:, 4 * c4:4 * c4 + 4, :], in0=bk[:].rearrange("p (a b) -> p a b", b=128),
                         in1=pcur[:, 4 * c4:4 * c4 + 4, :], op=ALU.add, reads=[k_bk, k_pcur], writes=[k_pnew])
                    yield
                lcur, k_lcur = lnew, k_lnew
                if j <= 4:
                    ncur_f = (lambda nn: (lambda ch: nn[:, ch, :]))(nnew)
                    k_ncur = k_nnew
                pcur, k_pcur = pnew, k_pnew
            for c2 in range(NCH // 2):
                bk, k_bk = big()
                bkb = bk[:].bitcast(BF16)
                for u in range(2):
                    ch = 2 * c2 + u
                    for i3, (src, ks) in enumerate([(bV, k_bV), (bBh, k_bBh), (bKh, k_bKh)]):
                        P.op("pe", "transpose", bkb[:, (u * 3 + i3) * 128:(u * 3 + i3 + 1) * 128], src[:, ch, :], identb,
                             reads=[ks, k_cstb], writes=[k_bk])
                evac("act", tka[:, 2 * c2:2 * c2 + 2, :, :].rearrange("p a b c -> p (a b c)"), bkb[:, 0:768], [k_bk], [k_tka])
                yield
            def chain_gen():
                for ch in range(NCH):
                    zb, k_zb = Zb[ch % 2]
                    ub, k_ub = Ub[ch % 2]
                    ps, k_ps = quarter()
                    P.op("pe", "matmul", ps, lhsT=bdar[:, ch, 0, :], rhs=s0b[:], start=True, stop=False, reads=[k_bdar, k_s0b], writes=[k_ps])
                    P.op("pe", "matmul", ps, lhsT=nm2a[:, ch, 0:128], rhs=tka[:, ch, 0, :], start=False, stop=True, reads=[k_nm2a, k_tka], writes=[k_ps])
                    evac("dve", zb[:], ps, [k_ps], [k_zb])
                    yield
                    ps, k_ps = quarter()
                    P.op("pe", "matmul", ps, lhsT=pcur[:, ch, :], rhs=zb[:], start=True, stop=True, reads=[k_pcur, k_zb], writes=[k_ps])
                    evac("act", ub[:], ps, [k_ps], [k_ub])
                    yield
                    ps, k_ps = quarter()
                    P.op("pe", "matmul", ps, lhsT=tka[:, ch, 1, :], rhs=ub[:], start=True, stop=False, reads=[k_tka, k_ub], writes=[k_ps])
                    P.op("pe", "matmul", ps, lhsT=tka[:, ch, 2, :], rhs=tka[:, ch, 0, :], start=False, stop=True, reads=[k_tka], writes=[k_ps])
                    yq, k_ybk = quarter()
                    P.op("pe", "matmul", yq, lhsT=s0b[:], rhs=bdar[:, ch, 1, :], start=True, stop=False, reads=[k_bdar, k_s0b], writes=[k_ybk])
                    P.op("pe", "matmul", yq, lhsT=ub[:], rhs=nm1a[:, ch, 128:256], start=False, stop=False, reads=[k_ub, k_nm1a], writes=[k_ybk])
                    P.op("pe", "matmul", yq, lhsT=tka[:, ch, 0, :], rhs=nm2a[:, ch, 128:256], start=False, stop=True, reads=[k_tka, k_nm2a], writes=[k_ybk])
                    P.op("dve", "scalar_tensor_tensor", out=s0f[:], in0=s0f[:], scalar=wc8[:, ch:ch + 1], in1=ps, op0=ALU.mult, op1=ALU.add,
                         reads=[k_s0f, k_wc8, k_ps], writes=[k_s0f])
                    yield
                    P.op("act", "activation", out=s0b[:], in_=s0f[:], func=AF.Copy, reads=[k_s0f], writes=[k_s0b])
                    yield
                    P.op("dve", "tensor_copy", out=F_("yT")[0:64, ch * 64:(ch + 1) * 64], in_=yq[0:64, 0:64], reads=[k_ybk], writes=[K_("yT")])
                    yield
                    P.op("act", "activation", out=F_("yT")[64:128, ch * 64:(ch + 1) * 64], in_=yq[64:128, 64:128], func=AF.Copy, reads=[k_ybk], writes=[K_("yT")])
                    yield
            yield from chain_gen()
            bk, k_bk = big()
            P.op("pe", "matmul", bk[:, 0:TB_], lhsT=b64m, rhs=F_("yT")[:], start=True, stop=True, reads=[k_cst, K_("yT")], writes=[k_bk])
            P.op("dve", "tensor_tensor", out=F_("yT")[:], in0=F_("yT")[:], in1=bk[:, 0:TB_], op=ALU.subtract, reads=[k_bk, K_("yT")], writes=[K_("yT")])
            yield
            P.op("pool", "tensor_tensor", out=F_("sq")[:], in0=F_("yT")[:], in1=F_("yT")[:], op=ALU.mult, reads=[K_("yT")], writes=[K_("sq")])
            yield
            bk, k_bk = big()
            P.op("pe", "matmul", bk[:, 0:TB_], lhsT=b64m, rhs=F_("sq")[:], start=True, stop=True, reads=[k_cst, K_("sq")], writes=[k_bk])
            P.op("dve", "tensor_scalar", out=F_("rn")[:], in0=bk[:, 0:TB_], scalar1=64e-5, scalar2=None, op0=ALU.add, reads=[k_bk], writes=[K_("rn")])
            yield
            P.op("act", "activation", out=F_("rn")[:], in_=F_("rn")[:], func=AF.Sqrt, reads=[K_("rn")], writes=[K_("rn")])
            yield
            P.op("dve", "reciprocal", out=F_("rn")[:], in_=F_("rn")[:], reads=[K_("rn")], writes=[K_("rn")])
            yield
            P.op("dve", "tensor_tensor", out=F_("yT")[:], in0=F_("yT")[:], in1=F_("rn")[:], op=ALU.mult, reads=[K_("yT"), K_("rn")], writes=[K_("yT")])
            yield
            P.op("dve", "tensor_scalar", out=F_("yT")[:], in0=F_("yT")[:], scalar1=pvc("lnw", pt), scalar2=pvc("lnb", pt), op0=ALU.mult, op1=ALU.add,
                 reads=[K_("yT"), k_pv], writes=[K_("yT")])
            yield
            P.op("dve", "tensor_tensor", out=F_("yT")[:], in0=F_("yT")[:], in1=F_("bon")[:], op=ALU.add, reads=[K_("yT"), K_("bon")], writes=[K_("yT")])
            yield
            P.op("dve", "tensor_tensor", out=ysB[pt][0][:], in0=F_("yT")[:], in1=F_("sgg")[:], op=ALU.mult, reads=[K_("yT"), K_("sgg")], writes=[ysB[pt][1]])
            yield
            out_toks.append(P.dma("sp", "dma_start", out=ys_d[1, pt * 128:(pt + 1) * 128, t0:t0 + TB_], in_=ysB[pt][0][:],
                                  reads=[ysB[pt][1]]))
            return
            yield

        gens = []
        gid = {}
        if DO_A or DO_C:
            def s1_gen():
                if DO_A:
                    yield from lru_gen()
                if DO_C:
                    yield from ret_gen()
            gens.append(s1_gen())
            gid[id(gens[-1])] = 0
        if DO_B:
            gens.append(rwkv_gen(0))
            gid[id(gens[-1])] = 1
            gens.append(rwkv_gen(1))
            gid[id(gens[-1])] = 2
        while gens:
            for g_ in list(gens):
                cur_stream[0] = gid[id(g_)]
                try:
                    next(g_)
                except StopIteration:
                    gens.remove(g_)
        cur_stream[0] = 0

    return out_toks


def _consts(g, S):
    cst = np.zeros((128, NCST), np.float32)
    cst[:, CST["ident"]:CST["ident"] + 128] = np.eye(128, dtype=np.float32)
    blk = np.kron(np.eye(2, dtype=np.float32), np.ones((64, 64), np.float32))
    cst[:, CST["b64"]:CST["b64"] + 128] = blk
    cst[:, CST["b64m"]:CST["b64m"] + 128] = blk / 64.0
    su = np.triu(np.ones((64, 64), np.float32), 1)
    ui = np.triu(np.ones((64, 64), np.float32), 0)
    cst[:, CST["m1"]:CST["m1"] + 128] = np.kron(np.eye(2, dtype=np.float32), su)
    cst[:, CST["m1"] + 128:CST["m1"] + 256] = np.kron(np.eye(2, dtype=np.float32), ui)
    cst[:, CST["m2"]:CST["m2"] + 128] = np.kron(np.eye(2, dtype=np.float32), su.T)
    gam = np.float32(1.0) - np.exp2(np.float32(-5.0 - g)).astype(np.float32)
    lg = np.log1p(-np.exp2(np.float32(-5.0 - g))).astype(np.float64)
    idx = np.arange(128)
    rel = idx[None, :] - idx[:, None]
    cst[:, CST["rmask"]:CST["rmask"] + 128] = np.where(rel >= 0, np.exp(lg * np.maximum(rel, 0)), 0.0).astype(np.float32)
    qrow = np.exp(lg * (idx + 1.0)).astype(np.float32)
    cst[:, CST["qdec"]:CST["qdec"] + 512] = np.tile(qrow[None, :], (128, 4))
    cst[:, CST["ones"]:CST["ones"] + 64] = 1.0
    kdec = np.exp(lg * (127.0 - idx)).astype(np.float32)
    cdec = np.float32(np.exp(lg * 128.0))
    half = 128
    inv_freq = (1.0 / (10000.0 ** np.linspace(0.0, 1.0, half, dtype=np.float32))).astype(np.float32)
    ang = np.arange(S, dtype=np.float32)[None, :] * inv_freq[:, None]
    return cst, kdec, cdec, np.cos(ang).astype(np.float32), np.sin(ang).astype(np.float32)


def _prep_A(inp, l, b, g, S, light=False):
    w_in = inp["w_in"][l]
    sl = lambda off: np.arange(off + g * 256, off + (g + 1) * 256)
    cols = np.concatenate([sl(0), sl(OFF_GA), sl(OFF_B), sl(OFF_B + 1024), sl(OFF_B + 2048), sl(OFF_B + 3072),
                           np.arange(OFF_B + 4096, OFF_B + 4224), sl(OFF_C), sl(OFF_C + 1024), sl(OFF_C + 2048), sl(OFF_C + 3072)])
    assert cols.shape[0] == NCOLA
    cst, kdec, cdec, cosT, sinT = _consts(g, S)
    pv = np.zeros((128, NPV), np.float32)

    def put(nm, vec256):
        pv[:, PV[nm]:PV[nm] + 2] = np.asarray(vec256, np.float32).reshape(2, 128).T
    ch = slice(g * 256, (g + 1) * 256)
    for j in range(4):
        put(f"cw{j}", inp["conv_w"][l][j, ch])
    put("cb", inp["conv_b"][l][ch])
    put("gbr", inp["lru_gate_b"][l][0, ch])
    put("gbi", inp["lru_gate_b"][l][1, ch])
    put("lam", inp["lru_lambda"][l][ch])
    mu = inp["shift_mu"][l]
    for j, nm in enumerate("rkvg"):
        put("mu_" + nm, mu[j * 1024 + g * 256: j * 1024 + (g + 1) * 256])
    pv[:, PV["mu_wl"]] = mu[4096:4224]
    put("w0", inp["decay_w0"][l][ch])
    put("a0", inp["iclr_a0"][l][ch])
    put("k_k", inp["k_k"][l][ch])
    put("k_a", inp["k_a"][l][ch])
    put("r_k", inp["r_k"][l].reshape(-1)[ch])
    put("lnw", inp["lnx_w"][l][ch])
    put("lnb", inp["lnx_b"][l][ch])
    pv[:, PV["kdec"]] = kdec
    pv[:, PV["cdec"]] = cdec
    z64 = np.zeros((64, 256), np.float32)
    return {
        "x": None if light else np.ascontiguousarray(inp["x"][b, :S]),
        "wA": None if light else np.ascontiguousarray(w_in[:, cols]),
        "gw": np.ascontiguousarray(inp["lru_gate_w"][l][:, g]),
        "w2p": np.concatenate([inp["decay_w2"][l][:, ch], z64], 0),
        "a2p": np.concatenate([z64, inp["iclr_a2"][l][:, ch]], 0),
        "pv": pv,
        "nwb": np.ascontiguousarray(np.broadcast_to(inp["norm_w"][l][None, :], (128, D))),
        "cst": cst, "cosT": cosT, "sinT": sinT,
    }


def emit_B(cx, NT, last, d):
    nc = cx.nc
    P = cx.P
    sb = cx.sb
    x_d, ys_d, nwb_d, fnwb_d, wM_d, bm_d, wBr_d, wO_d, id_d, sel_d, out_d = (d[k] for k in
        ("x", "ys", "nwb", "fnwb", "wM", "bm", "wBr", "wO", "ident", "sel", "out"))
    banks = cx.banks
    big_i = [0]

    def big():
        i = big_i[0] % 6
        big_i[0] += 1
        return banks[i], f"bank{i}"
    hps_b = banks[6][:].bitcast(BF16)
    mps_b = banks[7][:].bitcast(BF16)

    wM, k_wM = sb([128, 8, 3 * D], BF16, "wM")
    wBr, k_wBr = sb([128, 3, 8, D], BF16, "wBr")
    wO, k_wO = sb([128, 8, D], BF16, "wO")
    nwb, k_nwb = sb([128, D], F32, "nwb")
    fnwb, k_fnwb = sb([128, D], F32, "fnwb")
    bm, k_bm = sb([6, 512], F32, "bm")
    sel, k_sel = sb([6, 6, 128], F32, "sel")
    idf, k_idf = sb([128, 128], F32, "idf")
    idb, k_idb = sb([128, 128], BF16, "idb")
    wM_v = wM_d.rearrange("(kc p) n -> p kc n", p=128)
    for kc in range(8):
        P.dma("pool", "dma_start", out=wM[:, kc, :], in_=wM_v[:, kc, :], writes=[k_wM], par=True)
    for n in range(3):
        wv = wBr_d[n].rearrange("(kc p) d -> p kc d", p=128)
        for kc in range(0, 8, 4):
            P.dma("pool", "dma_start", out=wBr[:, n, kc:kc + 4, :], in_=wv[:, kc:kc + 4, :], writes=[k_wBr], par=True)
    wv = wO_d.rearrange("(kc p) d -> p kc d", p=128)
    for kc in range(0, 8, 4):
        P.dma("pool", "dma_start", out=wO[:, kc:kc + 4, :], in_=wv[:, kc:kc + 4, :], writes=[k_wO], par=True)
    P.dma("sp", "dma_start", out=nwb[:], in_=nwb_d, writes=[k_nwb])
    P.dma("sp", "dma_start", out=fnwb[:], in_=fnwb_d, writes=[k_fnwb])
    P.dma("sp", "dma_start", out=bm[:], in_=bm_d, writes=[k_bm])
    P.dma("sp", "dma_start", out=idf[:], in_=id_d, writes=[k_idf])
    P.op("dve", "tensor_copy", out=idb[:], in_=idf[:], reads=[k_idf], writes=[k_idb])
    P.dma("sp", "dma_start", out=sel[:], in_=sel_d, writes=[k_sel])

    YB = 256
    ysb = [sb([128, 3, 8, YB], BF16, f"ysb{i}") for i in range(2)]
    xt = [sb([128, D], F32, f"xt{i}") for i in range(2)]
    junk, k_junk = sb([128, D], BF16, "junk")
    ss = [sb([128, 8], F32, f"ss{i}") for i in range(2)]
    xs = [sb([128, D], BF16, f"xs{i}") for i in range(2)]
    hT = [sb([128, 8, 128], BF16, f"hT{i}") for i in range(2)]
    gt = [sb([128, 512], F32, f"gt{i}") for i in range(2)]
    tmp = [sb([128, 512], F32, f"tmp{i}") for i in range(2)]
    mg = [sb([128, D], F32, f"mg{i}") for i in range(1)] * 2
    mgb = [sb([128, D], BF16, f"mgb{i}") for i in range(1)] * 2
    mT = [sb([128, 8, 128], BF16, f"mT{i}") for i in range(2)]
    xn = [sb([128, D], F32, f"xn{i}") for i in range(1)] * 2
    out_toks = []

    def rms(src, k_src, ssb, k_ss, wb, k_wb, dst, k_dst):
        P.op("act", "activation", out=junk[:], in_=src[:], func=AF.Square, accum_out=ssb[:, 0:1], reads=[k_src], writes=[k_junk, k_ss])
        P.op("dve", "tensor_scalar", out=ssb[:, 1:2], in0=ssb[:, 0:1], scalar1=1.0 / D, scalar2=EPS, op0=ALU.mult, op1=ALU.add, reads=[k_ss], writes=[k_ss])
        P.op("act", "activation", out=ssb[:, 2:3], in_=ssb[:, 1:2], func=AF.Sqrt, reads=[k_ss], writes=[k_ss])
        P.op("dve", "reciprocal", out=ssb[:, 3:4], in_=ssb[:, 2:3], reads=[k_ss], writes=[k_ss])
        P.op("dve", "scalar_tensor_tensor", out=dst[:], in0=src[:], scalar=ssb[:, 3:4], in1=wb[:], op0=ALU.mult, op1=ALU.mult,
             reads=[k_src, k_ss, k_wb], writes=[k_dst])

    for tb in range(NT // YB):
        yb, k_yb = ysb[tb % 2]
        for n in range(3):
            yv = ys_d[n].rearrange("(wc p) t -> p wc t", p=128)
            P.dma("sp", "dma_start", out=yb[:, n, :, :], in_=yv[:, :, tb * YB:(tb + 1) * YB], writes=[k_yb])
        for tt in range(YB // 128):
            ti = tb * (YB // 128) + tt
            r0 = ti * 128
            xb_, k_x = xt[ti % 2]
            ssb, k_ss = ss[ti % 2]
            xsb, k_xs = xs[ti % 2]
            hTt, k_hT = hT[ti % 2]
            mgt, k_mg = mg[ti % 2]
            mgbt, k_mgb = mgb[ti % 2]
            mTt, k_mT = mT[ti % 2]
            xnt, k_xn = xn[ti % 2]
            xot, k_xo = xb_, k_x
            P.dma("sp", "dma_start", out=xb_[:], in_=x_d[r0:r0 + 128, :], writes=[k_x])
            rms(xb_, k_x, ssb, k_ss, nwb, k_nwb, xsb, k_xs)
            for kc in range(8):
                P.op("pe", "transpose", hps_b[:, kc * 128:(kc + 1) * 128], xsb[:, kc * 128:(kc + 1) * 128], idb[:], reads=[k_xs, k_idb], writes=["hps"])
            P.op("act", "activation", out=hTt[:].rearrange("p k t -> p (k t)"), in_=hps_b, func=AF.Copy, reads=["hps"], writes=[k_hT])
            for n in range(3):
                for hf in range(2):
                    c0 = n * D + hf * 512
                    g_, k_g = gt[(n * 2 + hf) % 2]
                    t_, k_t = tmp[(n * 2 + hf) % 2]
                    bk, k_bk = big()
                    P.op("pe", "matmul", bk[:], lhsT=sel[:, n * 2 + hf, :], rhs=bm[:], start=True, stop=False, reads=[k_sel, k_bm], writes=[k_bk])
                    for kc in range(8):
                        P.op("pe", "matmul", bk[:], lhsT=hTt[:, kc, :], rhs=wM[:, kc, c0:c0 + 512], start=False, stop=(kc == 7), reads=[k_hT, k_wM], writes=[k_bk])
                    P.op("act", "activation", out=g_[:], in_=bk[:], func=AF.Sigmoid, reads=[k_bk], writes=[k_g])
                    bk, k_bk = big()
                    for wc in range(8):
                        P.op("pe", "matmul", bk[:], lhsT=yb[:, n, wc, tt * 128:(tt + 1) * 128], rhs=wBr[:, n, wc, hf * 512:(hf + 1) * 512],
                             start=(wc == 0), stop=(wc == 7), reads=[k_yb, k_wBr], writes=[k_bk])
                    msl = mgt[:, hf * 512:(hf + 1) * 512]
                    if n == 0:
                        P.op("dve", "tensor_tensor", out=msl, in0=g_[:], in1=bk[:], op=ALU.mult, reads=[k_g, k_bk], writes=[k_mg])
                    else:
                        P.op("dve", "tensor_tensor", out=t_[:], in0=g_[:], in1=bk[:], op=ALU.mult, reads=[k_g, k_bk], writes=[k_t])
                        P.op("pool", "tensor_tensor", out=msl, in0=msl, in1=t_[:], op=ALU.add, reads=[k_mg, k_t], writes=[k_mg])
            P.op("pool", "tensor_copy", out=mgbt[:], in_=mgt[:], reads=[k_mg], writes=[k_mgb])
            for kc in range(8):
                P.op("pe", "transpose", mps_b[:, kc * 128:(kc + 1) * 128], mgbt[:, kc * 128:(kc + 1) * 128], idb[:], reads=[k_mgb, k_idb], writes=["mps"])
            P.op("act", "activation", out=mTt[:].rearrange("p k t -> p (k t)"), in_=mps_b, func=AF.Copy, reads=["mps"], writes=[k_mT])
            for hf in range(2):
                bk, k_bk = big()
                for kc in range(8):
                    P.op("pe", "matmul", bk[:], lhsT=mTt[:, kc, :], rhs=wO[:, kc, hf * 512:(hf + 1) * 512], start=(kc == 0), stop=(kc == 7),
                         reads=[k_mT, k_wO], writes=[k_bk])
                P.op("dve", "tensor_tensor", out=xnt[:, hf * 512:(hf + 1) * 512], in0=xb_[:, hf * 512:(hf + 1) * 512], in1=bk[:], op=ALU.add,
                     reads=[k_x, k_bk], writes=[k_xn])
            if last:
                rms(xnt, k_xn, ssb, k_ss, fnwb, k_fnwb, xot, k_xo)
                out_toks.append(P.dma("sp", "dma_start", out=out_d[r0:r0 + 128, :], in_=xot[:], reads=[k_xo]))
            else:
                out_toks.append(P.dma("sp", "dma_start", out=out_d[r0:r0 + 128, :], in_=xnt[:], reads=[k_xn]))
    return out_toks


def _prep_B(inp, l, xcur, ys_full, b, j, NT):
    return {
        "x": np.ascontiguousarray(xcur[b, j * NT:(j + 1) * NT]),
        "ys": np.ascontiguousarray(ys_full[b][:, :, j * NT:(j + 1) * NT]),
        "nwb": np.ascontiguousarray(np.broadcast_to(inp["norm_w"][l][None, :], (128, D))),
        "fnwb": np.ascontiguousarray(np.broadcast_to(inp["final_norm_w"][None, :], (128, D))),
        "wM": np.ascontiguousarray(inp["w_in"][l][:, OFF_M:]),
        "bm": np.ascontiguousarray(inp["b_merge"][l].reshape(6, 512)),
        "sel": np.ascontiguousarray(np.broadcast_to(np.eye(6, dtype=np.float32)[:, :, None], (6, 6, 128))),
        "wBr": np.ascontiguousarray(inp["w_branch"][l]),
        "wO": np.ascontiguousarray(inp["w_out"][l]),
        "ident": np.eye(128, dtype=np.float32),
    }


def build_A(S, flags=(1, 1, 1)):
    cx = Cx()
    d = {"x": cx.dram("x", [S, D]), "wA_cols": [(0, cx.dram("wA", [D, NCOLA]))], "gw": cx.dram("gw", [2, 256, 256]),
         "w2p": cx.dram("w2p", [128, 256]), "a2p": cx.dram("a2p", [128, 256]), "pv": cx.dram("pv", [128, NPV]),
         "nwb": cx.dram("nwb", [128, D]), "cst": cx.dram("cst", [128, NCST]), "cosT": cx.dram("cosT", [128, S]),
         "sinT": cx.dram("sinT", [128, S]), "ysT": cx.dram("ysT", [3, 256, S], BF16, "ExternalOutput")}
    toks = emit_A(cx, S, d, flags)
    cx.P.finish(toks)
    return cx.nc


def build_B(NT, last):
    cx = Cx()
    d = {"x": cx.dram("x", [NT, D]), "ys": cx.dram("ys", [3, D, NT], BF16), "nwb": cx.dram("nwb", [128, D]),
         "fnwb": cx.dram("fnwb", [128, D]), "wM": cx.dram("wM", [D, 3 * D]), "bm": cx.dram("bm", [6, 512]),
         "wBr": cx.dram("wBr", [3, D, D]), "wO": cx.dram("wO", [D, D]), "ident": cx.dram("ident", [128, 128]),
         "sel": cx.dram("sel", [6, 6, 128]), "out": cx.dram("out", [NT, D], F32, "ExternalOutput")}
    toks = emit_B(cx, NT, last, d)
    cx.P.finish(toks)
    return cx.nc


_COLS = [(0, 0), (256, OFF_GA), (512, OFF_B), (768, OFF_B + 1024), (1024, OFF_B + 2048), (1280, OFF_B + 3072),
         (1664, OFF_C), (1920, OFF_C + 1024), (2176, OFF_C + 2048), (2432, OFF_C + 3072)]


def build_fused(S, groups=(0, 1, 2, 3)):
    cx = Cx()
    nc, P = cx.nc, cx.P
    x_in = cx.dram("x", [S, D])
    w_in = cx.dram("w_in", [DEPTH, D, N_IN])
    gw = cx.dram("gw", [DEPTH, 4, 2, 256, 256])
    w2p = cx.dram("w2p", [DEPTH, 4, 128, 256])
    a2p = cx.dram("a2p", [DEPTH, 4, 128, 256])
    pv = cx.dram("pv", [DEPTH, 4, 128, NPV])
    nwb = cx.dram("nwb", [DEPTH, 128, D])
    fnwb = cx.dram("fnwb", [128, D])
    cst = cx.dram("cst", [4, 128, NCST])
    cosT = cx.dram("cosT", [128, S])
    sinT = cx.dram("sinT", [128, S])
    bm = cx.dram("bm", [DEPTH, 6, 512])
    sel = cx.dram("sel", [6, 6, 128])
    wBr = cx.dram("wBr", [DEPTH, 3, D, D])
    wO = cx.dram("wO", [DEPTH, D, D])
    ident = cx.dram("ident", [128, 128])
    out = cx.dram("out", [S, D], F32, "ExternalOutput")
    ys_scr = nc.dram_tensor("ys_scr", [3, D, S], BF16, kind="Internal").ap()
    x_scr = nc.dram_tensor("x_scr", [S, D], F32, kind="Internal").ap()
    toks = []
    for l in range(DEPTH):
        x_src = x_in if l == 0 else x_scr
        for g in groups:
            cx.reset()
            cols = [(c0, w_in[l][:, off + g * 256: off + (g + 1) * 256]) for c0, off in _COLS]
            cols.append((1536, w_in[l][:, OFF_B + 4096: OFF_B + 4224]))
            d = {"x": x_src, "wA_cols": cols, "gw": gw[l, g], "w2p": w2p[l, g], "a2p": a2p[l, g], "pv": pv[l, g],
                 "nwb": nwb[l], "cst": cst[g], "cosT": cosT, "sinT": sinT, "ysT": ys_scr[:, g * 256:(g + 1) * 256, :]}
            for t in emit_A(cx, S, d):
                P.wait_tok("sp", t)
            P.barrier()
        cx.reset()
        last = (l == DEPTH - 1)
        d = {"x": x_src, "ys": ys_scr, "nwb": nwb[l], "fnwb": fnwb, "wM": w_in[l][:, OFF_M:], "bm": bm[l], "wBr": wBr[l],
             "wO": wO[l], "ident": ident, "sel": sel, "out": out if last else x_scr}
        toks = emit_B(cx, S, last, d)
        for t in toks:
            P.wait_tok("sp", t)
        P.barrier()
    P.finish(toks)
    return nc


def _prep_fused(inp, b, S):
    pvs = np.zeros((DEPTH, 4, 128, NPV), np.float32)
    csts = np.zeros((4, 128, NCST), np.float32)
    w2 = np.zeros((DEPTH, 4, 128, 256), np.float32)
    a2 = np.zeros((DEPTH, 4, 128, 256), np.float32)
    gws = np.zeros((DEPTH, 4, 2, 256, 256), np.float32)
    cosT = sinT = None
    for l in range(DEPTH):
        for g in range(4):
            m = _prep_A(inp, l, b, g, S, light=True)
            pvs[l, g] = m["pv"]
            w2[l, g] = m["w2p"]
            a2[l, g] = m["a2p"]
            gws[l, g] = m["gw"]
            csts[g] = m["cst"]
            cosT, sinT = m["cosT"], m["sinT"]
    return {
        "x": np.ascontiguousarray(inp["x"][b, :S]), "w_in": np.ascontiguousarray(inp["w_in"]), "gw": gws, "w2p": w2, "a2p": a2,
        "pv": pvs, "nwb": np.ascontiguousarray(np.broadcast_to(inp["norm_w"][:, None, :], (DEPTH, 128, D))),
        "fnwb": np.ascontiguousarray(np.broadcast_to(inp["final_norm_w"][None, :], (128, D))),
        "cst": csts, "cosT": cosT, "sinT": sinT, "bm": np.ascontiguousarray(inp["b_merge"].reshape(DEPTH, 6, 512)),
        "sel": np.ascontiguousarray(np.broadcast_to(np.eye(6, dtype=np.float32)[:, :, None], (6, 6, 128))),
        "wBr": np.ascontiguousarray(inp["w_branch"]), "wO": np.ascontiguousarray(inp["w_out"]),
        "ident": np.eye(128, dtype=np.float32),
    }


def _forward_unfused(inp, S):
    NT = S // 4
    xcur = np.ascontiguousarray(inp["x"][:, :S]).astype(np.float32)
    for l in range(DEPTH):
        ncA = build_A(S)
        inp_l = dict(inp)
        inp_l["x"] = xcur
        in_maps = [_prep_A(inp_l, l, c // 4, c % 4, S) for c in range(8)]
        res = run_bass_kernel_spmd(ncA, in_maps, core_ids=list(range(8)))
        ys_full = [np.concatenate([res.results[b * 4 + g]["ysT"] for g in range(4)], axis=1) for b in range(2)]
        ncB = build_B(NT, last=(l == DEPTH - 1))
        in_maps = [_prep_B(inp, l, xcur, ys_full, c // 4, c % 4, NT) for c in range(8)]
        res = run_bass_kernel_spmd(ncB, in_maps, core_ids=list(range(8)))
        xcur = np.stack([np.concatenate([res.results[b * 4 + j]["out"] for j in range(4)], axis=0) for b in range(2)], axis=0)
    return xcur


def _forward_fused(inp, S):
    nc = build_fused(S)
    maps = [_prep_fused(inp, b, S) for b in range(2)]
    in_maps = [maps[c // 4] for c in range(8)]
    res = run_bass_kernel_spmd(nc, in_maps, core_ids=list(range(8)))
    return np.stack([res.results[0]["out"], res.results[4]["out"]], axis=0)


def kernel(**inputs):
    inp = {k: np.asarray(v) for k, v in inputs.items()}
    return _forward_fused(inp, inp["x"].shape[1]).astype(np.float32)
```

```python
import numpy as np
import ml_dtypes
import concourse.bass as bass
import concourse.mybir as mybir
from concourse.bass_utils import run_bass_kernel_spmd

F32 = mybir.dt.float32
BF16 = mybir.dt.bfloat16
AF = mybir.ActivationFunctionType
ALU = mybir.AluOpType

D = 1024
DEPTH = 2
EPS = 1e-6
OFF_GA = 1024
OFF_B = 2048
B_COLS = 4 * 1024 + 128
OFF_C = OFF_B + B_COLS
OFF_M = OFF_C + 4096
N_IN = OFF_M + 3072
TB = 512
NCOLA = 2688

ENGS = ("pe", "dve", "act", "pool", "sp")
NDMASEM = 8


class Prog:
    def __init__(self, nc, same_engine_sync=True):
        self.nc = nc
        self.ops = {e: [] for e in ENGS}
        self.csem = {e: nc.alloc_semaphore(name=f"c_{e}") for e in ENGS if e != "sp"}
        self.ccnt = {e: 0 for e in ENGS}
        self.dsem = {e: [nc.alloc_semaphore(name=f"d_{e}{i}") for i in range(NDMASEM)]
                     for e in ("sp", "act", "pool")}
        self.dval = {e: [0] * NDMASEM for e in ("sp", "act", "pool")}
        self.dk = {e: 0 for e in ("sp", "act", "pool")}
        self.waited = {}
        self.lastw = {}
        self.readers = {}
        self.ses = same_engine_sync
        self.nops = 0

    def _need(self, e, tok, out):
        semh, val, src = tok
        if src == e and (e == "pe" or not self.ses):
            return
        if self.waited.get((e, id(semh)), 0) >= val:
            return
        cur = out.get(id(semh))
        if cur is None or cur[1] < val:
            out[id(semh)] = (semh, val)

    def _deps(self, e, reads, writes, par=False):
        out = {}
        for k in reads:
            for t in self.lastw.get(k, ()):
                self._need(e, t, out)
        for k in writes:
            if not par:
                for t in self.lastw.get(k, ()):
                    self._need(e, t, out)
            for t in self.readers.get(k, ()):
                self._need(e, t, out)
        for semh, val in out.values():
            self.ops[e].append(("wait", semh, val))
            self.waited[(e, id(semh))] = val

    def _commit(self, tok, reads, writes, par=False):
        for k in reads:
            self.readers.setdefault(k, []).append(tok)
        for k in writes:
            if par:
                self.lastw.setdefault(k, []).append(tok)
            else:
                self.lastw[k] = [tok]
                self.readers[k] = []

    def op(self, e, meth, *args, reads=(), writes=(), **kw):
        fn = (lambda g: getattr(g, meth)(*args, **kw)) if isinstance(meth, str) else meth
        xr = [k for k in reads if k.startswith("bank") or k == "hps"]
        if xr:
            writes = list(writes) + xr
            reads = [k for k in reads if k not in xr]
        self._deps(e, reads, writes)
        self.ccnt[e] += 1
        tok = (self.csem[e], self.ccnt[e], e)
        self.ops[e].append(("op", fn, self.csem[e], 1))
        self._commit(tok, reads, writes)
        self.nops += 1
        return tok

    def dma(self, e, meth, *args, reads=(), writes=(), par=False, **kw):
        fn = (lambda g: getattr(g, meth)(*args, **kw)) if isinstance(meth, str) else meth
        i = self.dk[e] % NDMASEM
        self.dk[e] += 1
        semh = self.dsem[e][i]
        if self.dval[e][i] > 0:
            k = (e, id(semh))
            if self.waited.get(k, 0) < self.dval[e][i]:
                self.ops[e].append(("wait", semh, self.dval[e][i]))
                self.waited[k] = self.dval[e][i]
        self._deps(e, reads, writes, par)
        self.dval[e][i] += 16
        tok = (semh, self.dval[e][i], "dma_" + e)
        self.ops[e].append(("op", fn, semh, 16))
        self._commit(tok, reads, writes, par)
        self.nops += 1
        return tok

    def wait_tok(self, e, tok):
        semh, val, _ = tok
        k = (e, id(semh))
        if self.waited.get(k, 0) < val:
            self.ops[e].append(("wait", semh, val))
            self.waited[k] = val

    def barrier(self):
        for e in ENGS:
            for e2 in self.csem:
                if e2 != e and self.ccnt[e2] > 0:
                    self.wait_tok(e, (self.csem[e2], self.ccnt[e2], e2))
            for q in self.dsem:
                for i in range(NDMASEM):
                    if self.dval[q][i] > 0:
                        self.wait_tok(e, (self.dsem[q][i], self.dval[q][i], "dma_" + q))

    def finish(self, final_toks):
        for t in final_toks:
            self.wait_tok("sp", t)
        nc = self.nc
        ops = self.ops

        def replay(engine, lst):
            for it in lst:
                if it[0] == "wait":
                    engine.wait_ge(it[1], it[2])
                else:
                    it[1](engine).then_inc(it[2], it[3])

        with nc.Block() as block:
            @block.tensor
            def _(e):
                replay(e, ops["pe"])

            @block.vector
            def _(e):
                replay(e, ops["dve"])

            @block.scalar
            def _(e):
                replay(e, ops["act"])

            @block.gpsimd
            def _(e):
                replay(e, ops["pool"])

            @block.sync
            def _(e):
                replay(e, ops["sp"])


class Cx:
    def __init__(self):
        self.nc = bass.Bass("TRN2", target_bir_lowering=False)
        self.P = Prog(self.nc)
        self.banks = [self.nc.alloc_psum_tensor(f"bank{i}", [128, 512], F32) for i in range(8)]
        self.lo = (int(self.nc.sbuf_base) + 63) // 64 * 64
        self.hi = int(self.nc.sbuf_top)
        self.ptr = self.lo
        self.n = 0

    def reset(self):
        self.ptr = self.lo

    def sb(self, shape, dt=F32, name=None):
        self.n += 1
        nm = f"s{self.n}_" + (name or "t")
        esz = 2 if dt == BF16 else 4
        nbytes = esz
        for v in shape[1:]:
            nbytes *= v
        nbytes = (nbytes + 63) // 64 * 64
        assert self.ptr + nbytes <= self.hi, f"SBUF arena overflow allocating {nm} {shape}: {self.ptr + nbytes - self.hi} over"
        t = self.nc.alloc_sbuf_tensor_at(nm, list(shape), dt, offset=self.ptr)
        self.ptr += nbytes
        return t, nm

    def dram(self, n, s, d=F32, k="ExternalInput"):
        return self.nc.dram_tensor(n, list(s), d, kind=k).ap()

PV = {}
_c = 0
for _nm, _n in [("cw0", 2), ("cw1", 2), ("cw2", 2), ("cw3", 2), ("cb", 2), ("gbr", 2), ("gbi", 2),
                ("lam", 2), ("mu_r", 2), ("mu_k", 2), ("mu_v", 2), ("mu_g", 2), ("mu_wl", 1),
                ("w0", 2), ("a0", 2), ("k_k", 2), ("k_a", 2), ("r_k", 2), ("lnw", 2), ("lnb", 2),
                ("kdec", 1), ("cdec", 1)]:
    PV[_nm] = _c
    _c += _n
NPV = _c
DV = {"omu_r": 0, "omu_k": 2, "omu_v": 4, "omu_g": 6, "omu_wl": 8, "lrs": 9}
NDV = 11

CST = {"ident": 0, "b64": 128, "b64m": 256, "m1": 384, "m2": 640, "rmask": 768, "qdec": 896, "ones": 1408}
NCST = 1408 + 64


def emit_A(cx, S, d, flags=(1, 1, 1)):
    DO_A, DO_B, DO_C = flags
    rstop = 99
    TB_ = 256
    NTT = TB_ // 128
    NCH = TB_ // 64
    NTB = S // TB_
    nc = cx.nc
    P = cx.P
    sb = cx.sb
    x_d, gw_d, w2p_d, a2p_d, pv_d, nwb_d, cst_d, cos_d, sin_d, ys_d = (d[k] for k in
        ("x", "gw", "w2p", "a2p", "pv", "nwb", "cst", "cosT", "sinT", "ysT"))
    out_toks = []

    def dump(*a, **k):
        return

    banks = cx.banks
    cur_stream = [0]
    bank_sets = {0: [0, 1], 1: [3, 4], 2: [5, 6]}
    bank_ctr = {0: 0, 1: 0, 2: 0}

    def _nb():
        st = cur_stream[0]
        bs = bank_sets[st]
        i = bs[bank_ctr[st] % len(bs)]
        bank_ctr[st] += 1
        return i

    def big():
        i = _nb()
        return banks[i], f"bank{i}"

    def quarter():
        i = _nb()
        return banks[i][:, 0:128], f"bank{i}"

    def half():
        i = _nb()
        return banks[i][:, 0:256], [f"bank{i}"]
    hps = banks[7]

    wA, k_wA = sb([128, 8, NCOLA], BF16, "wA")
    gw, k_gw = sb([128, 2, 2, 256], BF16, "gw")
    w2p, k_w2p = sb([128, 256], BF16, "w2p")
    a2p, k_a2p = sb([128, 256], BF16, "a2p")
    pv, k_pv = sb([128, NPV], F32, "pv")
    dv, k_dv = sb([128, NDV], F32, "dv")
    nwb, k_nwb = sb([128, D], F32, "nwb")
    cst, k_cst = sb([128, NCST], F32, "cst")
    cstb, k_cstb = sb([128, 640 + 128], BF16, "cstb")

    for c0, piece in d["wA_cols"]:
        npc = piece.shape[1]
        pv_ = piece.rearrange("(kc p) n -> p kc n", p=128)
        for kc in range(0, 8, 4):
            P.dma("pool", "dma_start", out=wA[:, kc:kc + 4, c0:c0 + npc], in_=pv_[:, kc:kc + 4, :], writes=[k_wA], par=True)
    P.dma("pool", "dma_start", out=gw[:], in_=gw_d.rearrange("n (cc p) d -> p n cc d", p=128), writes=[k_gw])
    P.dma("pool", "dma_start", out=w2p[:], in_=w2p_d, writes=[k_w2p])
    P.dma("pool", "dma_start", out=a2p[:], in_=a2p_d, writes=[k_a2p])
    P.dma("sp", "dma_start", out=pv[:], in_=pv_d, writes=[k_pv])
    P.dma("sp", "dma_start", out=nwb[:], in_=nwb_d, writes=[k_nwb])
    P.dma("sp", "dma_start", out=cst[:], in_=cst_d, writes=[k_cst])
    P.op("dve", "tensor_copy", out=cstb[:, 0:128], in_=cst[:, 0:128], reads=[k_cst], writes=[k_cstb])
    identb = cstb[:, 0:128]
    ident_f = cst[:, CST["ident"]:CST["ident"] + 128]
    b64 = cst[:, CST["b64"]:CST["b64"] + 128]
    b64m = cst[:, CST["b64m"]:CST["b64m"] + 128]
    m1 = cst[:, CST["m1"]:CST["m1"] + 256]
    m2 = cst[:, CST["m2"]:CST["m2"] + 128]
    rmask = cst[:, CST["rmask"]:CST["rmask"] + 128]
    qdec = cst[:, CST["qdec"]:CST["qdec"] + 512]
    ones64 = cst[:, CST["ones"]:CST["ones"] + 64]

    def pvc(nm, j=0):
        c = PV[nm] + j
        return pv[:, c:c + 1]

    def dvc(nm, j=0):
        c = DV[nm] + j
        return dv[:, c:c + 1]

    P.op("dve", "tensor_scalar", out=dv[:, 0:9], in0=pv[:, PV["mu_r"]:PV["mu_r"] + 9], scalar1=-1.0, scalar2=1.0,
                                          op0=ALU.mult, op1=ALU.add, reads=[k_pv], writes=[k_dv])
    tl, k_tl = sb([128, 2], F32, "tl")
    P.op("act", "activation", out=tl[:], in_=pv[:, PV["lam"]:PV["lam"] + 2], func=AF.Exp, scale=-1.0, reads=[k_pv], writes=[k_tl])
    P.op("act", "activation", out=tl[:], in_=tl[:], func=AF.Ln, bias=1.0, reads=[k_tl], writes=[k_tl])
    P.op("dve", "tensor_scalar", out=dv[:, 9:11], in0=tl[:], scalar1=-8.0, scalar2=None, op0=ALU.mult, reads=[k_tl, k_dv], writes=[k_dv])

    xt = [sb([128, D], F32, f"xt{i}") for i in range(2)]
    junk, k_junk = sb([128, D], BF16, "junk")
    ss = [sb([128, 4], F32, f"ss{i}") for i in range(2)]
    xs = [sb([128, D], BF16, f"xs{i}") for i in range(1)] * 2
    hT = [sb([128, 8, TB_], BF16, f"hT{i}") for i in range(2)]
    hps_b = hps[:].bitcast(BF16)
    pool = [sb([128, TB_], F32, f"pl{i}") for i in range(20)]
    poolB = [sb([128, TB_], F32, f"plB{i}") for i in range(19)]
    poolC = [sb([128, TB_], F32, f"plC{i}") for i in range(11)]

    xa = [sb([128, TB_ + 3], F32, f"xa{i}") for i in range(2)]
    cv = [poolC[0], poolC[1]]
    lr = [poolC[2], poolC[3]]
    li = [poolC[4], poolC[5]]
    la, k_la = poolC[6]
    lt, k_lt = poolC[7]
    lh, k_lh = poolC[8]
    sga = [poolC[9], poolC[10]]
    cvb = [sb([128, TB_], BF16, f"cvb{i}") for i in range(2)]
    hcar = [sb([128, 1], F32, f"hcar{i}") for i in range(2)]
    ysA = [sb([128, TB_], BF16, f"ysA{i}") for i in range(2)]
    for i in range(2):
        P.op("pool", "memset", xa[i][0][:, 0:3], 0.0, writes=[xa[i][1]])
        P.op("pool", "memset", hcar[i][0][:], 0.0, writes=[hcar[i][1]])

    pB = {nm: [sb([128, TB_ + 1], F32, f"pB{nm}{i}") for i in range(2)] for nm in "rkvg"}
    pWL, k_pWL = sb([128, TB_ + 1], F32, "pWL")
    P.op("pool", "memset", pWL[:, 0:1], 0.0, writes=[k_pWL])
    for nm in "rkvg":
        for i in range(2):
            P.op("pool", "memset", pB[nm][i][0][:, 0:1], 0.0, writes=[pB[nm][i][1]])
    sWL, k_sWL = pool[19]
    tnh, k_tnh = sb([128, TB_], BF16, "tnh")
    sWLb, k_sWLb = sb([128, TB_], BF16, "sWLb")
    sS_pt = [{nm: pl[i] for i, nm in enumerate("rkvg")} for pl in (pool, poolB)]
    f_pt = [{nm: pl[4 + i] for i, nm in enumerate(
        ["logw", "cum", "Wt", "iW", "Wp", "kk", "sq", "rn", "kmod", "kka", "iWC", "bon", "sgg", "yT", "aa"])} for pl in (pool, poolB)]
    BD_AR = [sb([128, NCH, 2, 128], BF16, f"BD_AR{i}") for i in range(2)]
    BDn = {nm: [sb([128, NCH, 128], BF16, f"BD_{nm}{i}") for i in range(2)] for nm in ["B", "K", "Bh", "Kh", "V"]}
    for i in range(2):
        P.op("pool", "memset", BD_AR[i][0][:], 0.0, writes=[BD_AR[i][1]])
        for nm in BDn:
            P.op("pool", "memset", BDn[nm][i][0][:], 0.0, writes=[BDn[nm][i][1]])
    S0f = [sb([128, 128], F32, f"S0f{i}") for i in range(2)]
    S0b = [sb([128, 128], BF16, f"S0b{i}") for i in range(2)]
    for i in range(2):
        P.op("pool", "memset", S0f[i][0][:], 0.0, writes=[S0f[i][1]])
        P.op("pool", "memset", S0b[i][0][:], 0.0, writes=[S0b[i][1]])
    NM1a_l = [sb([128, NCH, 256], BF16, f"NM1a{i}") for i in range(2)]
    NM2a_l = [sb([128, NCH, 256], BF16, f"NM2a{i}") for i in range(2)]
    LNP_l = [[sb([128, NCH, 128], BF16, f"LNP{j}_{i}") for i in range(6)] for j in range(2)]
    TKa_l = [sb([128, NCH, 3, 128], BF16, f"TKa{i}") for i in range(2)]
    wc8_l = [sb([128, NCH], F32, f"wc8_{i}") for i in range(2)]
    Zb_l = [[sb([128, 128], BF16, f"Zb{j}_{i}") for i in range(2)] for j in range(2)]
    Ub_l = [[sb([128, 128], BF16, f"Ub{j}_{i}") for i in range(2)] for j in range(2)]
    ysB = [sb([128, TB_], BF16, f"ysB{i}") for i in range(2)]

    qf = [poolC[0], poolC[1]]
    kf = [poolC[2], poolC[3]]
    cosb, k_cos = poolC[4]
    sinb, k_sin = poolC[5]
    rt = [poolC[6 + i] for i in range(4)]
    qT = [sb([128, TB_], BF16, f"qT{i}") for i in range(2)]
    kT = [sb([128, TB_], BF16, f"kT{i}") for i in range(2)]
    qd = [sb([128, TB_], BF16, f"qd{i}") for i in range(2)]
    v_tok = [sb([128, 256], BF16, f"vtok{i}") for i in range(4)]
    sg_tok = [sb([128, 256], F32, f"sgtok{i}") for i in range(4)]
    STm = [sb([128, 128], BF16, f"ST{i}") for i in range(2)]
    kdt = [sb([128, 256], BF16, f"kdt{i}") for i in range(2)]
    stf = [sb([128, 256], F32, f"stf{i}") for i in range(2)]
    stb = [sb([128, 256], BF16, f"stb{i}") for i in range(2)]
    for i in range(2):
        P.op("pool", "memset", stf[i][0][:], 0.0, writes=[stf[i][1]])
        P.op("pool", "memset", stb[i][0][:], 0.0, writes=[stb[i][1]])
    st6 = [sb([128, 6], F32, f"st6_{i}") for i in range(2)]
    mv = [sb([128, 4], F32, f"mv{i}") for i in range(2)]
    yn = [sb([128, 256], F32, f"yn{i}") for i in range(1)] * 2
    ycb = [sb([128, 256], BF16, f"ycb{i}") for i in range(2)]
    ysC = [sb([128, TB_], BF16, f"ysC{i}") for i in range(2)]

    cp_i = [0]

    def copy_eng():
        cp_i[0] += 1
        return "act" if cp_i[0] % 2 else "dve"

    def evac(e, out, in_, reads, writes, scale=None):
        if e == "act":
            if scale is None:
                P.op("act", "activation", out=out, in_=in_, func=AF.Copy, reads=reads, writes=writes)
            else:
                P.op("act", "activation", out=out, in_=in_, func=AF.Copy, scale=scale, reads=reads, writes=writes)
        else:
            if scale is None:
                P.op(e, "tensor_copy", out=out, in_=in_, reads=reads, writes=writes)
            else:
                P.op(e, "tensor_scalar", out=out, in0=in_, scalar1=scale, scalar2=None, op0=ALU.mult, reads=reads, writes=writes)

    def stage1_gen(tbn):
        hTn, k_hTn = hT[tbn % 2]
        for tt in range(NTT):
            xb_, k_x = xt[tt % 2]
            ssb, k_ss = ss[tt % 2]
            xsb, k_xs = xs[tt % 2]
            r0 = tbn * TB_ + tt * 128
            P.dma("sp", "dma_start", out=xb_[:], in_=x_d[r0:r0 + 128, :], writes=[k_x])
            P.op("act", "activation", out=junk[:], in_=xb_[:], func=AF.Square, accum_out=ssb[:, 0:1],
                 reads=[k_x], writes=[k_junk, k_ss])
            yield
            P.op("dve", "tensor_scalar", out=ssb[:, 1:2], in0=ssb[:, 0:1], scalar1=1.0 / D, scalar2=EPS, op0=ALU.mult, op1=ALU.add,
                 reads=[k_ss], writes=[k_ss])
            yield
            P.op("act", "activation", out=ssb[:, 2:3], in_=ssb[:, 1:2], func=AF.Sqrt, reads=[k_ss], writes=[k_ss])
            yield
            P.op("dve", "reciprocal", out=ssb[:, 3:4], in_=ssb[:, 2:3], reads=[k_ss], writes=[k_ss])
            yield
            P.op("dve", "scalar_tensor_tensor", out=xsb[:], in0=xb_[:], scalar=ssb[:, 3:4], in1=nwb[:], op0=ALU.mult, op1=ALU.mult,
                 reads=[k_x, k_ss, k_nwb], writes=[k_xs])
            yield
            for kc in range(8):
                P.op("pe", "transpose", hps_b[:, kc * 128:(kc + 1) * 128], xsb[:, kc * 128:(kc + 1) * 128], identb,
                     reads=[k_xs, k_cstb], writes=["hps"])
            P.op("act", "activation", out=hTn[:, :, tt * 128:(tt + 1) * 128], in_=hps_b.rearrange("p (k t) -> p k t", t=128), func=AF.Copy,
                 reads=["hps"], writes=[k_hTn])
            yield

    for _ in stage1_gen(0):
        pass
    for tb in range(NTB):
        t0 = tb * TB_
        hTt, k_hT = hT[tb % 2]
        def proj_fm(cb):
            bk, k_bk = big()
            for kc in range(8):
                P.op("pe", "matmul", bk[:, 0:TB_], lhsT=wA[:, kc, cb * 128:(cb + 1) * 128], rhs=hTt[:, kc, :],
                                                           start=(kc == 0), stop=(kc == 7),
                     reads=[k_wA, k_hT], writes=[k_bk])
            return bk, k_bk

        if tb == 0:
            dump("hT", hTt[:, 0, :], k_hT)
            dump("xs", xs[1][0][:, 0:TB_], xs[1][1])
            dump("ss", ss[1][0][:, 0:4], ss[1][1], 4)
        def lru_gen():
            for pt in range(2):
                bk, k_bk = proj_fm(pt)
                evac("act", xa[pt][0][:, 3:TB_ + 3], bk[:, 0:TB_], [k_bk], [xa[pt][1]])
                yield
                if tb == 0 and pt == 0:
                    dump("xa", xa[0][0][:, 3:TB_ + 3], xa[0][1])
                bk, k_bk = proj_fm(2 + pt)
                P.op("act", "activation", out=sga[pt][0][:], in_=bk[:, 0:TB_], func=AF.Silu, reads=[k_bk], writes=[sga[pt][1]])
                yield
            for pt in range(2):
                xat, k_xa = xa[pt]
                cvt, k_cv = cv[pt]
                P.op("dve", "tensor_scalar", out=cvt[:], in0=xat[:, 3:TB_ + 3], scalar1=pvc("cw3", pt), scalar2=pvc("cb", pt),
                                                                             op0=ALU.mult, op1=ALU.add, reads=[k_xa, k_pv], writes=[k_cv])
                yield
                for j in range(3):
                    P.op("dve", "scalar_tensor_tensor", out=cvt[:], in0=xat[:, j:j + TB_], scalar=pvc(f"cw{j}", pt), in1=cvt[:],
                                                                                             op0=ALU.mult, op1=ALU.add, reads=[k_xa, k_cv, k_pv], writes=[k_cv])
                    yield
                P.op("pool", "tensor_copy", out=xat[:, 0:3], in_=xat[:, TB_:TB_ + 3], reads=[k_xa], writes=[k_xa])
                yield
                P.op("pool", "tensor_copy", out=cvb[pt][0][:], in_=cvt[:], reads=[k_cv], writes=[cvb[pt][1]])
                yield
            for n in range(2):
                for dblk in range(2):
                    bk, k_bk = big()
                    for cc in range(2):
                        P.op("pe", "matmul", bk[:, 0:TB_], lhsT=gw[:, n, cc, dblk * 128:(dblk + 1) * 128], rhs=cvb[cc][0][:],
                                                                                   start=(cc == 0), stop=(cc == 1),
                             reads=[k_gw, cvb[cc][1]], writes=[k_bk])
                    dst = (lr if n == 0 else li)[dblk]
                    P.op("act", "activation", out=dst[0][:], in_=bk[:, 0:TB_], func=AF.Sigmoid,
                                                                                      bias=pvc("gbr" if n == 0 else "gbi", dblk),
                         reads=[k_bk, k_pv], writes=[dst[1]])
                    yield
            for pt in range(2):
                cvt, k_cv = cv[pt]
                P.op("act", "activation", out=la[:], in_=lr[pt][0][:], func=AF.Exp, scale=dvc("lrs", pt), reads=[lr[pt][1], k_dv], writes=[k_la])
                yield
                P.op("dve", "scalar_tensor_tensor", out=lt[:], in0=la[:], scalar=-1.0, in1=la[:], op0=ALU.mult, op1=ALU.mult, reads=[k_la], writes=[k_lt])
                yield
                P.op("dve", "tensor_scalar", out=lt[:], in0=lt[:], scalar1=1.0, scalar2=0.0, op0=ALU.add, op1=ALU.max, reads=[k_lt], writes=[k_lt])
                yield
                P.op("act", "activation", out=lt[:], in_=lt[:], func=AF.Sqrt, reads=[k_lt], writes=[k_lt])
                yield
                if tb == 0:
                    P.op("dve", "memset", lt[:, 0:1], 1.0, reads=[k_lt], writes=[k_lt])
                    yield
                P.op("dve", "tensor_tensor", out=lt[:], in0=lt[:], in1=li[pt][0][:], op=ALU.mult, reads=[k_lt, li[pt][1]], writes=[k_lt])
                yield
                P.op("dve", "tensor_tensor", out=lt[:], in0=lt[:], in1=cvt[:], op=ALU.mult, reads=[k_lt, k_cv], writes=[k_lt])
                yield
                P.op("dve", "tensor_tensor_scan", out=lh[:], data0=la[:], data1=lt[:], initial=hcar[pt][0][:, 0:1], op0=ALU.mult, op1=ALU.add,
                     reads=[k_la, k_lt, hcar[pt][1]], writes=[k_lh])
                yield
                P.op("dve", "tensor_copy", out=hcar[pt][0][:], in_=lh[:, TB_ - 1:TB_], reads=[k_lh], writes=[hcar[pt][1]])
                yield
                if tb == 0 and pt == 0:
                    dump("cv", cv[0][0][:], cv[0][1]); dump("lr", lr[0][0][:], lr[0][1]); dump("li", li[0][0][:], li[0][1])
                    dump("la", la[:], k_la); dump("lt", lt[:], k_lt); dump("lh", lh[:], k_lh); dump("sga", sga[0][0][:], sga[0][1])
                P.op("dve", "tensor_tensor", out=ysA[pt][0][:], in0=lh[:], in1=sga[pt][0][:], op=ALU.mult, reads=[k_lh, sga[pt][1]], writes=[ysA[pt][1]])
                yield
                out_toks.append(P.dma("sp", "dma_start", out=ys_d[0, pt * 128:(pt + 1) * 128, t0:t0 + TB_], in_=ysA[pt][0][:],
                                      reads=[ysA[pt][1]]))

            return
            yield

        ret_done = [True]

        def ret_gen():
            if DO_C:
                for pt in range(2):
                    bk, k_bk = proj_fm(13 + pt)
                    evac("act", qf[pt][0][:], bk[:, 0:TB_], [k_bk], [qf[pt][1]], scale=1.0 / 16.0)
                    yield
                    bk, k_bk = proj_fm(15 + pt)
                    evac("dve", kf[pt][0][:], bk[:, 0:TB_], [k_bk], [kf[pt][1]])
                    yield
                for tt in range(NTT):
                    bk, k_bk = big()
                    for kc in range(8):
                        P.op("pe", "matmul", bk[:, 0:512], lhsT=hTt[:, kc, tt * 128:(tt + 1) * 128], rhs=wA[:, kc, 2176:2688],
                                                                          start=(kc == 0), stop=(kc == 7),
                             reads=[k_wA, k_hT], writes=[k_bk])
                    evac("dve", v_tok[tt][0][:], bk[:, 0:256], [k_bk], [v_tok[tt][1]])
                    yield
                    P.op("act", "activation", out=sg_tok[tt][0][:], in_=bk[:, 256:512], func=AF.Silu,
                         reads=[k_bk], writes=[sg_tok[tt][1]])
                    yield

            if DO_C and rstop >= 2:
                P.dma("sp", "dma_start", out=cosb[:], in_=cos_d[:, t0:t0 + TB_], writes=[k_cos])
                P.dma("sp", "dma_start", out=sinb[:], in_=sin_d[:, t0:t0 + TB_], writes=[k_sin])
                for src, dst in ((qf, qT), (kf, kT)):
                    s1, k1 = src[0]
                    s2, k2 = src[1]
                    P.op("dve", "tensor_tensor", out=rt[0][0][:], in0=s1[:], in1=cosb[:], op=ALU.mult, reads=[k1, k_cos], writes=[rt[0][1]])
                    yield
                    P.op("pool", "tensor_tensor", out=rt[1][0][:], in0=s2[:], in1=sinb[:], op=ALU.mult, reads=[k2, k_sin], writes=[rt[1][1]])
                    yield
                    P.op("dve", "tensor_tensor", out=dst[0][0][:], in0=rt[0][0][:], in1=rt[1][0][:], op=ALU.subtract, reads=[rt[0][1], rt[1][1]], writes=[dst[0][1]])
                    yield
                    P.op("pool", "tensor_tensor", out=rt[2][0][:], in0=s1[:], in1=sinb[:], op=ALU.mult, reads=[k1, k_sin], writes=[rt[2][1]])
                    yield
                    P.op("dve", "tensor_tensor", out=rt[3][0][:], in0=s2[:], in1=cosb[:], op=ALU.mult, reads=[k2, k_cos], writes=[rt[3][1]])
                    yield
                    P.op("dve", "tensor_tensor", out=dst[1][0][:], in0=rt[2][0][:], in1=rt[3][0][:], op=ALU.add, reads=[rt[2][1], rt[3][1]], writes=[dst[1][1]])
                    yield
                for pt in range(2):
                    P.op("pool", "tensor_tensor", out=qd[pt][0][:], in0=qT[pt][0][:], in1=qdec[:, 0:TB_], op=ALU.mult, reads=[qT[pt][1], k_cst], writes=[qd[pt][1]])
                    yield
                for c in range(NTT):
                    cs = slice(c * 128, (c + 1) * 128)
                    stm, k_stm = STm[c % 2]
                    kd, k_kd = kdt[c % 2]
                    ps, k_ps = quarter()
                    for pt in range(2):
                        P.op("pe", "matmul", ps, lhsT=kT[pt][0][:, cs], rhs=qT[pt][0][:, cs], start=(pt == 0), stop=(pt == 1),
                             reads=[kT[pt][1], qT[pt][1]], writes=[k_ps])
                    P.op("dve", "tensor_tensor", out=stm[:], in0=ps, in1=rmask, op=ALU.mult, reads=[k_ps, k_cst], writes=[k_stm])
                    yield
                    if rstop < 5:
                        continue
                    psq, k_psq = quarter()
                    psq_b = psq.bitcast(BF16)
                    for pt in range(2):
                        P.op("pe", "transpose", psq_b[:, pt * 128:(pt + 1) * 128], kT[pt][0][:, cs], identb,
                             reads=[kT[pt][1], k_cstb], writes=[k_psq])
                    P.op("dve", "tensor_scalar", out=kd[:], in0=psq_b, scalar1=pvc("kdec"), scalar2=None, op0=ALU.mult, reads=[k_psq, k_pv], writes=[k_kd])
                    yield
                    if rstop < 6:
                        continue
                    pso, k_pso = banks[2][:, 0:256], ["bank2"]
                    P.op("pe", "matmul", pso, lhsT=stm[:], rhs=v_tok[c][0][:], start=True, stop=False, reads=[k_stm, v_tok[c][1]], writes=k_pso)
                    for pt in range(2):
                        P.op("pe", "matmul", pso, lhsT=qd[pt][0][:, cs], rhs=stb[pt][0][:], start=False, stop=(pt == 1),
                             reads=[qd[pt][1], stb[pt][1]], writes=k_pso)
                    for dc in range(2 if rstop >= 7 else 0):
                        pss, k_pss = half()
                        P.op("pe", "matmul", pss, lhsT=kd[:, dc * 128:(dc + 1) * 128], rhs=v_tok[c][0][:], start=True, stop=True,
                             reads=[k_kd, v_tok[c][1]], writes=k_pss)
                        P.op("dve", "scalar_tensor_tensor", out=stf[dc][0][:], in0=stf[dc][0][:], scalar=pvc("cdec"), in1=pss, op0=ALU.mult, op1=ALU.add,
                             reads=[stf[dc][1], k_pv] + k_pss, writes=[stf[dc][1]])
                        yield
                        P.op("act", "activation", out=stb[dc][0][:], in_=stf[dc][0][:], func=AF.Copy, reads=[stf[dc][1]], writes=[stb[dc][1]])
                        yield
                    if rstop < 8:
                        continue
                    s6, k_s6 = st6[c % 2]
                    mvt, k_mv = mv[c % 2]
                    ynt, k_yn = yn[c % 2]
                    ycbt, k_ycb = ycb[c % 2]
                    P.op("dve", "tensor_reduce", out=mvt[:, 0:1], in_=pso, axis=mybir.AxisListType.X, op=ALU.add, reads=k_pso, writes=[k_mv])
                    yield
                    P.op("dve", "tensor_scalar", out=mvt[:, 1:2], in0=mvt[:, 0:1], scalar1=-1.0 / 256.0, scalar2=None, op0=ALU.mult, reads=[k_mv], writes=[k_mv])
                    yield
                    P.op("act", "activation", out=ynt[:], in_=pso, func=AF.Identity, bias=mvt[:, 1:2], reads=k_pso + [k_mv], writes=[k_yn])
                    yield
                    P.op("act", "activation", out=junk[:, 0:256], in_=ynt[:], func=AF.Square, accum_out=s6[:, 0:1], reads=[k_yn], writes=[k_junk, k_s6])
                    yield
                    P.op("dve", "tensor_scalar", out=s6[:, 1:2], in0=s6[:, 0:1], scalar1=1.0 / 256.0, scalar2=1e-5, op0=ALU.mult, op1=ALU.add, reads=[k_s6], writes=[k_s6])
                    yield
                    P.op("act", "activation", out=s6[:, 2:3], in_=s6[:, 1:2], func=AF.Sqrt, reads=[k_s6], writes=[k_s6])
                    yield
                    P.op("dve", "reciprocal", out=s6[:, 3:4], in_=s6[:, 2:3], reads=[k_s6], writes=[k_s6])
                    yield
                    P.op("dve", "scalar_tensor_tensor", out=ycbt[:], in0=ynt[:], scalar=s6[:, 3:4], in1=sg_tok[c][0][:], op0=ALU.mult, op1=ALU.mult,
                         reads=[k_yn, k_s6, sg_tok[c][1]], writes=[k_ycb])
                    yield
                    if rstop < 9:
                        continue
                    pst, k_pst = quarter()
                    pst_b = pst.bitcast(BF16)
                    for pt in range(2):
                        P.op("pe", "transpose", pst_b[:, pt * 128:(pt + 1) * 128], ycbt[:, pt * 128:(pt + 1) * 128], identb,
                             reads=[k_ycb, k_cstb], writes=[k_pst])
                    for pt in range(2):
                        evac(copy_eng(), ysC[pt][0][:, cs], pst_b[:, pt * 128:(pt + 1) * 128], [k_pst], [ysC[pt][1]])
                        yield
                for pt in range(2 if rstop >= 10 else 0):
                    out_toks.append(P.dma("sp", "dma_start", out=ys_d[2, pt * 128:(pt + 1) * 128, t0:t0 + TB_], in_=ysC[pt][0][:],
                                          reads=[ysC[pt][1]]))
            return
            yield

        if DO_B:
            bk, k_bk = proj_fm(12)
            evac(copy_eng(), pWL[:, 1:TB_ + 1], bk[:, 0:TB_], [k_bk], [k_pWL])
            def shift(buf, k_buf, dst, k_dst, mu, omu):
                P.op("act", "activation", out=dst[:], in_=buf[:, 0:TB_], func=AF.Copy, scale=mu,
                     reads=[k_buf, k_pv], writes=[k_dst])
                P.op("dve", "scalar_tensor_tensor", out=dst[:], in0=buf[:, 1:TB_ + 1], scalar=omu, in1=dst[:], op0=ALU.mult, op1=ALU.add,
                     reads=[k_buf, k_dst, k_dv], writes=[k_dst])
                P.op("pool", "tensor_copy", out=buf[:, 0:1], in_=buf[:, TB_:TB_ + 1], reads=[k_buf], writes=[k_buf])

            shift(pWL, k_pWL, sWL, k_sWL, pvc("mu_wl"), dvc("omu_wl"))
            P.op("act", "activation", out=tnh[:], in_=sWL[:], func=AF.Tanh, reads=[k_sWL], writes=[k_tnh])
            P.op("pool", "tensor_copy", out=sWLb[:], in_=sWL[:], reads=[k_sWL], writes=[k_sWLb])

        def rwkv_gen(pt):
            sS = sS_pt[pt]
            f = f_pt[pt]
            NM1a = NM1a_l[pt]; NM2a = NM2a_l[pt]; LNP = LNP_l[pt]; TKa = TKa_l[pt]; Zb = Zb_l[pt]; Ub = Ub_l[pt]
            wc8, k_wc8 = wc8_l[pt]
            for j, nm in enumerate("rkvg"):
                bk, k_bk = proj_fm(4 + 2 * j + pt)
                evac(copy_eng(), pB[nm][pt][0][:, 1:TB_ + 1], bk[:, 0:TB_], [k_bk], [pB[nm][pt][1]])
                yield
            F_ = lambda nm: f[nm][0]
            K_ = lambda nm: f[nm][1]
            for nm in "rkvg":
                shift(pB[nm][pt][0], pB[nm][pt][1], sS[nm][0], sS[nm][1], pvc("mu_" + nm, pt), dvc("omu_" + nm, pt))
            s_r, s_k, s_v, s_g = (sS[nm][0] for nm in "rkvg")
            ksr, ksk, ksv, ksg = (sS[nm][1] for nm in "rkvg")
            bk, k_bk = big()
            P.op("pe", "matmul", bk[:, 0:TB_], lhsT=w2p[:, pt * 128:(pt + 1) * 128], rhs=tnh[:], start=True, stop=True,
                 reads=[k_w2p, k_tnh], writes=[k_bk])
            P.op("act", "activation", out=F_("logw")[:], in_=bk[:, 0:TB_], func=AF.Sigmoid, bias=pvc("w0", pt), reads=[k_bk, k_pv], writes=[K_("logw")])
            yield
            bk, k_bk = big()
            P.op("pe", "matmul", bk[:, 0:TB_], lhsT=a2p[:, pt * 128:(pt + 1) * 128], rhs=sWLb[:], start=True, stop=True,
                 reads=[k_a2p, k_sWLb], writes=[k_bk])
            P.op("act", "activation", out=F_("aa")[:], in_=bk[:, 0:TB_], func=AF.Sigmoid, bias=pvc("a0", pt), reads=[k_bk, k_pv], writes=[K_("aa")])
            yield
            P.op("act", "activation", out=F_("logw")[:], in_=F_("logw")[:], func=AF.Copy, scale=-0.6065306597126334,
                 reads=[K_("logw")], writes=[K_("logw")])
            yield
            for ch in range(NCH):
                P.op("dve", "tensor_tensor_scan", out=F_("cum")[:, ch * 64:(ch + 1) * 64], data0=ones64, data1=F_("logw")[:, ch * 64:(ch + 1) * 64],
                                                                 initial=0.0, op0=ALU.mult, op1=ALU.add, reads=[K_("logw"), k_cst], writes=[K_("cum")])
                yield
            P.op("act", "activation", out=F_("Wt")[:], in_=F_("cum")[:], func=AF.Exp, reads=[K_("cum")], writes=[K_("Wt")])
            yield
            P.op("act", "activation", out=F_("iW")[:], in_=F_("cum")[:], func=AF.Exp, scale=-1.0, reads=[K_("cum")], writes=[K_("iW")])
            yield
            P.op("pool", "tensor_tensor", out=F_("Wp")[:], in0=F_("cum")[:], in1=F_("logw")[:], op=ALU.subtract, reads=[K_("cum"), K_("logw")], writes=[K_("Wp")])
            yield
            P.op("act", "activation", out=F_("Wp")[:], in_=F_("Wp")[:], func=AF.Exp, reads=[K_("Wp")], writes=[K_("Wp")])
            yield
            P.op("act", "activation", out=F_("kk")[:], in_=s_k[:], func=AF.Copy, scale=pvc("k_k", pt),
                 reads=[ksk, k_pv], writes=[K_("kk")])
            yield
            P.op("pool", "tensor_tensor", out=F_("sq")[:], in0=F_("kk")[:], in1=F_("kk")[:], op=ALU.mult, reads=[K_("kk")], writes=[K_("sq")])
            yield
            bk, k_bk = big()
            P.op("pe", "matmul", bk[:, 0:TB_], lhsT=b64, rhs=F_("sq")[:], start=True, stop=True, reads=[k_cst, K_("sq")], writes=[k_bk])
            P.op("act", "activation", out=F_("rn")[:], in_=bk[:, 0:TB_], func=AF.Sqrt, reads=[k_bk], writes=[K_("rn")])
            yield
            P.op("dve", "tensor_scalar", out=F_("rn")[:], in0=F_("rn")[:], scalar1=1e-12, scalar2=None, op0=ALU.max, reads=[K_("rn")], writes=[K_("rn")])
            yield
            P.op("dve", "reciprocal", out=F_("rn")[:], in_=F_("rn")[:], reads=[K_("rn")], writes=[K_("rn")])
            yield
            P.op("dve", "tensor_tensor", out=F_("kk")[:], in0=F_("kk")[:], in1=F_("rn")[:], op=ALU.mult, reads=[K_("kk"), K_("rn")], writes=[K_("kk")])
            yield
            P.op("pool", "tensor_scalar", out=F_("kmod")[:], in0=F_("aa")[:], scalar1=-1.0, scalar2=pvc("k_a", pt), op0=ALU.add, op1=ALU.mult,
                 reads=[K_("aa"), k_pv], writes=[K_("kmod")])
            yield
            P.op("dve", "scalar_tensor_tensor", out=F_("kmod")[:], in0=F_("kmod")[:], scalar=1.0, in1=s_k[:], op0=ALU.add, op1=ALU.mult,
                 reads=[K_("kmod"), ksk], writes=[K_("kmod")])
            yield
            P.op("pool", "tensor_tensor", out=F_("kka")[:], in0=F_("kk")[:], in1=F_("aa")[:], op=ALU.mult, reads=[K_("kk"), K_("aa")], writes=[K_("kka")])
            yield
            P.op("dve", "tensor_tensor", out=F_("iWC")[:].rearrange("p (c t) -> p c t", t=64), in0=F_("iW")[:].rearrange("p (c t) -> p c t", t=64),
                                                  in1=F_("Wt")[:].rearrange("p (c t) -> p c t", t=64)[:, :, 63:64].broadcast_to([128, NCH, 64]), op=ALU.mult,
                 reads=[K_("iW"), K_("Wt")], writes=[K_("iWC")])
            yield
            bdar, k_bdar = BD_AR[pt]
            v3 = lambda t, hh: t[hh * 64:(hh + 1) * 64, :].rearrange("p (c t) -> p c t", t=64)
            for hh in range(2):
                hs_ = slice(hh * 64, (hh + 1) * 64)
                P.op("dve", "scalar_tensor_tensor", out=bdar[hs_, :, 0, hh * 64:(hh + 1) * 64], in0=v3(F_("kk"), hh), scalar=-1.0,
                                                                             in1=v3(F_("Wp"), hh), op0=ALU.mult, op1=ALU.mult,
                     reads=[K_("kk"), K_("Wp")], writes=[k_bdar])
                yield
                P.op("pool", "tensor_tensor", out=bdar[hs_, :, 1, hh * 64:(hh + 1) * 64], in0=v3(s_r, hh), in1=v3(F_("Wt"), hh), op=ALU.mult,
                     reads=[ksr, K_("Wt")], writes=[k_bdar])
                yield
                for nm, a_, b_, eng in [("B", "kka", "iW", "dve"), ("K", "kmod", "iW", "pool"), ("Bh", "kka", "iWC", "dve"), ("Kh", "kmod", "iWC", "pool")]:
                    P.op(eng, "tensor_tensor", out=BDn[nm][pt][0][hs_, :, hh * 64:(hh + 1) * 64], in0=v3(F_(a_), hh),
                                                                                            in1=v3(F_(b_), hh), op=ALU.mult,
                         reads=[K_(a_), K_(b_)], writes=[BDn[nm][pt][1]])
                P.op("act", "activation", out=BDn["V"][pt][0][hs_, :, hh * 64:(hh + 1) * 64], in_=v3(s_v, hh), func=AF.Copy,
                     reads=[ksv], writes=[BDn["V"][pt][1]])
                yield
            P.op("dve", "scalar_tensor_tensor", out=F_("sq")[:], in0=s_r[:], scalar=pvc("r_k", pt), in1=F_("kmod")[:], op0=ALU.mult, op1=ALU.mult,
                 reads=[ksr, K_("kmod"), k_pv], writes=[K_("sq")])
            yield
            bk, k_bk = big()
            P.op("pe", "matmul", bk[:, 0:TB_], lhsT=b64, rhs=F_("sq")[:], start=True, stop=True, reads=[k_cst, K_("sq")], writes=[k_bk])
            P.op("dve", "tensor_tensor", out=F_("bon")[:], in0=bk[:, 0:TB_], in1=s_v[:], op=ALU.mult, reads=[k_bk, ksv], writes=[K_("bon")])
            yield
            P.op("act", "activation", out=F_("sgg")[:], in_=s_g[:], func=AF.Silu, reads=[ksg], writes=[K_("sgg")])
            yield

            s0f, k_s0f = S0f[pt]
            s0b, k_s0b = S0b[pt]
            bB, k_bB = BDn["B"][pt]
            bK, k_bK = BDn["K"][pt]
            bBh, k_bBh = BDn["Bh"][pt]
            bKh, k_bKh = BDn["Kh"][pt]
            bV, k_bV = BDn["V"][pt]
            P.op("dve", "tensor_copy", out=wc8[:], in_=F_("Wt")[:].rearrange("p (c t) -> p c t", t=64)[:, :, 63], reads=[K_("Wt")], writes=[k_wc8])
            yield
            nm1a, k_nm1a = NM1a
            nm2a, k_nm2a = NM2a
            tka, k_tka = TKa
            m1b = m1.unsqueeze(1).broadcast_to([128, 2, 256])
            m2b = m2.unsqueeze(1).broadcast_to([128, 4, 128])
            idb4 = ident_f.unsqueeze(1).broadcast_to([128, NCH, 128])
            for lhs_t, k_lhs, dst, k_dst in ((bB, k_bB, nm1a, k_nm1a), (bK, k_bK, nm2a, k_nm2a)):
                for c2 in range(NCH // 2):
                    bk, k_bk = big()
                    for u in range(2):
                        ch = 2 * c2 + u
                        P.op("pe", "matmul", bk[:, u * 256:(u + 1) * 256], lhsT=lhs_t[:, ch, :], rhs=bdar[:, ch, :, :].rearrange("p a b -> p (a b)"),
                             start=True, stop=True, reads=[k_lhs, k_bdar], writes=[k_bk])
                    P.op("dve", "tensor_tensor", out=dst[:, 2 * c2:2 * c2 + 2, :], in0=bk[:].rearrange("p (a b) -> p a b", b=256), in1=m1b, op=ALU.mult,
                         reads=[k_bk, k_cst], writes=[k_dst])
                    yield
            lcur, k_lcur = LNP[0]
            for c4 in range(NCH // 4):
                bk, k_bk = big()
                for u in range(4):
                    ch = 4 * c4 + u
                    P.op("pe", "matmul", bk[:, u * 128:(u + 1) * 128], lhsT=bdar[:, ch, 0, :], rhs=bB[:, ch, :], start=True, stop=True,
                         reads=[k_bdar, k_bB], writes=[k_bk])
                P.op("dve", "tensor_tensor", out=lcur[:, 4 * c4:4 * c4 + 4, :], in0=bk[:].rearrange("p (a b) -> p a b", b=128), in1=m2b, op=ALU.mult,
                     reads=[k_bk, k_cst], writes=[k_lcur])
                yield
            pcur, k_pcur = LNP[4]
            P.op("pool", "tensor_tensor", out=pcur[:], in0=nm1a[:, :, 0:128], in1=idb4, op=ALU.add, reads=[k_nm1a, k_cst], writes=[k_pcur])
            yield
            ncur_f = lambda ch: nm1a[:, ch, 0:128]
            k_ncur = k_nm1a
            for j in range(1, 6):
                lnew, k_lnew = LNP[j % 2]
                nnew, k_nnew = LNP[2 + j % 2]
                pnew, k_pnew = LNP[4 + j % 2]
                lb = []
                for c4 in range(NCH // 4):
                    bk, k_bk = big()
                    for u in range(4):
                        ch = 4 * c4 + u
                        P.op("pe", "matmul", bk[:, u * 128:(u + 1) * 128], lhsT=ncur_f(ch), rhs=lcur[:, ch, :], start=True, stop=True,
                             reads=[k_ncur, k_lcur], writes=[k_bk])
                    lb.append((bk, k_bk))
                nb = []
                if j <= 4:
                    for c4 in range(NCH // 4):
                        bk, k_bk = big()
                        for u in range(4):
                            ch = 4 * c4 + u
                            P.op("pe", "matmul", bk[:, u * 128:(u + 1) * 128], lhsT=lcur[:, ch, :], rhs=ncur_f(ch), start=True, stop=True,
                                 reads=[k_ncur, k_lcur], writes=[k_bk])
                        nb.append((bk, k_bk))
                for c4, (bk, k_bk) in enumerate(lb):
                    evac("act", lnew[:, 4 * c4:4 * c4 + 4, :], bk[:].rearrange("p (a b) -> p a b", b=128), [k_bk], [k_lnew])
                    yield
                for c4, (bk, k_bk) in enumerate(nb):
                    evac("dve", nnew[:, 4 * c4:4 * c4 + 4, :], bk[:].rearrange("p (a b) -> p a b", b=128), [k_bk], [k_nnew])
                    yield
                for c4 in range(NCH // 4):
                    bk, k_bk = big()
                    for u in range(4):
                        ch = 4 * c4 + u
                        P.op("pe", "matmul", bk[:, u * 128:(u + 1) * 128], lhsT=lnew[:, ch, :], rhs=pcur[:, ch, :], start=True, stop=True,
                             reads=[k_lnew, k_pcur], writes=[k_bk])
                    P.op("dve", "tensor_tensor", out=pnew[:, 4 * c4:4 * c4 + 4, :], in0=bk[:].rearrange("p (a b) -> p a b", b=128),
                         in1=pcur[:, 4 * c4:4 * c4 + 4, :], op=ALU.add, reads=[k_bk, k_pcur], writes=[k_pnew])
                    yield
                lcur, k_lcur = lnew, k_lnew
                if j <= 4:
                    ncur_f = (lambda nn: (lambda ch: nn[:, ch, :]))(nnew)
                    k_ncur = k_nnew
                pcur, k_pcur = pnew, k_pnew
            for c2 in range(NCH // 2):
                bk, k_bk = big()
                bkb = bk[:].bitcast(BF16)
                for u in range(2):
                    ch = 2 * c2 + u
                    for i3, (src, ks) in enumerate([(bV, k_bV), (bBh, k_bBh), (bKh, k_bKh)]):
                        P.op("pe", "transpose", bkb[:, (u * 3 + i3) * 128:(u * 3 + i3 + 1) * 128], src[:, ch, :], identb,
                             reads=[ks, k_cstb], writes=[k_bk])
                evac("act", tka[:, 2 * c2:2 * c2 + 2, :, :].rearrange("p a b c -> p (a b c)"), bkb[:, 0:768], [k_bk], [k_tka])
                yield
            def chain_gen():
                for ch in range(NCH):
                    zb, k_zb = Zb[ch % 2]
                    ub, k_ub = Ub[ch % 2]
                    ps, k_ps = quarter()
                    P.op("pe", "matmul", ps, lhsT=bdar[:, ch, 0, :], rhs=s0b[:], start=True, stop=False, reads=[k_bdar, k_s0b], writes=[k_ps])
                    P.op("pe", "matmul", ps, lhsT=nm2a[:, ch, 0:128], rhs=tka[:, ch, 0, :], start=False, stop=True, reads=[k_nm2a, k_tka], writes=[k_ps])
                    evac("dve", zb[:], ps, [k_ps], [k_zb])
                    yield
                    ps, k_ps = quarter()
                    P.op("pe", "matmul", ps, lhsT=pcur[:, ch, :], rhs=zb[:], start=True, stop=True, reads=[k_pcur, k_zb], writes=[k_ps])
                    evac("act", ub[:], ps, [k_ps], [k_ub])
                    yield
                    ps, k_ps = quarter()
                    P.op("pe", "matmul", ps, lhsT=tka[:, ch, 1, :], rhs=ub[:], start=True, stop=False, reads=[k_tka, k_ub], writes=[k_ps])
                    P.op("pe", "matmul", ps, lhsT=tka[:, ch, 2, :], rhs=tka[:, ch, 0, :], start=False, stop=True, reads=[k_tka], writes=[k_ps])
                    yq, k_ybk = quarter()
                    P.op("pe", "matmul", yq, lhsT=s0b[:], rhs=bdar[:, ch, 1, :], start=True, stop=False, reads=[k_bdar, k_s0b], writes=[k_ybk])
                    P.op("pe", "matmul", yq, lhsT=ub[:], rhs=nm1a[:, ch, 128:256], start=False, stop=False, reads=[k_ub, k_nm1a], writes=[k_ybk])
                    P.op("pe", "matmul", yq, lhsT=tka[:, ch, 0, :], rhs=nm2a[:, ch, 128:256], start=False, stop=True, reads=[k_tka, k_nm2a], writes=[k_ybk])
                    P.op("dve", "scalar_tensor_tensor", out=s0f[:], in0=s0f[:], scalar=wc8[:, ch:ch + 1], in1=ps, op0=ALU.mult, op1=ALU.add,
                         reads=[k_s0f, k_wc8, k_ps], writes=[k_s0f])
                    yield
                    P.op("act", "activation", out=s0b[:], in_=s0f[:], func=AF.Copy, reads=[k_s0f], writes=[k_s0b])
                    yield
                    P.op("dve", "tensor_copy", out=F_("yT")[0:64, ch * 64:(ch + 1) * 64], in_=yq[0:64, 0:64], reads=[k_ybk], writes=[K_("yT")])
                    yield
                    P.op("act", "activation", out=F_("yT")[64:128, ch * 64:(ch + 1) * 64], in_=yq[64:128, 64:128], func=AF.Copy, reads=[k_ybk], writes=[K_("yT")])
                    yield
            yield from chain_gen()
            bk, k_bk = big()
            P.op("pe", "matmul", bk[:, 0:TB_], lhsT=b64m, rhs=F_("yT")[:], start=True, stop=True, reads=[k_cst, K_("yT")], writes=[k_bk])
            P.op("dve", "tensor_tensor", out=F_("yT")[:], in0=F_("yT")[:], in1=bk[:, 0:TB_], op=ALU.subtract, reads=[k_bk, K_("yT")], writes=[K_("yT")])
            yield
            P.op("pool", "tensor_tensor", out=F_("sq")[:], in0=F_("yT")[:], in1=F_("yT")[:], op=ALU.mult, reads=[K_("yT")], writes=[K_("sq")])
            yield
            bk, k_bk = big()
            P.op("pe", "matmul", bk[:, 0:TB_], lhsT=b64m, rhs=F_("sq")[:], start=True, stop=True, reads=[k_cst, K_("sq")], writes=[k_bk])
            P.op("dve", "tensor_scalar", out=F_("rn")[:], in0=bk[:, 0:TB_], scalar1=64e-5, scalar2=None, op0=ALU.add, reads=[k_bk], writes=[K_("rn")])
            yield
            P.op("act", "activation", out=F_("rn")[:], in_=F_("rn")[:], func=AF.Sqrt, reads=[K_("rn")], writes=[K_("rn")])
            yield
            P.op("dve", "reciprocal", out=F_("rn")[:], in_=F_("rn")[:], reads=[K_("rn")], writes=[K_("rn")])
            yield
            P.op("dve", "tensor_tensor", out=F_("yT")[:], in0=F_("yT")[:], in1=F_("rn")[:], op=ALU.mult, reads=[K_("yT"), K_("rn")], writes=[K_("yT")])
            yield
            P.op("dve", "tensor_scalar", out=F_("yT")[:], in0=F_("yT")[:], scalar1=pvc("lnw", pt), scalar2=pvc("lnb", pt), op0=ALU.mult, op1=ALU.add,
                 reads=[K_("yT"), k_pv], writes=[K_("yT")])
            yield
            P.op("dve", "tensor_tensor", out=F_("yT")[:], in0=F_("yT")[:], in1=F_("bon")[:], op=ALU.add, reads=[K_("yT"), K_("bon")], writes=[K_("yT")])
            yield
            P.op("dve", "tensor_tensor", out=ysB[pt][0][:], in0=F_("yT")[:], in1=F_("sgg")[:], op=ALU.mult, reads=[K_("yT"), K_("sgg")], writes=[ysB[pt][1]])
            yield
            out_toks.append(P.dma("sp", "dma_start", out=ys_d[1, pt * 128:(pt + 1) * 128, t0:t0 + TB_], in_=ysB[pt][0][:],
                                  reads=[ysB[pt][1]]))
            return
            yield

        gens = []
        gid = {}
        if DO_A or DO_C:
            def s1_gen():
                if DO_A:
                    yield from lru_gen()
                if DO_C:
                    yield from ret_gen()
            gens.append(s1_gen())
            gid[id(gens[-1])] = 0
        if DO_B:
            gens.append(rwkv_gen(0))
            gid[id(gens[-1])] = 1
            gens.append(rwkv_gen(1))
            gid[id(gens[-1])] = 2
        if tb + 1 < NTB:
            gens.append(stage1_gen(tb + 1))
            gid[id(gens[-1])] = 0
        while gens:
            for g_ in list(gens):
                cur_stream[0] = gid[id(g_)]
                try:
                    next(g_)
                except StopIteration:
                    gens.remove(g_)
        cur_stream[0] = 0

    return out_toks


def _consts(g, S):
    cst = np.zeros((128, NCST), np.float32)
    cst[:, CST["ident"]:CST["ident"] + 128] = np.eye(128, dtype=np.float32)
    blk = np.kron(np.eye(2, dtype=np.float32), np.ones((64, 64), np.float32))
    cst[:, CST["b64"]:CST["b64"] + 128] = blk
    cst[:, CST["b64m"]:CST["b64m"] + 128] = blk / 64.0
    su = np.triu(np.ones((64, 64), np.float32), 1)
    ui = np.triu(np.ones((64, 64), np.float32), 0)
    cst[:, CST["m1"]:CST["m1"] + 128] = np.kron(np.eye(2, dtype=np.float32), su)
    cst[:, CST["m1"] + 128:CST["m1"] + 256] = np.kron(np.eye(2, dtype=np.float32), ui)
    cst[:, CST["m2"]:CST["m2"] + 128] = np.kron(np.eye(2, dtype=np.float32), su.T)
    gam = np.float32(1.0) - np.exp2(np.float32(-5.0 - g)).astype(np.float32)
    lg = np.log1p(-np.exp2(np.float32(-5.0 - g))).astype(np.float64)
    idx = np.arange(128)
    rel = idx[None, :] - idx[:, None]
    cst[:, CST["rmask"]:CST["rmask"] + 128] = np.where(rel >= 0, np.exp(lg * np.maximum(rel, 0)), 0.0).astype(np.float32)
    qrow = np.exp(lg * (idx + 1.0)).astype(np.float32)
    cst[:, CST["qdec"]:CST["qdec"] + 512] = np.tile(qrow[None, :], (128, 4))
    cst[:, CST["ones"]:CST["ones"] + 64] = 1.0
    kdec = np.exp(lg * (127.0 - idx)).astype(np.float32)
    cdec = np.float32(np.exp(lg * 128.0))
    half = 128
    inv_freq = (1.0 / (10000.0 ** np.linspace(0.0, 1.0, half, dtype=np.float32))).astype(np.float32)
    ang = np.arange(S, dtype=np.float32)[None, :] * inv_freq[:, None]
    return cst, kdec, cdec, np.cos(ang).astype(np.float32), np.sin(ang).astype(np.float32)


def _prep_A(inp, l, b, g, S, light=False):
    w_in = inp["w_in"][l]
    sl = lambda off: np.arange(off + g * 256, off + (g + 1) * 256)
    cols = np.concatenate([sl(0), sl(OFF_GA), sl(OFF_B), sl(OFF_B + 1024), sl(OFF_B + 2048), sl(OFF_B + 3072),
                           np.arange(OFF_B + 4096, OFF_B + 4224), sl(OFF_C), sl(OFF_C + 1024), sl(OFF_C + 2048), sl(OFF_C + 3072)])
    assert cols.shape[0] == NCOLA
    cst, kdec, cdec, cosT, sinT = _consts(g, S)
    pv = np.zeros((128, NPV), np.float32)

    def put(nm, vec256):
        pv[:, PV[nm]:PV[nm] + 2] = np.asarray(vec256, np.float32).reshape(2, 128).T
    ch = slice(g * 256, (g + 1) * 256)
    for j in range(4):
        put(f"cw{j}", inp["conv_w"][l][j, ch])
    put("cb", inp["conv_b"][l][ch])
    put("gbr", inp["lru_gate_b"][l][0, ch])
    put("gbi", inp["lru_gate_b"][l][1, ch])
    put("lam", inp["lru_lambda"][l][ch])
    mu = inp["shift_mu"][l]
    for j, nm in enumerate("rkvg"):
        put("mu_" + nm, mu[j * 1024 + g * 256: j * 1024 + (g + 1) * 256])
    pv[:, PV["mu_wl"]] = mu[4096:4224]
    put("w0", inp["decay_w0"][l][ch])
    put("a0", inp["iclr_a0"][l][ch])
    put("k_k", inp["k_k"][l][ch])
    put("k_a", inp["k_a"][l][ch])
    put("r_k", inp["r_k"][l].reshape(-1)[ch])
    put("lnw", inp["lnx_w"][l][ch])
    put("lnb", inp["lnx_b"][l][ch])
    pv[:, PV["kdec"]] = kdec
    pv[:, PV["cdec"]] = cdec
    z64 = np.zeros((64, 256), np.float32)
    return {
        "x": None if light else np.ascontiguousarray(inp["x"][b, :S]),
        "wA": None if light else np.ascontiguousarray(w_in[:, cols]),
        "gw": np.ascontiguousarray(inp["lru_gate_w"][l][:, g]),
        "w2p": np.concatenate([inp["decay_w2"][l][:, ch], z64], 0),
        "a2p": np.concatenate([z64, inp["iclr_a2"][l][:, ch]], 0),
        "pv": pv,
        "nwb": np.ascontiguousarray(np.broadcast_to(inp["norm_w"][l][None, :], (128, D))),
        "cst": cst, "cosT": cosT, "sinT": sinT,
    }


def emit_B(cx, NT, last, d):
    nc = cx.nc
    P = cx.P
    sb = cx.sb
    x_d, ys_d, nwb_d, fnwb_d, wM_d, bm_d, wBr_d, wO_d, id_d, sel_d, out_d = (d[k] for k in
        ("x", "ys", "nwb", "fnwb", "wM", "bm", "wBr", "wO", "ident", "sel", "out"))
    banks = cx.banks
    cur_stream = [0]
    bank_ctr = [0, 0]

    def big():
        st = cur_stream[0]
        i = 3 * st + bank_ctr[st] % 3
        bank_ctr[st] += 1
        return banks[i], f"bank{i}"
    hps_l = [banks[6][:].bitcast(BF16), banks[7][:].bitcast(BF16)]

    wM, k_wM = sb([128, 8, 3 * D], BF16, "wM")
    wBr, k_wBr = sb([128, 3, 8, D], BF16, "wBr")
    wO, k_wO = sb([128, 8, D], BF16, "wO")
    nwb, k_nwb = sb([128, D], F32, "nwb")
    fnwb, k_fnwb = sb([128, D], F32, "fnwb")
    bm, k_bm = sb([6, 512], F32, "bm")
    sel, k_sel = sb([6, 6, 128], F32, "sel")
    idb, k_idb = sb([128, 128], BF16, "idb")
    wM_v = wM_d.rearrange("(kc p) n -> p kc n", p=128)
    for kc in range(8):
        P.dma("pool", "dma_start", out=wM[:, kc, :], in_=wM_v[:, kc, :], writes=[k_wM], par=True)
    for n in range(3):
        wv = wBr_d[n].rearrange("(kc p) d -> p kc d", p=128)
        for kc in range(0, 8, 4):
            P.dma("pool", "dma_start", out=wBr[:, n, kc:kc + 4, :], in_=wv[:, kc:kc + 4, :], writes=[k_wBr], par=True)
    wv = wO_d.rearrange("(kc p) d -> p kc d", p=128)
    for kc in range(0, 8, 4):
        P.dma("pool", "dma_start", out=wO[:, kc:kc + 4, :], in_=wv[:, kc:kc + 4, :], writes=[k_wO], par=True)
    P.dma("sp", "dma_start", out=nwb[:], in_=nwb_d, writes=[k_nwb])
    P.dma("sp", "dma_start", out=fnwb[:], in_=fnwb_d, writes=[k_fnwb])
    P.dma("sp", "dma_start", out=bm[:], in_=bm_d, writes=[k_bm])
    P.dma("pool", "dma_start", out=idb[:], in_=id_d, writes=[k_idb])
    P.dma("sp", "dma_start", out=sel[:], in_=sel_d, writes=[k_sel])

    YB = 256
    ysb = [sb([128, 3, 8, YB], BF16, f"ysb{i}") for i in range(2)]
    xt = [sb([128, D], F32, f"xt{i}") for i in range(2)]
    junk, k_junk = sb([128, D], BF16, "junk")
    ss = [sb([128, 4], F32, f"ss{i}") for i in range(2)]
    xs = [sb([128, D], BF16, f"xs{i}") for i in range(2)]
    hT = [sb([128, 8, 128], BF16, f"hT{i}") for i in range(2)]
    gt = [sb([128, 512], F32, f"gt{i}") for i in range(4)]
    tmp = [sb([128, 512], F32, f"tmp{i}") for i in range(4)]
    mg = [sb([128, D], F32, f"mg{i}") for i in range(2)]
    mgb = [sb([128, D], BF16, f"mgb{i}") for i in range(2)]
    mT = [sb([128, 8, 128], BF16, f"mT{i}") for i in range(2)]
    xn = [sb([128, D], F32, f"xn{i}") for i in range(2)]
    out_toks = []

    def rms(src, k_src, ssb, k_ss, wb, k_wb, dst, k_dst):
        P.op("act", "activation", out=junk[:], in_=src[:], func=AF.Square, accum_out=ssb[:, 0:1], reads=[k_src], writes=[k_junk, k_ss])
        P.op("dve", "tensor_scalar", out=ssb[:, 1:2], in0=ssb[:, 0:1], scalar1=1.0 / D, scalar2=EPS, op0=ALU.mult, op1=ALU.add, reads=[k_ss], writes=[k_ss])
        P.op("act", "activation", out=ssb[:, 2:3], in_=ssb[:, 1:2], func=AF.Sqrt, reads=[k_ss], writes=[k_ss])
        P.op("dve", "reciprocal", out=ssb[:, 3:4], in_=ssb[:, 2:3], reads=[k_ss], writes=[k_ss])
        P.op("dve", "scalar_tensor_tensor", out=dst[:], in0=src[:], scalar=ssb[:, 3:4], in1=wb[:], op0=ALU.mult, op1=ALU.mult,
             reads=[k_src, k_ss, k_wb], writes=[k_dst])

    for tb in range(NT // YB):
        yb, k_yb = ysb[tb % 2]
        for n in range(3):
            yv = ys_d[n].rearrange("(wc p) t -> p wc t", p=128)
            P.dma("sp", "dma_start", out=yb[:, n, :, :], in_=yv[:, :, tb * YB:(tb + 1) * YB], writes=[k_yb])
        def tile_gen(tt):
            ti = tb * (YB // 128) + tt
            hps_b = hps_l[tt]
            mps_b = hps_l[tt]
            k_hps = f"bank{6 + tt}"
            r0 = ti * 128
            xb_, k_x = xt[ti % 2]
            ssb, k_ss = ss[ti % 2]
            xsb, k_xs = xs[ti % 2]
            hTt, k_hT = hT[ti % 2]
            mgt, k_mg = mg[ti % 2]
            mgbt, k_mgb = mgb[ti % 2]
            mTt, k_mT = mT[ti % 2]
            xnt, k_xn = xn[ti % 2]
            xot, k_xo = xb_, k_x
            P.dma("sp", "dma_start", out=xb_[:], in_=x_d[r0:r0 + 128, :], writes=[k_x])
            rms(xb_, k_x, ssb, k_ss, nwb, k_nwb, xsb, k_xs)
            yield
            for kc in range(8):
                P.op("pe", "transpose", hps_b[:, kc * 128:(kc + 1) * 128], xsb[:, kc * 128:(kc + 1) * 128], idb[:], reads=[k_xs, k_idb], writes=[k_hps])
            P.op("act", "activation", out=hTt[:].rearrange("p k t -> p (k t)"), in_=hps_b, func=AF.Copy, reads=[k_hps], writes=[k_hT])
            yield
            for n in range(3):
                for hf in range(2):
                    c0 = n * D + hf * 512
                    g_, k_g = gt[tt * 2 + (n * 2 + hf) % 2]
                    t_, k_t = tmp[tt * 2 + (n * 2 + hf) % 2]
                    bk, k_bk = big()
                    P.op("pe", "matmul", bk[:], lhsT=sel[:, n * 2 + hf, :], rhs=bm[:], start=True, stop=False, reads=[k_sel, k_bm], writes=[k_bk])
                    for kc in range(8):
                        P.op("pe", "matmul", bk[:], lhsT=hTt[:, kc, :], rhs=wM[:, kc, c0:c0 + 512], start=False, stop=(kc == 7), reads=[k_hT, k_wM], writes=[k_bk])
                    P.op("act", "activation", out=g_[:], in_=bk[:], func=AF.Sigmoid, reads=[k_bk], writes=[k_g])
                    yield
                    bk, k_bk = big()
                    for wc in range(8):
                        P.op("pe", "matmul", bk[:], lhsT=yb[:, n, wc, tt * 128:(tt + 1) * 128], rhs=wBr[:, n, wc, hf * 512:(hf + 1) * 512],
                             start=(wc == 0), stop=(wc == 7), reads=[k_yb, k_wBr], writes=[k_bk])
                    msl = mgt[:, hf * 512:(hf + 1) * 512]
                    if n == 0:
                        P.op("dve", "tensor_tensor", out=msl, in0=g_[:], in1=bk[:], op=ALU.mult, reads=[k_g, k_bk], writes=[k_mg])
                        yield
                    else:
                        P.op("dve", "tensor_tensor", out=t_[:], in0=g_[:], in1=bk[:], op=ALU.mult, reads=[k_g, k_bk], writes=[k_t])
                        yield
                        P.op("pool", "tensor_tensor", out=msl, in0=msl, in1=t_[:], op=ALU.add, reads=[k_mg, k_t], writes=[k_mg])
                        yield
            P.op("pool", "tensor_copy", out=mgbt[:], in_=mgt[:], reads=[k_mg], writes=[k_mgb])
            yield
            for kc in range(8):
                P.op("pe", "transpose", mps_b[:, kc * 128:(kc + 1) * 128], mgbt[:, kc * 128:(kc + 1) * 128], idb[:], reads=[k_mgb, k_idb], writes=[k_hps])
            P.op("act", "activation", out=mTt[:].rearrange("p k t -> p (k t)"), in_=mps_b, func=AF.Copy, reads=[k_hps], writes=[k_mT])
            yield
            for hf in range(2):
                bk, k_bk = big()
                for kc in range(8):
                    P.op("pe", "matmul", bk[:], lhsT=mTt[:, kc, :], rhs=wO[:, kc, hf * 512:(hf + 1) * 512], start=(kc == 0), stop=(kc == 7),
                         reads=[k_mT, k_wO], writes=[k_bk])
                P.op("dve", "tensor_tensor", out=xnt[:, hf * 512:(hf + 1) * 512], in0=xb_[:, hf * 512:(hf + 1) * 512], in1=bk[:], op=ALU.add,
                     reads=[k_x, k_bk], writes=[k_xn])
                yield
            if last:
                rms(xnt, k_xn, ssb, k_ss, fnwb, k_fnwb, xot, k_xo)
                yield
                out_toks.append(P.dma("sp", "dma_start", out=out_d[r0:r0 + 128, :], in_=xot[:], reads=[k_xo]))
            else:
                out_toks.append(P.dma("sp", "dma_start", out=out_d[r0:r0 + 128, :], in_=xnt[:], reads=[k_xn]))
            return
            yield

        gens = [tile_gen(tt) for tt in range(YB // 128)]
        gidx = {id(g): i for i, g in enumerate(gens)}
        while gens:
            for g_ in list(gens):
                cur_stream[0] = gidx[id(g_)]
                try:
                    next(g_)
                except StopIteration:
                    gens.remove(g_)
    return out_toks


def _prep_B(inp, l, xcur, ys_full, b, j, NT):
    return {
        "x": np.ascontiguousarray(xcur[b, j * NT:(j + 1) * NT]),
        "ys": np.ascontiguousarray(ys_full[b][:, :, j * NT:(j + 1) * NT]),
        "nwb": np.ascontiguousarray(np.broadcast_to(inp["norm_w"][l][None, :], (128, D))),
        "fnwb": np.ascontiguousarray(np.broadcast_to(inp["final_norm_w"][None, :], (128, D))),
        "wM": np.ascontiguousarray(inp["w_in"][l][:, OFF_M:]),
        "bm": np.ascontiguousarray(inp["b_merge"][l].reshape(6, 512)),
        "sel": np.ascontiguousarray(np.broadcast_to(np.eye(6, dtype=np.float32)[:, :, None], (6, 6, 128))),
        "wBr": np.ascontiguousarray(inp["w_branch"][l]),
        "wO": np.ascontiguousarray(inp["w_out"][l]),
        "ident": np.eye(128, dtype=np.float32),
    }


def build_A(S, flags=(1, 1, 1)):
    cx = Cx()
    d = {"x": cx.dram("x", [S, D]), "wA_cols": [(0, cx.dram("wA", [D, NCOLA]))], "gw": cx.dram("gw", [2, 256, 256]),
         "w2p": cx.dram("w2p", [128, 256]), "a2p": cx.dram("a2p", [128, 256]), "pv": cx.dram("pv", [128, NPV]),
         "nwb": cx.dram("nwb", [128, D]), "cst": cx.dram("cst", [128, NCST]), "cosT": cx.dram("cosT", [128, S]),
         "sinT": cx.dram("sinT", [128, S]), "ysT": cx.dram("ysT", [3, 256, S], BF16, "ExternalOutput")}
    toks = emit_A(cx, S, d, flags)
    cx.P.finish(toks)
    return cx.nc


def build_B(NT, last):
    cx = Cx()
    d = {"x": cx.dram("x", [NT, D]), "ys": cx.dram("ys", [3, D, NT], BF16), "nwb": cx.dram("nwb", [128, D]),
         "fnwb": cx.dram("fnwb", [128, D]), "wM": cx.dram("wM", [D, 3 * D]), "bm": cx.dram("bm", [6, 512]),
         "wBr": cx.dram("wBr", [3, D, D]), "wO": cx.dram("wO", [D, D]), "ident": cx.dram("ident", [128, 128]),
         "sel": cx.dram("sel", [6, 6, 128]), "out": cx.dram("out", [NT, D], F32, "ExternalOutput")}
    toks = emit_B(cx, NT, last, d)
    cx.P.finish(toks)
    return cx.nc


_COLS = [(0, 0), (256, OFF_GA), (512, OFF_B), (768, OFF_B + 1024), (1024, OFF_B + 2048), (1280, OFF_B + 3072),
         (1664, OFF_C), (1920, OFF_C + 1024), (2176, OFF_C + 2048), (2432, OFF_C + 3072)]


def build_fused(S, groups=(0, 1, 2, 3)):
    cx = Cx()
    nc, P = cx.nc, cx.P
    x_in = cx.dram("x", [S, D])
    w_in = cx.dram("w_in", [DEPTH, D, N_IN])
    gw = cx.dram("gw", [DEPTH, 4, 2, 256, 256])
    w2p = cx.dram("w2p", [DEPTH, 4, 128, 256])
    a2p = cx.dram("a2p", [DEPTH, 4, 128, 256])
    pv = cx.dram("pv", [DEPTH, 4, 128, NPV])
    nwb = cx.dram("nwb", [DEPTH, 128, D])
    fnwb = cx.dram("fnwb", [128, D])
    cst = cx.dram("cst", [4, 128, NCST])
    cosT = cx.dram("cosT", [128, S])
    sinT = cx.dram("sinT", [128, S])
    bm = cx.dram("bm", [DEPTH, 6, 512])
    sel = cx.dram("sel", [6, 6, 128])
    wBr = cx.dram("wBr", [DEPTH, 3, D, D])
    wO = cx.dram("wO", [DEPTH, D, D])
    ident = cx.dram("ident", [128, 128])
    out = cx.dram("out", [S, D], F32, "ExternalOutput")
    ys_scr = nc.dram_tensor("ys_scr", [3, D, S], BF16, kind="Internal").ap()
    x_scr = nc.dram_tensor("x_scr", [S, D], F32, kind="Internal").ap()
    toks = []
    for l in range(DEPTH):
        x_src = x_in if l == 0 else x_scr
        for g in groups:
            cx.reset()
            cols = [(c0, w_in[l][:, off + g * 256: off + (g + 1) * 256]) for c0, off in _COLS]
            cols.append((1536, w_in[l][:, OFF_B + 4096: OFF_B + 4224]))
            d = {"x": x_src, "wA_cols": cols, "gw": gw[l, g], "w2p": w2p[l, g], "a2p": a2p[l, g], "pv": pv[l, g],
                 "nwb": nwb[l], "cst": cst[g], "cosT": cosT, "sinT": sinT, "ysT": ys_scr[:, g * 256:(g + 1) * 256, :]}
            for t in emit_A(cx, S, d):
                P.wait_tok("sp", t)
            P.barrier()
        cx.reset()
        last = (l == DEPTH - 1)
        d = {"x": x_src, "ys": ys_scr, "nwb": nwb[l], "fnwb": fnwb, "wM": w_in[l][:, OFF_M:], "bm": bm[l], "wBr": wBr[l],
             "wO": wO[l], "ident": ident, "sel": sel, "out": out if last else x_scr}
        toks = emit_B(cx, S, last, d)
        for t in toks:
            P.wait_tok("sp", t)
        P.barrier()
    P.finish(toks)
    return nc


def _prep_fused(inp, b, S):
    pvs = np.zeros((DEPTH, 4, 128, NPV), np.float32)
    csts = np.zeros((4, 128, NCST), np.float32)
    w2 = np.zeros((DEPTH, 4, 128, 256), np.float32)
    a2 = np.zeros((DEPTH, 4, 128, 256), np.float32)
    gws = np.zeros((DEPTH, 4, 2, 256, 256), np.float32)
    cosT = sinT = None
    for l in range(DEPTH):
        for g in range(4):
            m = _prep_A(inp, l, b, g, S, light=True)
            pvs[l, g] = m["pv"]
            w2[l, g] = m["w2p"]
            a2[l, g] = m["a2p"]
            gws[l, g] = m["gw"]
            csts[g] = m["cst"]
            cosT, sinT = m["cosT"], m["sinT"]
    return {
        "x": np.ascontiguousarray(inp["x"][b, :S]), "w_in": np.ascontiguousarray(inp["w_in"]), "gw": gws, "w2p": w2, "a2p": a2,
        "pv": pvs, "nwb": np.ascontiguousarray(np.broadcast_to(inp["norm_w"][:, None, :], (DEPTH, 128, D))),
        "fnwb": np.ascontiguousarray(np.broadcast_to(inp["final_norm_w"][None, :], (128, D))),
        "cst": csts, "cosT": cosT, "sinT": sinT, "bm": np.ascontiguousarray(inp["b_merge"].reshape(DEPTH, 6, 512)),
        "sel": np.ascontiguousarray(np.broadcast_to(np.eye(6, dtype=np.float32)[:, :, None], (6, 6, 128))),
        "wBr": np.ascontiguousarray(inp["w_branch"]), "wO": np.ascontiguousarray(inp["w_out"]),
        "ident": np.eye(128, dtype=np.float32),
    }


def _forward_unfused(inp, S):
    NT = S // 4
    xcur = np.ascontiguousarray(inp["x"][:, :S]).astype(np.float32)
    for l in range(DEPTH):
        ncA = build_A(S)
        inp_l = dict(inp)
        inp_l["x"] = xcur
        in_maps = [_prep_A(inp_l, l, c // 4, c % 4, S) for c in range(8)]
        res = run_bass_kernel_spmd(ncA, in_maps, core_ids=list(range(8)))
        ys_full = [np.concatenate([res.results[b * 4 + g]["ysT"] for g in range(4)], axis=1) for b in range(2)]
        ncB = build_B(NT, last=(l == DEPTH - 1))
        in_maps = [_prep_B(inp, l, xcur, ys_full, c // 4, c % 4, NT) for c in range(8)]
        res = run_bass_kernel_spmd(ncB, in_maps, core_ids=list(range(8)))
        xcur = np.stack([np.concatenate([res.results[b * 4 + j]["out"] for j in range(4)], axis=0) for b in range(2)], axis=0)
    return xcur


def _forward_fused(inp, S):
    nc = build_fused(S)
    maps = [_prep_fused(inp, b, S) for b in range(2)]
    in_maps = [maps[c // 4] for c in range(8)]
    res = run_bass_kernel_spmd(nc, in_maps, core_ids=list(range(8)))
    return np.stack([res.results[0]["out"], res.results[4]["out"]], axis=0)


def kernel(**inputs):
    inp = {k: np.asarray(v) for k, v in inputs.items()}
    return _forward_fused(inp, inp["x"].shape[1]).astype(np.float32)
```

```python
import numpy as np
import ml_dtypes
import concourse.bass as bass
import concourse.mybir as mybir
from concourse.bass_utils import run_bass_kernel_spmd

F32 = mybir.dt.float32
BF16 = mybir.dt.bfloat16
AF = mybir.ActivationFunctionType
ALU = mybir.AluOpType

D = 1024
DEPTH = 2
EPS = 1e-6
OFF_GA = 1024
OFF_B = 2048
B_COLS = 4 * 1024 + 128
OFF_C = OFF_B + B_COLS
OFF_M = OFF_C + 4096
N_IN = OFF_M + 3072
TB = 512
NCOLA = 2688

ENGS = ("pe", "dve", "act", "pool", "sp")
NDMASEM = 8


class Prog:
    def __init__(self, nc, same_engine_sync=True):
        self.nc = nc
        self.ops = {e: [] for e in ENGS}
        self.csem = {e: nc.alloc_semaphore(name=f"c_{e}") for e in ENGS if e != "sp"}
        self.ccnt = {e: 0 for e in ENGS}
        self.dsem = {e: [nc.alloc_semaphore(name=f"d_{e}{i}") for i in range(NDMASEM)]
                     for e in ("sp", "act", "pool")}
        self.dval = {e: [0] * NDMASEM for e in ("sp", "act", "pool")}
        self.dk = {e: 0 for e in ("sp", "act", "pool")}
        self.waited = {}
        self.lastw = {}
        self.readers = {}
        self.ses = same_engine_sync
        self.nops = 0

    def _need(self, e, tok, out):
        semh, val, src = tok
        if src == e and (e == "pe" or not self.ses):
            return
        if self.waited.get((e, id(semh)), 0) >= val:
            return
        cur = out.get(id(semh))
        if cur is None or cur[1] < val:
            out[id(semh)] = (semh, val)

    def _deps(self, e, reads, writes, par=False):
        out = {}
        for k in reads:
            for t in self.lastw.get(k, ()):
                self._need(e, t, out)
        for k in writes:
            if not par:
                for t in self.lastw.get(k, ()):
                    self._need(e, t, out)
            for t in self.readers.get(k, ()):
                self._need(e, t, out)
        for semh, val in out.values():
            self.ops[e].append(("wait", semh, val))
            self.waited[(e, id(semh))] = val

    def _commit(self, tok, reads, writes, par=False):
        for k in reads:
            self.readers.setdefault(k, []).append(tok)
        for k in writes:
            if par:
                self.lastw.setdefault(k, []).append(tok)
            else:
                self.lastw[k] = [tok]
                self.readers[k] = []

    def op(self, e, meth, *args, reads=(), writes=(), **kw):
        fn = (lambda g: getattr(g, meth)(*args, **kw)) if isinstance(meth, str) else meth
        xr = [k for k in reads if k.startswith("bank") or k == "hps"]
        if xr:
            writes = list(writes) + xr
            reads = [k for k in reads if k not in xr]
        self._deps(e, reads, writes)
        self.ccnt[e] += 1
        tok = (self.csem[e], self.ccnt[e], e)
        self.ops[e].append(("op", fn, self.csem[e], 1))
        self._commit(tok, reads, writes)
        self.nops += 1
        return tok

    def dma(self, e, meth, *args, reads=(), writes=(), par=False, **kw):
        fn = (lambda g: getattr(g, meth)(*args, **kw)) if isinstance(meth, str) else meth
        i = self.dk[e] % NDMASEM
        self.dk[e] += 1
        semh = self.dsem[e][i]
        if self.dval[e][i] > 0:
            k = (e, id(semh))
            if self.waited.get(k, 0) < self.dval[e][i]:
                self.ops[e].append(("wait", semh, self.dval[e][i]))
                self.waited[k] = self.dval[e][i]
        self._deps(e, reads, writes, par)
        self.dval[e][i] += 16
        tok = (semh, self.dval[e][i], "dma_" + e)
        self.ops[e].append(("op", fn, semh, 16))
        self._commit(tok, reads, writes, par)
        self.nops += 1
        return tok

    def wait_tok(self, e, tok):
        semh, val, _ = tok
        k = (e, id(semh))
        if self.waited.get(k, 0) < val:
            self.ops[e].append(("wait", semh, val))
            self.waited[k] = val

    def barrier(self):
        for e in ENGS:
            for e2 in self.csem:
                if e2 != e and self.ccnt[e2] > 0:
                    self.wait_tok(e, (self.csem[e2], self.ccnt[e2], e2))
            for q in self.dsem:
                for i in range(NDMASEM):
                    if self.dval[q][i] > 0:
                        self.wait_tok(e, (self.dsem[q][i], self.dval[q][i], "dma_" + q))

    def finish(self, final_toks):
        for t in final_toks:
            self.wait_tok("sp", t)
        nc = self.nc
        ops = self.ops

        def replay(engine, lst):
            for it in lst:
                if it[0] == "wait":
                    engine.wait_ge(it[1], it[2])
                else:
                    it[1](engine).then_inc(it[2], it[3])

        with nc.Block() as block:
            @block.tensor
            def _(e):
                replay(e, ops["pe"])

            @block.vector
            def _(e):
                replay(e, ops["dve"])

            @block.scalar
            def _(e):
                replay(e, ops["act"])

            @block.gpsimd
            def _(e):
                replay(e, ops["pool"])

            @block.sync
            def _(e):
                replay(e, ops["sp"])


class Cx:
    def __init__(self):
        self.nc = bass.Bass("TRN2", target_bir_lowering=False)
        self.P = Prog(self.nc)
        self.banks = [self.nc.alloc_psum_tensor(f"bank{i}", [128, 512], F32) for i in range(8)]
        self.lo = (int(self.nc.sbuf_base) + 63) // 64 * 64
        self.hi = int(self.nc.sbuf_top)
        self.ptr = self.lo
        self.n = 0

    def reset(self):
        self.ptr = self.lo

    def sb(self, shape, dt=F32, name=None):
        self.n += 1
        nm = f"s{self.n}_" + (name or "t")
        esz = 2 if dt == BF16 else 4
        nbytes = esz
        for v in shape[1:]:
            nbytes *= v
        nbytes = (nbytes + 63) // 64 * 64
        assert self.ptr + nbytes <= self.hi, f"SBUF arena overflow allocating {nm} {shape}: {self.ptr + nbytes - self.hi} over"
        t = self.nc.alloc_sbuf_tensor_at(nm, list(shape), dt, offset=self.ptr)
        self.ptr += nbytes
        return t, nm

    def dram(self, n, s, d=F32, k="ExternalInput"):
        return self.nc.dram_tensor(n, list(s), d, kind=k).ap()

PV = {}
_c = 0
for _nm, _n in [("cw0", 2), ("cw1", 2), ("cw2", 2), ("cw3", 2), ("cb", 2), ("gbr", 2), ("gbi", 2),
                ("lam", 2), ("mu_r", 2), ("mu_k", 2), ("mu_v", 2), ("mu_g", 2), ("mu_wl", 1),
                ("w0", 2), ("a0", 2), ("k_k", 2), ("k_a", 2), ("r_k", 2), ("lnw", 2), ("lnb", 2),
                ("kdec", 1), ("cdec", 1)]:
    PV[_nm] = _c
    _c += _n
NPV = _c
DV = {"omu_r": 0, "omu_k": 2, "omu_v": 4, "omu_g": 6, "omu_wl": 8, "lrs": 9}
NDV = 11

CST = {"ident": 0, "b64": 128, "b64m": 256, "m1": 384, "m2": 640, "rmask": 768, "qdec": 896, "ones": 1408}
NCST = 1408 + 64


def emit_A(cx, S, d, flags=(1, 1, 1)):
    DO_A, DO_B, DO_C = flags
    rstop = 99
    TB_ = 256
    NTT = TB_ // 128
    NCH = TB_ // 64
    NTB = S // TB_
    nc = cx.nc
    P = cx.P
    sb = cx.sb
    x_d, gw_d, w2p_d, a2p_d, pv_d, nwb_d, cst_d, cos_d, sin_d, ys_d = (d[k] for k in
        ("x", "gw", "w2p", "a2p", "pv", "nwb", "cst", "cosT", "sinT", "ysT"))
    out_toks = []

    def dump(*a, **k):
        return

    banks = cx.banks
    cur_stream = [0]
    bank_sets = {0: [0, 1], 1: [3, 4], 2: [5, 6]}
    bank_ctr = {0: 0, 1: 0, 2: 0}

    def _nb():
        st = cur_stream[0]
        bs = bank_sets[st]
        i = bs[bank_ctr[st] % len(bs)]
        bank_ctr[st] += 1
        return i

    def big():
        i = _nb()
        return banks[i], f"bank{i}"

    def quarter():
        i = _nb()
        return banks[i][:, 0:128], f"bank{i}"

    def half():
        i = _nb()
        return banks[i][:, 0:256], [f"bank{i}"]
    hps = banks[7]

    wA, k_wA = sb([128, 8, NCOLA], BF16, "wA")
    gw, k_gw = sb([128, 2, 2, 256], BF16, "gw")
    w2p, k_w2p = sb([128, 256], BF16, "w2p")
    a2p, k_a2p = sb([128, 256], BF16, "a2p")
    pv, k_pv = sb([128, NPV], F32, "pv")
    dv, k_dv = sb([128, NDV], F32, "dv")
    nwb, k_nwb = sb([128, D], F32, "nwb")
    cst, k_cst = sb([128, NCST], F32, "cst")
    cstb, k_cstb = sb([128, 640 + 128], BF16, "cstb")

    for c0, piece in d["wA_cols"]:
        npc = piece.shape[1]
        pv_ = piece.rearrange("(kc p) n -> p kc n", p=128)
        for kc in range(0, 8, 4):
            P.dma("pool", "dma_start", out=wA[:, kc:kc + 4, c0:c0 + npc], in_=pv_[:, kc:kc + 4, :], writes=[k_wA], par=True)
    P.dma("pool", "dma_start", out=gw[:], in_=gw_d.rearrange("n (cc p) d -> p n cc d", p=128), writes=[k_gw])
    P.dma("pool", "dma_start", out=w2p[:], in_=w2p_d, writes=[k_w2p])
    P.dma("pool", "dma_start", out=a2p[:], in_=a2p_d, writes=[k_a2p])
    P.dma("sp", "dma_start", out=pv[:], in_=pv_d, writes=[k_pv])
    P.dma("sp", "dma_start", out=nwb[:], in_=nwb_d, writes=[k_nwb])
    P.dma("sp", "dma_start", out=cst[:], in_=cst_d, writes=[k_cst])
    P.op("dve", "tensor_copy", out=cstb[:, 0:128], in_=cst[:, 0:128], reads=[k_cst], writes=[k_cstb])
    identb = cstb[:, 0:128]
    ident_f = cst[:, CST["ident"]:CST["ident"] + 128]
    b64 = cst[:, CST["b64"]:CST["b64"] + 128]
    b64m = cst[:, CST["b64m"]:CST["b64m"] + 128]
    m1 = cst[:, CST["m1"]:CST["m1"] + 256]
    m2 = cst[:, CST["m2"]:CST["m2"] + 128]
    rmask = cst[:, CST["rmask"]:CST["rmask"] + 128]
    qdec = cst[:, CST["qdec"]:CST["qdec"] + 512]
    ones64 = cst[:, CST["ones"]:CST["ones"] + 64]

    def pvc(nm, j=0):
        c = PV[nm] + j
        return pv[:, c:c + 1]

    def dvc(nm, j=0):
        c = DV[nm] + j
        return dv[:, c:c + 1]

    P.op("dve", "tensor_scalar", out=dv[:, 0:9], in0=pv[:, PV["mu_r"]:PV["mu_r"] + 9], scalar1=-1.0, scalar2=1.0,
                                          op0=ALU.mult, op1=ALU.add, reads=[k_pv], writes=[k_dv])
    tl, k_tl = sb([128, 2], F32, "tl")
    P.op("act", "activation", out=tl[:], in_=pv[:, PV["lam"]:PV["lam"] + 2], func=AF.Exp, scale=-1.0, reads=[k_pv], writes=[k_tl])
    P.op("act", "activation", out=tl[:], in_=tl[:], func=AF.Ln, bias=1.0, reads=[k_tl], writes=[k_tl])
    P.op("dve", "tensor_scalar", out=dv[:, 9:11], in0=tl[:], scalar1=-8.0, scalar2=None, op0=ALU.mult, reads=[k_tl, k_dv], writes=[k_dv])

    xt = [sb([128, D], F32, f"xt{i}") for i in range(2)]
    junk, k_junk = sb([128, D], BF16, "junk")
    ss = [sb([128, 4], F32, f"ss{i}") for i in range(2)]
    xs = [sb([128, D], BF16, f"xs{i}") for i in range(1)] * 2
    hT = [sb([128, 8, TB_], BF16, f"hT{i}") for i in range(2)]
    hps_b = hps[:].bitcast(BF16)
    pool = [sb([128, TB_], F32, f"pl{i}") for i in range(20)]
    poolB = [sb([128, TB_], F32, f"plB{i}") for i in range(19)]
    poolC = [sb([128, TB_], F32, f"plC{i}") for i in range(11)]

    xa = [sb([128, TB_ + 3], F32, f"xa{i}") for i in range(2)]
    cv = [poolC[0], poolC[1]]
    lr = [poolC[2], poolC[3]]
    li = [poolC[4], poolC[5]]
    la, k_la = poolC[6]
    lt, k_lt = poolC[7]
    lh, k_lh = poolC[8]
    sga = [poolC[9], poolC[10]]
    cvb = [sb([128, TB_], BF16, f"cvb{i}") for i in range(2)]
    hcar = [sb([128, 1], F32, f"hcar{i}") for i in range(2)]
    ysA = [sb([128, TB_], BF16, f"ysA{i}") for i in range(2)]
    for i in range(2):
        P.op("pool", "memset", xa[i][0][:, 0:3], 0.0, writes=[xa[i][1]])
        P.op("pool", "memset", hcar[i][0][:], 0.0, writes=[hcar[i][1]])

    pB = {nm: [sb([128, TB_ + 1], F32, f"pB{nm}{i}") for i in range(2)] for nm in "rkvg"}
    pWL, k_pWL = sb([128, TB_ + 1], F32, "pWL")
    P.op("pool", "memset", pWL[:, 0:1], 0.0, writes=[k_pWL])
    for nm in "rkvg":
        for i in range(2):
            P.op("pool", "memset", pB[nm][i][0][:, 0:1], 0.0, writes=[pB[nm][i][1]])
    sWL, k_sWL = pool[19]
    tnh, k_tnh = sb([128, TB_], BF16, "tnh")
    sWLb, k_sWLb = sb([128, TB_], BF16, "sWLb")
    sS_pt = [{nm: pl[i] for i, nm in enumerate("rkvg")} for pl in (pool, poolB)]
    f_pt = [{nm: pl[4 + i] for i, nm in enumerate(
        ["logw", "cum", "Wt", "iW", "Wp", "kk", "sq", "rn", "kmod", "kka", "iWC", "bon", "sgg", "yT", "aa"])} for pl in (pool, poolB)]
    BD_AR = [sb([128, NCH, 2, 128], BF16, f"BD_AR{i}") for i in range(2)]
    BDn = {nm: [sb([128, NCH, 128], BF16, f"BD_{nm}{i}") for i in range(2)] for nm in ["B", "K", "Bh", "Kh", "V"]}
    for i in range(2):
        P.op("pool", "memset", BD_AR[i][0][:], 0.0, writes=[BD_AR[i][1]])
        for nm in BDn:
            P.op("pool", "memset", BDn[nm][i][0][:], 0.0, writes=[BDn[nm][i][1]])
    S0f = [sb([128, 128], F32, f"S0f{i}") for i in range(2)]
    S0b = [sb([128, 128], BF16, f"S0b{i}") for i in range(2)]
    for i in range(2):
        P.op("pool", "memset", S0f[i][0][:], 0.0, writes=[S0f[i][1]])
        P.op("pool", "memset", S0b[i][0][:], 0.0, writes=[S0b[i][1]])
    NM1a_l = [sb([128, NCH, 256], BF16, f"NM1a{i}") for i in range(2)]
    NM2a_l = [sb([128, NCH, 256], BF16, f"NM2a{i}") for i in range(2)]
    LNP_l = [[sb([128, NCH, 128], BF16, f"LNP{j}_{i}") for i in range(6)] for j in range(2)]
    TKa_l = [sb([128, NCH, 3, 128], BF16, f"TKa{i}") for i in range(2)]
    wc8_l = [sb([128, NCH], F32, f"wc8_{i}") for i in range(2)]
    Zb_l = [[sb([128, 128], BF16, f"Zb{j}_{i}") for i in range(2)] for j in range(2)]
    Ub_l = [[sb([128, 128], BF16, f"Ub{j}_{i}") for i in range(2)] for j in range(2)]
    ysB = [sb([128, TB_], BF16, f"ysB{i}") for i in range(2)]

    qf = [poolC[0], poolC[1]]
    kf = [poolC[2], poolC[3]]
    cosb, k_cos = poolC[4]
    sinb, k_sin = poolC[5]
    rt = [poolC[6 + i] for i in range(4)]
    qT = [sb([128, TB_], BF16, f"qT{i}") for i in range(2)]
    kT = [sb([128, TB_], BF16, f"kT{i}") for i in range(2)]
    qd = [sb([128, TB_], BF16, f"qd{i}") for i in range(2)]
    v_tok = [sb([128, 256], BF16, f"vtok{i}") for i in range(4)]
    sg_tok = [sb([128, 256], F32, f"sgtok{i}") for i in range(4)]
    STm = [sb([128, 128], BF16, f"ST{i}") for i in range(2)]
    kdt = [sb([128, 256], BF16, f"kdt{i}") for i in range(2)]
    stf = [sb([128, 256], F32, f"stf{i}") for i in range(2)]
    stb = [sb([128, 256], BF16, f"stb{i}") for i in range(2)]
    for i in range(2):
        P.op("pool", "memset", stf[i][0][:], 0.0, writes=[stf[i][1]])
        P.op("pool", "memset", stb[i][0][:], 0.0, writes=[stb[i][1]])
    st6 = [sb([128, 6], F32, f"st6_{i}") for i in range(2)]
    mv = [sb([128, 4], F32, f"mv{i}") for i in range(2)]
    yn = [sb([128, 256], F32, f"yn{i}") for i in range(1)] * 2
    ycb = [sb([128, 256], BF16, f"ycb{i}") for i in range(2)]
    ysC = [sb([128, TB_], BF16, f"ysC{i}") for i in range(2)]

    cp_i = [0]

    def copy_eng():
        cp_i[0] += 1
        return "act" if cp_i[0] % 2 else "dve"

    def evac(e, out, in_, reads, writes, scale=None):
        if e == "act":
            if scale is None:
                P.op("act", "activation", out=out, in_=in_, func=AF.Copy, reads=reads, writes=writes)
            else:
                P.op("act", "activation", out=out, in_=in_, func=AF.Copy, scale=scale, reads=reads, writes=writes)
        else:
            if scale is None:
                P.op(e, "tensor_copy", out=out, in_=in_, reads=reads, writes=writes)
            else:
                P.op(e, "tensor_scalar", out=out, in0=in_, scalar1=scale, scalar2=None, op0=ALU.mult, reads=reads, writes=writes)

    def stage1_gen(tbn):
        hTn, k_hTn = hT[tbn % 2]
        for tt in range(NTT):
            xb_, k_x = xt[tt % 2]
            ssb, k_ss = ss[tt % 2]
            xsb, k_xs = xs[tt % 2]
            r0 = tbn * TB_ + tt * 128
            P.dma("sp", "dma_start", out=xb_[:], in_=x_d[r0:r0 + 128, :], writes=[k_x])
            P.op("act", "activation", out=junk[:], in_=xb_[:], func=AF.Square, accum_out=ssb[:, 0:1],
                 reads=[k_x], writes=[k_junk, k_ss])
            yield
            P.op("dve", "tensor_scalar", out=ssb[:, 1:2], in0=ssb[:, 0:1], scalar1=1.0 / D, scalar2=EPS, op0=ALU.mult, op1=ALU.add,
                 reads=[k_ss], writes=[k_ss])
            yield
            P.op("act", "activation", out=ssb[:, 2:3], in_=ssb[:, 1:2], func=AF.Sqrt, reads=[k_ss], writes=[k_ss])
            yield
            P.op("dve", "reciprocal", out=ssb[:, 3:4], in_=ssb[:, 2:3], reads=[k_ss], writes=[k_ss])
            yield
            P.op("dve", "scalar_tensor_tensor", out=xsb[:], in0=xb_[:], scalar=ssb[:, 3:4], in1=nwb[:], op0=ALU.mult, op1=ALU.mult,
                 reads=[k_x, k_ss, k_nwb], writes=[k_xs])
            yield
            for kc in range(8):
                P.op("pe", "transpose", hps_b[:, kc * 128:(kc + 1) * 128], xsb[:, kc * 128:(kc + 1) * 128], identb,
                     reads=[k_xs, k_cstb], writes=["hps"])
            P.op("act", "activation", out=hTn[:, :, tt * 128:(tt + 1) * 128], in_=hps_b.rearrange("p (k t) -> p k t", t=128), func=AF.Copy,
                 reads=["hps"], writes=[k_hTn])
            yield

    for _ in stage1_gen(0):
        pass
    for tb in range(NTB):
        t0 = tb * TB_
        hTt, k_hT = hT[tb % 2]
        def proj_fm(cb):
            bk, k_bk = big()
            for kc in range(8):
                P.op("pe", "matmul", bk[:, 0:TB_], lhsT=wA[:, kc, cb * 128:(cb + 1) * 128], rhs=hTt[:, kc, :],
                                                           start=(kc == 0), stop=(kc == 7),
                     reads=[k_wA, k_hT], writes=[k_bk])
            return bk, k_bk

        if tb == 0:
            dump("hT", hTt[:, 0, :], k_hT)
            dump("xs", xs[1][0][:, 0:TB_], xs[1][1])
            dump("ss", ss[1][0][:, 0:4], ss[1][1], 4)
        def lru_gen():
            for pt in range(2):
                bk, k_bk = proj_fm(pt)
                evac("act", xa[pt][0][:, 3:TB_ + 3], bk[:, 0:TB_], [k_bk], [xa[pt][1]])
                yield
                if tb == 0 and pt == 0:
                    dump("xa", xa[0][0][:, 3:TB_ + 3], xa[0][1])
                bk, k_bk = proj_fm(2 + pt)
                P.op("act", "activation", out=sga[pt][0][:], in_=bk[:, 0:TB_], func=AF.Silu, reads=[k_bk], writes=[sga[pt][1]])
                yield
            for pt in range(2):
                xat, k_xa = xa[pt]
                cvt, k_cv = cv[pt]
                P.op("dve", "tensor_scalar", out=cvt[:], in0=xat[:, 3:TB_ + 3], scalar1=pvc("cw3", pt), scalar2=pvc("cb", pt),
                                                                             op0=ALU.mult, op1=ALU.add, reads=[k_xa, k_pv], writes=[k_cv])
                yield
                for j in range(3):
                    P.op("dve", "scalar_tensor_tensor", out=cvt[:], in0=xat[:, j:j + TB_], scalar=pvc(f"cw{j}", pt), in1=cvt[:],
                                                                                             op0=ALU.mult, op1=ALU.add, reads=[k_xa, k_cv, k_pv], writes=[k_cv])
                    yield
                P.op("pool", "tensor_copy", out=xat[:, 0:3], in_=xat[:, TB_:TB_ + 3], reads=[k_xa], writes=[k_xa])
                yield
                P.op("pool", "tensor_copy", out=cvb[pt][0][:], in_=cvt[:], reads=[k_cv], writes=[cvb[pt][1]])
                yield
            for n in range(2):
                for dblk in range(2):
                    bk, k_bk = big()
                    for cc in range(2):
                        P.op("pe", "matmul", bk[:, 0:TB_], lhsT=gw[:, n, cc, dblk * 128:(dblk + 1) * 128], rhs=cvb[cc][0][:],
                                                                                   start=(cc == 0), stop=(cc == 1),
                             reads=[k_gw, cvb[cc][1]], writes=[k_bk])
                    dst = (lr if n == 0 else li)[dblk]
                    P.op("act", "activation", out=dst[0][:], in_=bk[:, 0:TB_], func=AF.Sigmoid,
                                                                                      bias=pvc("gbr" if n == 0 else "gbi", dblk),
                         reads=[k_bk, k_pv], writes=[dst[1]])
                    yield
            for pt in range(2):
                cvt, k_cv = cv[pt]
                P.op("act", "activation", out=la[:], in_=lr[pt][0][:], func=AF.Exp, scale=dvc("lrs", pt), reads=[lr[pt][1], k_dv], writes=[k_la])
                yield
                P.op("dve", "scalar_tensor_tensor", out=lt[:], in0=la[:], scalar=-1.0, in1=la[:], op0=ALU.mult, op1=ALU.mult, reads=[k_la], writes=[k_lt])
                yield
                P.op("dve", "tensor_scalar", out=lt[:], in0=lt[:], scalar1=1.0, scalar2=0.0, op0=ALU.add, op1=ALU.max, reads=[k_lt], writes=[k_lt])
                yield
                P.op("act", "activation", out=lt[:], in_=lt[:], func=AF.Sqrt, reads=[k_lt], writes=[k_lt])
                yield
                if tb == 0:
                    P.op("dve", "memset", lt[:, 0:1], 1.0, reads=[k_lt], writes=[k_lt])
                    yield
                P.op("dve", "tensor_tensor", out=lt[:], in0=lt[:], in1=li[pt][0][:], op=ALU.mult, reads=[k_lt, li[pt][1]], writes=[k_lt])
                yield
                P.op("dve", "tensor_tensor", out=lt[:], in0=lt[:], in1=cvt[:], op=ALU.mult, reads=[k_lt, k_cv], writes=[k_lt])
                yield
                P.op("dve", "tensor_tensor_scan", out=lh[:], data0=la[:], data1=lt[:], initial=hcar[pt][0][:, 0:1], op0=ALU.mult, op1=ALU.add,
                     reads=[k_la, k_lt, hcar[pt][1]], writes=[k_lh])
                yield
                P.op("dve", "tensor_copy", out=hcar[pt][0][:], in_=lh[:, TB_ - 1:TB_], reads=[k_lh], writes=[hcar[pt][1]])
                yield
                if tb == 0 and pt == 0:
                    dump("cv", cv[0][0][:], cv[0][1]); dump("lr", lr[0][0][:], lr[0][1]); dump("li", li[0][0][:], li[0][1])
                    dump("la", la[:], k_la); dump("lt", lt[:], k_lt); dump("lh", lh[:], k_lh); dump("sga", sga[0][0][:], sga[0][1])
                P.op("dve", "tensor_tensor", out=ysA[pt][0][:], in0=lh[:], in1=sga[pt][0][:], op=ALU.mult, reads=[k_lh, sga[pt][1]], writes=[ysA[pt][1]])
                yield
                out_toks.append(P.dma("sp", "dma_start", out=ys_d[0, pt * 128:(pt + 1) * 128, t0:t0 + TB_], in_=ysA[pt][0][:],
                                      reads=[ysA[pt][1]]))

            return
            yield

        ret_done = [True]

        def ret_gen():
            if DO_C:
                for pt in range(2):
                    bk, k_bk = proj_fm(13 + pt)
                    evac("act", qf[pt][0][:], bk[:, 0:TB_], [k_bk], [qf[pt][1]], scale=1.0 / 16.0)
                    yield
                    bk, k_bk = proj_fm(15 + pt)
                    evac("dve", kf[pt][0][:], bk[:, 0:TB_], [k_bk], [kf[pt][1]])
                    yield
                for tt in range(NTT):
                    bk, k_bk = big()
                    for kc in range(8):
                        P.op("pe", "matmul", bk[:, 0:512], lhsT=hTt[:, kc, tt * 128:(tt + 1) * 128], rhs=wA[:, kc, 2176:2688],
                                                                          start=(kc == 0), stop=(kc == 7),
                             reads=[k_wA, k_hT], writes=[k_bk])
                    evac("dve", v_tok[tt][0][:], bk[:, 0:256], [k_bk], [v_tok[tt][1]])
                    yield
                    P.op("act", "activation", out=sg_tok[tt][0][:], in_=bk[:, 256:512], func=AF.Silu,
                         reads=[k_bk], writes=[sg_tok[tt][1]])
                    yield

            if DO_C and rstop >= 2:
                P.dma("sp", "dma_start", out=cosb[:], in_=cos_d[:, t0:t0 + TB_], writes=[k_cos])
                P.dma("sp", "dma_start", out=sinb[:], in_=sin_d[:, t0:t0 + TB_], writes=[k_sin])
                for src, dst in ((qf, qT), (kf, kT)):
                    s1, k1 = src[0]
                    s2, k2 = src[1]
                    P.op("dve", "tensor_tensor", out=rt[0][0][:], in0=s1[:], in1=cosb[:], op=ALU.mult, reads=[k1, k_cos], writes=[rt[0][1]])
                    yield
                    P.op("pool", "tensor_tensor", out=rt[1][0][:], in0=s2[:], in1=sinb[:], op=ALU.mult, reads=[k2, k_sin], writes=[rt[1][1]])
                    yield
                    P.op("dve", "tensor_tensor", out=dst[0][0][:], in0=rt[0][0][:], in1=rt[1][0][:], op=ALU.subtract, reads=[rt[0][1], rt[1][1]], writes=[dst[0][1]])
                    yield
                    P.op("pool", "tensor_tensor", out=rt[2][0][:], in0=s1[:], in1=sinb[:], op=ALU.mult, reads=[k1, k_sin], writes=[rt[2][1]])
                    yield
                    P.op("dve", "tensor_tensor", out=rt[3][0][:], in0=s2[:], in1=cosb[:], op=ALU.mult, reads=[k2, k_cos], writes=[rt[3][1]])
                    yield
                    P.op("dve", "tensor_tensor", out=dst[1][0][:], in0=rt[2][0][:], in1=rt[3][0][:], op=ALU.add, reads=[rt[2][1], rt[3][1]], writes=[dst[1][1]])
                    yield
                for pt in range(2):
                    P.op("pool", "tensor_tensor", out=qd[pt][0][:], in0=qT[pt][0][:], in1=qdec[:, 0:TB_], op=ALU.mult, reads=[qT[pt][1], k_cst], writes=[qd[pt][1]])
                    yield
                for c in range(NTT):
                    cs = slice(c * 128, (c + 1) * 128)
                    stm, k_stm = STm[c % 2]
                    kd, k_kd = kdt[c % 2]
                    ps, k_ps = quarter()
                    for pt in range(2):
                        P.op("pe", "matmul", ps, lhsT=kT[pt][0][:, cs], rhs=qT[pt][0][:, cs], start=(pt == 0), stop=(pt == 1),
                             reads=[kT[pt][1], qT[pt][1]], writes=[k_ps])
                    P.op("dve", "tensor_tensor", out=stm[:], in0=ps, in1=rmask, op=ALU.mult, reads=[k_ps, k_cst], writes=[k_stm])
                    yield
                    if rstop < 5:
                        continue
                    psq, k_psq = quarter()
                    psq_b = psq.bitcast(BF16)
                    for pt in range(2):
                        P.op("pe", "transpose", psq_b[:, pt * 128:(pt + 1) * 128], kT[pt][0][:, cs], identb,
                             reads=[kT[pt][1], k_cstb], writes=[k_psq])
                    P.op("dve", "tensor_scalar", out=kd[:], in0=psq_b, scalar1=pvc("kdec"), scalar2=None, op0=ALU.mult, reads=[k_psq, k_pv], writes=[k_kd])
                    yield
                    if rstop < 6:
                        continue
                    pso, k_pso = banks[2][:, 0:256], ["bank2"]
                    P.op("pe", "matmul", pso, lhsT=stm[:], rhs=v_tok[c][0][:], start=True, stop=False, reads=[k_stm, v_tok[c][1]], writes=k_pso)
                    for pt in range(2):
                        P.op("pe", "matmul", pso, lhsT=qd[pt][0][:, cs], rhs=stb[pt][0][:], start=False, stop=(pt == 1),
                             reads=[qd[pt][1], stb[pt][1]], writes=k_pso)
                    for dc in range(2 if rstop >= 7 else 0):
                        pss, k_pss = half()
                        P.op("pe", "matmul", pss, lhsT=kd[:, dc * 128:(dc + 1) * 128], rhs=v_tok[c][0][:], start=True, stop=True,
                             reads=[k_kd, v_tok[c][1]], writes=k_pss)
                        P.op("dve", "scalar_tensor_tensor", out=stf[dc][0][:], in0=stf[dc][0][:], scalar=pvc("cdec"), in1=pss, op0=ALU.mult, op1=ALU.add,
                             reads=[stf[dc][1], k_pv] + k_pss, writes=[stf[dc][1]])
                        yield
                        P.op("act", "activation", out=stb[dc][0][:], in_=stf[dc][0][:], func=AF.Copy, reads=[stf[dc][1]], writes=[stb[dc][1]])
                        yield
                    if rstop < 8:
                        continue
                    s6, k_s6 = st6[c % 2]
                    mvt, k_mv = mv[c % 2]
                    ynt, k_yn = yn[c % 2]
                    ycbt, k_ycb = ycb[c % 2]
                    P.op("dve", "tensor_reduce", out=mvt[:, 0:1], in_=pso, axis=mybir.AxisListType.X, op=ALU.add, reads=k_pso, writes=[k_mv])
                    yield
                    P.op("dve", "tensor_scalar", out=mvt[:, 1:2], in0=mvt[:, 0:1], scalar1=-1.0 / 256.0, scalar2=None, op0=ALU.mult, reads=[k_mv], writes=[k_mv])
                    yield
                    P.op("act", "activation", out=ynt[:], in_=pso, func=AF.Identity, bias=mvt[:, 1:2], reads=k_pso + [k_mv], writes=[k_yn])
                    yield
                    P.op("act", "activation", out=junk[:, 0:256], in_=ynt[:], func=AF.Square, accum_out=s6[:, 0:1], reads=[k_yn], writes=[k_junk, k_s6])
                    yield
                    P.op("dve", "tensor_scalar", out=s6[:, 1:2], in0=s6[:, 0:1], scalar1=1.0 / 256.0, scalar2=1e-5, op0=ALU.mult, op1=ALU.add, reads=[k_s6], writes=[k_s6])
                    yield
                    P.op("act", "activation", out=s6[:, 2:3], in_=s6[:, 1:2], func=AF.Sqrt, reads=[k_s6], writes=[k_s6])
                    yield
                    P.op("dve", "reciprocal", out=s6[:, 3:4], in_=s6[:, 2:3], reads=[k_s6], writes=[k_s6])
                    yield
                    P.op("dve", "scalar_tensor_tensor", out=ycbt[:], in0=ynt[:], scalar=s6[:, 3:4], in1=sg_tok[c][0][:], op0=ALU.mult, op1=ALU.mult,
                         reads=[k_yn, k_s6, sg_tok[c][1]], writes=[k_ycb])
                    yield
                    if rstop < 9:
                        continue
                    pst, k_pst = quarter()
                    pst_b = pst.bitcast(BF16)
                    for pt in range(2):
                        P.op("pe", "transpose", pst_b[:, pt * 128:(pt + 1) * 128], ycbt[:, pt * 128:(pt + 1) * 128], identb,
                             reads=[k_ycb, k_cstb], writes=[k_pst])
                    for pt in range(2):
                        evac(copy_eng(), ysC[pt][0][:, cs], pst_b[:, pt * 128:(pt + 1) * 128], [k_pst], [ysC[pt][1]])
                        yield
                for pt in range(2 if rstop >= 10 else 0):
                    out_toks.append(P.dma("sp", "dma_start", out=ys_d[2, pt * 128:(pt + 1) * 128, t0:t0 + TB_], in_=ysC[pt][0][:],
                                          reads=[ysC[pt][1]]))
            return
            yield

        if DO_B:
            bk, k_bk = proj_fm(12)
            evac(copy_eng(), pWL[:, 1:TB_ + 1], bk[:, 0:TB_], [k_bk], [k_pWL])
            def shift(buf, k_buf, dst, k_dst, mu, omu):
                P.op("act", "activation", out=dst[:], in_=buf[:, 0:TB_], func=AF.Copy, scale=mu,
                     reads=[k_buf, k_pv], writes=[k_dst])
                P.op("dve", "scalar_tensor_tensor", out=dst[:], in0=buf[:, 1:TB_ + 1], scalar=omu, in1=dst[:], op0=ALU.mult, op1=ALU.add,
                     reads=[k_buf, k_dst, k_dv], writes=[k_dst])
                P.op("pool", "tensor_copy", out=buf[:, 0:1], in_=buf[:, TB_:TB_ + 1], reads=[k_buf], writes=[k_buf])

            shift(pWL, k_pWL, sWL, k_sWL, pvc("mu_wl"), dvc("omu_wl"))
            P.op("act", "activation", out=tnh[:], in_=sWL[:], func=AF.Tanh, reads=[k_sWL], writes=[k_tnh])
            P.op("pool", "tensor_copy", out=sWLb[:], in_=sWL[:], reads=[k_sWL], writes=[k_sWLb])

        def rwkv_gen(pt):
            sS = sS_pt[pt]
            f = f_pt[pt]
            NM1a = NM1a_l[pt]; NM2a = NM2a_l[pt]; LNP = LNP_l[pt]; TKa = TKa_l[pt]; Zb = Zb_l[pt]; Ub = Ub_l[pt]
            wc8, k_wc8 = wc8_l[pt]
            for j, nm in enumerate("rkvg"):
                bk, k_bk = proj_fm(4 + 2 * j + pt)
                evac(copy_eng(), pB[nm][pt][0][:, 1:TB_ + 1], bk[:, 0:TB_], [k_bk], [pB[nm][pt][1]])
                yield
            F_ = lambda nm: f[nm][0]
            K_ = lambda nm: f[nm][1]
            for nm in "rkvg":
                shift(pB[nm][pt][0], pB[nm][pt][1], sS[nm][0], sS[nm][1], pvc("mu_" + nm, pt), dvc("omu_" + nm, pt))
            s_r, s_k, s_v, s_g = (sS[nm][0] for nm in "rkvg")
            ksr, ksk, ksv, ksg = (sS[nm][1] for nm in "rkvg")
            bk, k_bk = big()
            P.op("pe", "matmul", bk[:, 0:TB_], lhsT=w2p[:, pt * 128:(pt + 1) * 128], rhs=tnh[:], start=True, stop=True,
                 reads=[k_w2p, k_tnh], writes=[k_bk])
            P.op("act", "activation", out=F_("logw")[:], in_=bk[:, 0:TB_], func=AF.Sigmoid, bias=pvc("w0", pt), reads=[k_bk, k_pv], writes=[K_("logw")])
            yield
            bk, k_bk = big()
            P.op("pe", "matmul", bk[:, 0:TB_], lhsT=a2p[:, pt * 128:(pt + 1) * 128], rhs=sWLb[:], start=True, stop=True,
                 reads=[k_a2p, k_sWLb], writes=[k_bk])
            P.op("act", "activation", out=F_("aa")[:], in_=bk[:, 0:TB_], func=AF.Sigmoid, bias=pvc("a0", pt), reads=[k_bk, k_pv], writes=[K_("aa")])
            yield
            P.op("act", "activation", out=F_("logw")[:], in_=F_("logw")[:], func=AF.Copy, scale=-0.6065306597126334,
                 reads=[K_("logw")], writes=[K_("logw")])
            yield
            for ch in range(NCH):
                P.op("dve", "tensor_tensor_scan", out=F_("cum")[:, ch * 64:(ch + 1) * 64], data0=ones64, data1=F_("logw")[:, ch * 64:(ch + 1) * 64],
                                                                 initial=0.0, op0=ALU.mult, op1=ALU.add, reads=[K_("logw"), k_cst], writes=[K_("cum")])
                yield
            P.op("act", "activation", out=F_("Wt")[:], in_=F_("cum")[:], func=AF.Exp, reads=[K_("cum")], writes=[K_("Wt")])
            yield
            P.op("act", "activation", out=F_("iW")[:], in_=F_("cum")[:], func=AF.Exp, scale=-1.0, reads=[K_("cum")], writes=[K_("iW")])
            yield
            P.op("pool", "tensor_tensor", out=F_("Wp")[:], in0=F_("cum")[:], in1=F_("logw")[:], op=ALU.subtract, reads=[K_("cum"), K_("logw")], writes=[K_("Wp")])
            yield
            P.op("act", "activation", out=F_("Wp")[:], in_=F_("Wp")[:], func=AF.Exp, reads=[K_("Wp")], writes=[K_("Wp")])
            yield
            P.op("act", "activation", out=F_("kk")[:], in_=s_k[:], func=AF.Copy, scale=pvc("k_k", pt),
                 reads=[ksk, k_pv], writes=[K_("kk")])
            yield
            P.op("pool", "tensor_tensor", out=F_("sq")[:], in0=F_("kk")[:], in1=F_("kk")[:], op=ALU.mult, reads=[K_("kk")], writes=[K_("sq")])
            yield
            bk, k_bk = big()
            P.op("pe", "matmul", bk[:, 0:TB_], lhsT=b64, rhs=F_("sq")[:], start=True, stop=True, reads=[k_cst, K_("sq")], writes=[k_bk])
            P.op("act", "activation", out=F_("rn")[:], in_=bk[:, 0:TB_], func=AF.Sqrt, reads=[k_bk], writes=[K_("rn")])
            yield
            P.op("dve", "tensor_scalar", out=F_("rn")[:], in0=F_("rn")[:], scalar1=1e-12, scalar2=None, op0=ALU.max, reads=[K_("rn")], writes=[K_("rn")])
            yield
            P.op("dve", "reciprocal", out=F_("rn")[:], in_=F_("rn")[:], reads=[K_("rn")], writes=[K_("rn")])
            yield
            P.op("dve", "tensor_tensor", out=F_("kk")[:], in0=F_("kk")[:], in1=F_("rn")[:], op=ALU.mult, reads=[K_("kk"), K_("rn")], writes=[K_("kk")])
            yield
            P.op("pool", "tensor_scalar", out=F_("kmod")[:], in0=F_("aa")[:], scalar1=-1.0, scalar2=pvc("k_a", pt), op0=ALU.add, op1=ALU.mult,
                 reads=[K_("aa"), k_pv], writes=[K_("kmod")])
            yield
            P.op("dve", "scalar_tensor_tensor", out=F_("kmod")[:], in0=F_("kmod")[:], scalar=1.0, in1=s_k[:], op0=ALU.add, op1=ALU.mult,
                 reads=[K_("kmod"), ksk], writes=[K_("kmod")])
            yield
            P.op("pool", "tensor_tensor", out=F_("kka")[:], in0=F_("kk")[:], in1=F_("aa")[:], op=ALU.mult, reads=[K_("kk"), K_("aa")], writes=[K_("kka")])
            yield
            P.op("dve", "tensor_tensor", out=F_("iWC")[:].rearrange("p (c t) -> p c t", t=64), in0=F_("iW")[:].rearrange("p (c t) -> p c t", t=64),
                                                  in1=F_("Wt")[:].rearrange("p (c t) -> p c t", t=64)[:, :, 63:64].broadcast_to([128, NCH, 64]), op=ALU.mult,
                 reads=[K_("iW"), K_("Wt")], writes=[K_("iWC")])
            yield
            bdar, k_bdar = BD_AR[pt]
            v3 = lambda t, hh: t[hh * 64:(hh + 1) * 64, :].rearrange("p (c t) -> p c t", t=64)
            for hh in range(2):
                hs_ = slice(hh * 64, (hh + 1) * 64)
                P.op("dve", "scalar_tensor_tensor", out=bdar[hs_, :, 0, hh * 64:(hh + 1) * 64], in0=v3(F_("kk"), hh), scalar=-1.0,
                                                                             in1=v3(F_("Wp"), hh), op0=ALU.mult, op1=ALU.mult,
                     reads=[K_("kk"), K_("Wp")], writes=[k_bdar])
                yield
                P.op("pool", "tensor_tensor", out=bdar[hs_, :, 1, hh * 64:(hh + 1) * 64], in0=v3(s_r, hh), in1=v3(F_("Wt"), hh), op=ALU.mult,
                     reads=[ksr, K_("Wt")], writes=[k_bdar])
                yield
                for nm, a_, b_, eng in [("B", "kka", "iW", "dve"), ("K", "kmod", "iW", "pool"), ("Bh", "kka", "iWC", "dve"), ("Kh", "kmod", "iWC", "pool")]:
                    P.op(eng, "tensor_tensor", out=BDn[nm][pt][0][hs_, :, hh * 64:(hh + 1) * 64], in0=v3(F_(a_), hh),
                                                                                            in1=v3(F_(b_), hh), op=ALU.mult,
                         reads=[K_(a_), K_(b_)], writes=[BDn[nm][pt][1]])
                P.op("act", "activation", out=BDn["V"][pt][0][hs_, :, hh * 64:(hh + 1) * 64], in_=v3(s_v, hh), func=AF.Copy,
                     reads=[ksv], writes=[BDn["V"][pt][1]])
                yield
            P.op("dve", "scalar_tensor_tensor", out=F_("sq")[:], in0=s_r[:], scalar=pvc("r_k", pt), in1=F_("kmod")[:], op0=ALU.mult, op1=ALU.mult,
                 reads=[ksr, K_("kmod"), k_pv], writes=[K_("sq")])
            yield
            bk, k_bk = big()
            P.op("pe", "matmul", bk[:, 0:TB_], lhsT=b64, rhs=F_("sq")[:], start=True, stop=True, reads=[k_cst, K_("sq")], writes=[k_bk])
            P.op("dve", "tensor_tensor", out=F_("bon")[:], in0=bk[:, 0:TB_], in1=s_v[:], op=ALU.mult, reads=[k_bk, ksv], writes=[K_("bon")])
            yield
            P.op("act", "activation", out=F_("sgg")[:], in_=s_g[:], func=AF.Silu, reads=[ksg], writes=[K_("sgg")])
            yield

            s0f, k_s0f = S0f[pt]
            s0b, k_s0b = S0b[pt]
            bB, k_bB = BDn["B"][pt]
            bK, k_bK = BDn["K"][pt]
            bBh, k_bBh = BDn["Bh"][pt]
            bKh, k_bKh = BDn["Kh"][pt]
            bV, k_bV = BDn["V"][pt]
            P.op("dve", "tensor_copy", out=wc8[:], in_=F_("Wt")[:].rearrange("p (c t) -> p c t", t=64)[:, :, 63], reads=[K_("Wt")], writes=[k_wc8])
            yield
            nm1a, k_nm1a = NM1a
            nm2a, k_nm2a = NM2a
            tka, k_tka = TKa
            m1b = m1.unsqueeze(1).broadcast_to([128, 2, 256])
            m2b = m2.unsqueeze(1).broadcast_to([128, 4, 128])
            idb4 = ident_f.unsqueeze(1).broadcast_to([128, NCH, 128])
            for lhs_t, k_lhs, dst, k_dst in ((bB, k_bB, nm1a, k_nm1a), (bK, k_bK, nm2a, k_nm2a)):
                for c2 in range(NCH // 2):
                    bk, k_bk = big()
                    for u in range(2):
                        ch = 2 * c2 + u
                        P.op("pe", "matmul", bk[:, u * 256:(u + 1) * 256], lhsT=lhs_t[:, ch, :], rhs=bdar[:, ch, :, :].rearrange("p a b -> p (a b)"),
                             start=True, stop=True, reads=[k_lhs, k_bdar], writes=[k_bk])
                    P.op("dve", "tensor_tensor", out=dst[:, 2 * c2:2 * c2 + 2, :], in0=bk[:].rearrange("p (a b) -> p a b", b=256), in1=m1b, op=ALU.mult,
                         reads=[k_bk, k_cst], writes=[k_dst])
                    yield
            lcur, k_lcur = LNP[0]
            for c4 in range(NCH // 4):
                bk, k_bk = big()
                for u in range(4):
                    ch = 4 * c4 + u
                    P.op("pe", "matmul", bk[:, u * 128:(u + 1) * 128], lhsT=bdar[:, ch, 0, :], rhs=bB[:, ch, :], start=True, stop=True,
                         reads=[k_bdar, k_bB], writes=[k_bk])
                P.op("dve", "tensor_tensor", out=lcur[:, 4 * c4:4 * c4 + 4, :], in0=bk[:].rearrange("p (a b) -> p a b", b=128), in1=m2b, op=ALU.mult,
                     reads=[k_bk, k_cst], writes=[k_lcur])
                yield
            pcur, k_pcur = LNP[4]
            P.op("pool", "tensor_tensor", out=pcur[:], in0=nm1a[:, :, 0:128], in1=idb4, op=ALU.add, reads=[k_nm1a, k_cst], writes=[k_pcur])
            yield
            ncur_f = lambda ch: nm1a[:, ch, 0:128]
            k_ncur = k_nm1a
            for j in range(1, 6):
                lnew, k_lnew = LNP[j % 2]
                nnew, k_nnew = LNP[2 + j % 2]
                pnew, k_pnew = LNP[4 + j % 2]
                lb = []
                for c4 in range(NCH // 4):
                    bk, k_bk = big()
                    for u in range(4):
                        ch = 4 * c4 + u
                        P.op("pe", "matmul", bk[:, u * 128:(u + 1) * 128], lhsT=ncur_f(ch), rhs=lcur[:, ch, :], start=True, stop=True,
                             reads=[k_ncur, k_lcur], writes=[k_bk])
                    lb.append((bk, k_bk))
                nb = []
                if j <= 4:
                    for c4 in range(NCH // 4):
                        bk, k_bk = big()
                        for u in range(4):
                            ch = 4 * c4 + u
                            P.op("pe", "matmul", bk[:, u * 128:(u + 1) * 128], lhsT=lcur[:, ch, :], rhs=ncur_f(ch), start=True, stop=True,
                                 reads=[k_ncur, k_lcur], writes=[k_bk])
                        nb.append((bk, k_bk))
                for c4, (bk, k_bk) in enumerate(lb):
                    evac("act", lnew[:, 4 * c4:4 * c4 + 4, :], bk[:].rearrange("p (a b) -> p a b", b=128), [k_bk], [k_lnew])
                    yield
                for c4, (bk, k_bk) in enumerate(nb):
                    evac("dve", nnew[:, 4 * c4:4 * c4 + 4, :], bk[:].rearrange("p (a b) -> p a b", b=128), [k_bk], [k_nnew])
                    yield
                for c4 in range(NCH // 4):
                    bk, k_bk = big()
                    for u in range(4):
                        ch = 4 * c4 + u
                        P.op("pe", "matmul", bk[:, u * 128:(u + 1) * 128], lhsT=lnew[:, ch, :], rhs=pcur[:, ch, :], start=True, stop=True,
                             reads=[k_lnew, k_pcur], writes=[k_bk])
                    P.op("dve", "tensor_tensor", out=pnew[:, 4 * c4:4 * c4 + 4, :], in0=bk[:].rearrange("p (a b) -> p a b", b=128),
                         in1=pcur[:, 4 * c4:4 * c4 + 4, :], op=ALU.add, reads=[k_bk, k_pcur], writes=[k_pnew])
                    yield
                lcur, k_lcur = lnew, k_lnew
                if j <= 4:
                    ncur_f = (lambda nn: (lambda ch: nn[:, ch, :]))(nnew)
                    k_ncur = k_nnew
                pcur, k_pcur = pnew, k_pnew
            for c2 in range(NCH // 2):
                bk, k_bk = big()
                bkb = bk[:].bitcast(BF16)
                for u in range(2):
                    ch = 2 * c2 + u
                    for i3, (src, ks) in enumerate([(bV, k_bV), (bBh, k_bBh), (bKh, k_bKh)]):
                        P.op("pe", "transpose", bkb[:, (u * 3 + i3) * 128:(u * 3 + i3 + 1) * 128], src[:, ch, :], identb,
                             reads=[ks, k_cstb], writes=[k_bk])
                evac("act", tka[:, 2 * c2:2 * c2 + 2, :, :].rearrange("p a b c -> p (a b c)"), bkb[:, 0:768], [k_bk], [k_tka])
                yield
            def chain_gen():
                for ch in range(NCH):
                    zb, k_zb = Zb[ch % 2]
                    ub, k_ub = Ub[ch % 2]
                    ps, k_ps = quarter()
                    P.op("pe", "matmul", ps, lhsT=bdar[:, ch, 0, :], rhs=s0b[:], start=True, stop=False, reads=[k_bdar, k_s0b], writes=[k_ps])
                    P.op("pe", "matmul", ps, lhsT=nm2a[:, ch, 0:128], rhs=tka[:, ch, 0, :], start=False, stop=True, reads=[k_nm2a, k_tka], writes=[k_ps])
                    evac("dve", zb[:], ps, [k_ps], [k_zb])
                    yield
                    ps, k_ps = quarter()
                    P.op("pe", "matmul", ps, lhsT=pcur[:, ch, :], rhs=zb[:], start=True, stop=True, reads=[k_pcur, k_zb], writes=[k_ps])
                    evac("act", ub[:], ps, [k_ps], [k_ub])
                    yield
                    ps, k_ps = quarter()
                    P.op("pe", "matmul", ps, lhsT=tka[:, ch, 1, :], rhs=ub[:], start=True, stop=False, reads=[k_tka, k_ub], writes=[k_ps])
                    P.op("pe", "matmul", ps, lhsT=tka[:, ch, 2, :], rhs=tka[:, ch, 0, :], start=False, stop=True, reads=[k_tka], writes=[k_ps])
                    yq, k_ybk = quarter()
                    P.op("pe", "matmul", yq, lhsT=s0b[:], rhs=bdar[:, ch, 1, :], start=True, stop=False, reads=[k_bdar, k_s0b], writes=[k_ybk])
                    P.op("pe", "matmul", yq, lhsT=ub[:], rhs=nm1a[:, ch, 128:256], start=False, stop=False, reads=[k_ub, k_nm1a], writes=[k_ybk])
                    P.op("pe", "matmul", yq, lhsT=tka[:, ch, 0, :], rhs=nm2a[:, ch, 128:256], start=False, stop=True, reads=[k_tka, k_nm2a], writes=[k_ybk])
                    P.op("dve", "scalar_tensor_tensor", out=s0f[:], in0=s0f[:], scalar=wc8[:, ch:ch + 1], in1=ps, op0=ALU.mult, op1=ALU.add,
                         reads=[k_s0f, k_wc8, k_ps], writes=[k_s0f])
                    yield
                    P.op("act", "activation", out=s0b[:], in_=s0f[:], func=AF.Copy, reads=[k_s0f], writes=[k_s0b])
                    yield
                    P.op("dve", "tensor_copy", out=F_("yT")[0:64, ch * 64:(ch + 1) * 64], in_=yq[0:64, 0:64], reads=[k_ybk], writes=[K_("yT")])
                    yield
                    P.op("act", "activation", out=F_("yT")[64:128, ch * 64:(ch + 1) * 64], in_=yq[64:128, 64:128], func=AF.Copy, reads=[k_ybk], writes=[K_("yT")])
                    yield
            yield from chain_gen()
            bk, k_bk = big()
            P.op("pe", "matmul", bk[:, 0:TB_], lhsT=b64m, rhs=F_("yT")[:], start=True, stop=True, reads=[k_cst, K_("yT")], writes=[k_bk])
            P.op("dve", "tensor_tensor", out=F_("yT")[:], in0=F_("yT")[:], in1=bk[:, 0:TB_], op=ALU.subtract, reads=[k_bk, K_("yT")], writes=[K_("yT")])
            yield
            P.op("pool", "tensor_tensor", out=F_("sq")[:], in0=F_("yT")[:], in1=F_("yT")[:], op=ALU.mult, reads=[K_("yT")], writes=[K_("sq")])
            yield
            bk, k_bk = big()
            P.op("pe", "matmul", bk[:, 0:TB_], lhsT=b64m, rhs=F_("sq")[:], start=True, stop=True, reads=[k_cst, K_("sq")], writes=[k_bk])
            P.op("dve", "tensor_scalar", out=F_("rn")[:], in0=bk[:, 0:TB_], scalar1=64e-5, scalar2=None, op0=ALU.add, reads=[k_bk], writes=[K_("rn")])
            yield
            P.op("act", "activation", out=F_("rn")[:], in_=F_("rn")[:], func=AF.Sqrt, reads=[K_("rn")], writes=[K_("rn")])
            yield
            P.op("dve", "reciprocal", out=F_("rn")[:], in_=F_("rn")[:], reads=[K_("rn")], writes=[K_("rn")])
            yield
            P.op("dve", "tensor_tensor", out=F_("yT")[:], in0=F_("yT")[:], in1=F_("rn")[:], op=ALU.mult, reads=[K_("yT"), K_("rn")], writes=[K_("yT")])
            yield
            P.op("dve", "tensor_scalar", out=F_("yT")[:], in0=F_("yT")[:], scalar1=pvc("lnw", pt), scalar2=pvc("lnb", pt), op0=ALU.mult, op1=ALU.add,
                 reads=[K_("yT"), k_pv], writes=[K_("yT")])
            yield
            P.op("dve", "tensor_tensor", out=F_("yT")[:], in0=F_("yT")[:], in1=F_("bon")[:], op=ALU.add, reads=[K_("yT"), K_("bon")], writes=[K_("yT")])
            yield
            P.op("dve", "tensor_tensor", out=ysB[pt][0][:], in0=F_("yT")[:], in1=F_("sgg")[:], op=ALU.mult, reads=[K_("yT"), K_("sgg")], writes=[ysB[pt][1]])
            yield
            out_toks.append(P.dma("sp", "dma_start", out=ys_d[1, pt * 128:(pt + 1) * 128, t0:t0 + TB_], in_=ysB[pt][0][:],
                                  reads=[ysB[pt][1]]))
            return
            yield

        gens = []
        gid = {}
        if DO_A or DO_C:
            def s1_gen():
                if DO_A:
                    yield from lru_gen()
                if DO_C:
                    yield from ret_gen()
            gens.append(s1_gen())
            gid[id(gens[-1])] = 0
        if DO_B:
            gens.append(rwkv_gen(0))
            gid[id(gens[-1])] = 1
            gens.append(rwkv_gen(1))
            gid[id(gens[-1])] = 2
        if tb + 1 < NTB:
            gens.append(stage1_gen(tb + 1))
            gid[id(gens[-1])] = 0
        while gens:
            for g_ in list(gens):
                cur_stream[0] = gid[id(g_)]
                try:
                    next(g_)
                except StopIteration:
                    gens.remove(g_)
        cur_stream[0] = 0

    return out_toks


def _consts(g, S):
    cst = np.zeros((128, NCST), np.float32)
    cst[:, CST["ident"]:CST["ident"] + 128] = np.eye(128, dtype=np.float32)
    blk = np.kron(np.eye(2, dtype=np.float32), np.ones((64, 64), np.float32))
    cst[:, CST["b64"]:CST["b64"] + 128] = blk
    cst[:, CST["b64m"]:CST["b64m"] + 128] = blk / 64.0
    su = np.triu(np.ones((64, 64), np.float32), 1)
    ui = np.triu(np.ones((64, 64), np.float32), 0)
    cst[:, CST["m1"]:CST["m1"] + 128] = np.kron(np.eye(2, dtype=np.float32), su)
    cst[:, CST["m1"] + 128:CST["m1"] + 256] = np.kron(np.eye(2, dtype=np.float32), ui)
    cst[:, CST["m2"]:CST["m2"] + 128] = np.kron(np.eye(2, dtype=np.float32), su.T)
    gam = np.float32(1.0) - np.exp2(np.float32(-5.0 - g)).astype(np.float32)
    lg = np.log1p(-np.exp2(np.float32(-5.0 - g))).astype(np.float64)
    idx = np.arange(128)
    rel = idx[None, :] - idx[:, None]
    cst[:, CST["rmask"]:CST["rmask"] + 128] = np.where(rel >= 0, np.exp(lg * np.maximum(rel, 0)), 0.0).astype(np.float32)
    qrow = np.exp(lg * (idx + 1.0)).astype(np.float32)
    cst[:, CST["qdec"]:CST["qdec"] + 512] = np.tile(qrow[None, :], (128, 4))
    cst[:, CST["ones"]:CST["ones"] + 64] = 1.0
    kdec = np.exp(lg * (127.0 - idx)).astype(np.float32)
    cdec = np.float32(np.exp(lg * 128.0))
    half = 128
    inv_freq = (1.0 / (10000.0 ** np.linspace(0.0, 1.0, half, dtype=np.float32))).astype(np.float32)
    ang = np.arange(S, dtype=np.float32)[None, :] * inv_freq[:, None]
    return cst, kdec, cdec, np.cos(ang).astype(np.float32), np.sin(ang).astype(np.float32)


def _prep_A(inp, l, b, g, S, light=False):
    w_in = inp["w_in"][l]
    sl = lambda off: np.arange(off + g * 256, off + (g + 1) * 256)
    cols = np.concatenate([sl(0), sl(OFF_GA), sl(OFF_B), sl(OFF_B + 1024), sl(OFF_B + 2048), sl(OFF_B + 3072),
                           np.arange(OFF_B + 4096, OFF_B + 4224), sl(OFF_C), sl(OFF_C + 1024), sl(OFF_C + 2048), sl(OFF_C + 3072)])
    assert cols.shape[0] == NCOLA
    cst, kdec, cdec, cosT, sinT = _consts(g, S)
    pv = np.zeros((128, NPV), np.float32)

    def put(nm, vec256):
        pv[:, PV[nm]:PV[nm] + 2] = np.asarray(vec256, np.float32).reshape(2, 128).T
    ch = slice(g * 256, (g + 1) * 256)
    for j in range(4):
        put(f"cw{j}", inp["conv_w"][l][j, ch])
    put("cb", inp["conv_b"][l][ch])
    put("gbr", inp["lru_gate_b"][l][0, ch])
    put("gbi", inp["lru_gate_b"][l][1, ch])
    put("lam", inp["lru_lambda"][l][ch])
    mu = inp["shift_mu"][l]
    for j, nm in enumerate("rkvg"):
        put("mu_" + nm, mu[j * 1024 + g * 256: j * 1024 + (g + 1) * 256])
    pv[:, PV["mu_wl"]] = mu[4096:4224]
    put("w0", inp["decay_w0"][l][ch])
    put("a0", inp["iclr_a0"][l][ch])
    put("k_k", inp["k_k"][l][ch])
    put("k_a", inp["k_a"][l][ch])
    put("r_k", inp["r_k"][l].reshape(-1)[ch])
    put("lnw", inp["lnx_w"][l][ch])
    put("lnb", inp["lnx_b"][l][ch])
    pv[:, PV["kdec"]] = kdec
    pv[:, PV["cdec"]] = cdec
    z64 = np.zeros((64, 256), np.float32)
    return {
        "x": None if light else np.ascontiguousarray(inp["x"][b, :S]),
        "wA": None if light else np.ascontiguousarray(w_in[:, cols]),
        "gw": np.ascontiguousarray(inp["lru_gate_w"][l][:, g]),
        "w2p": np.concatenate([inp["decay_w2"][l][:, ch], z64], 0),
        "a2p": np.concatenate([z64, inp["iclr_a2"][l][:, ch]], 0),
        "pv": pv,
        "nwb": np.ascontiguousarray(np.broadcast_to(inp["norm_w"][l][None, :], (128, D))),
        "cst": cst, "cosT": cosT, "sinT": sinT,
    }


def emit_B(cx, NT, last, d):
    nc = cx.nc
    P = cx.P
    sb = cx.sb
    x_d, ys_d, nwb_d, fnwb_d, wM_d, bm_d, wBr_d, wO_d, id_d, sel_d, out_d = (d[k] for k in
        ("x", "ys", "nwb", "fnwb", "wM", "bm", "wBr", "wO", "ident", "sel", "out"))
    banks = cx.banks
    cur_stream = [0]
    bank_ctr = [0, 0]

    def big():
        st = cur_stream[0]
        i = 3 * st + bank_ctr[st] % 3
        bank_ctr[st] += 1
        return banks[i], f"bank{i}"
    hps_l = [banks[6][:].bitcast(BF16), banks[7][:].bitcast(BF16)]

    wM, k_wM = sb([128, 8, 3 * D], BF16, "wM")
    wBr, k_wBr = sb([128, 3, 8, D], BF16, "wBr")
    wO, k_wO = sb([128, 8, D], BF16, "wO")
    nwb, k_nwb = sb([128, D], F32, "nwb")
    fnwb, k_fnwb = sb([128, D], F32, "fnwb")
    bm, k_bm = sb([6, 512], F32, "bm")
    sel, k_sel = sb([6, 6, 128], F32, "sel")
    idb, k_idb = sb([128, 128], BF16, "idb")
    wM_v = wM_d.rearrange("(kc p) n -> p kc n", p=128)
    for kc in range(8):
        P.dma("pool", "dma_start", out=wM[:, kc, :], in_=wM_v[:, kc, :], writes=[k_wM], par=True)
    for n in range(3):
        wv = wBr_d[n].rearrange("(kc p) d -> p kc d", p=128)
        for kc in range(0, 8, 4):
            P.dma("pool", "dma_start", out=wBr[:, n, kc:kc + 4, :], in_=wv[:, kc:kc + 4, :], writes=[k_wBr], par=True)
    wv = wO_d.rearrange("(kc p) d -> p kc d", p=128)
    for kc in range(0, 8, 4):
        P.dma("pool", "dma_start", out=wO[:, kc:kc + 4, :], in_=wv[:, kc:kc + 4, :], writes=[k_wO], par=True)
    P.dma("sp", "dma_start", out=nwb[:], in_=nwb_d, writes=[k_nwb])
    P.dma("sp", "dma_start", out=fnwb[:], in_=fnwb_d, writes=[k_fnwb])
    P.dma("sp", "dma_start", out=bm[:], in_=bm_d, writes=[k_bm])
    P.dma("pool", "dma_start", out=idb[:], in_=id_d, writes=[k_idb])
    P.dma("sp", "dma_start", out=sel[:], in_=sel_d, writes=[k_sel])

    YB = 256
    ysb = [sb([128, 3, 8, YB], BF16, f"ysb{i}") for i in range(2)]
    xt = [sb([128, D], F32, f"xt{i}") for i in range(2)]
    junk, k_junk = sb([128, D], BF16, "junk")
    ss = [sb([128, 4], F32, f"ss{i}") for i in range(2)]
    xs = [sb([128, D], BF16, f"xs{i}") for i in range(2)]
    hT = [sb([128, 8, 128], BF16, f"hT{i}") for i in range(2)]
    gt = [sb([128, 512], F32, f"gt{i}") for i in range(4)]
    tmp = [sb([128, 512], F32, f"tmp{i}") for i in range(4)]
    mg = [sb([128, D], F32, f"mg{i}") for i in range(2)]
    mgb = [sb([128, D], BF16, f"mgb{i}") for i in range(2)]
    mT = [sb([128, 8, 128], BF16, f"mT{i}") for i in range(2)]
    xn = [sb([128, D], F32, f"xn{i}") for i in range(2)]
    out_toks = []

    def rms(src, k_src, ssb, k_ss, wb, k_wb, dst, k_dst):
        P.op("act", "activation", out=junk[:], in_=src[:], func=AF.Square, accum_out=ssb[:, 0:1], reads=[k_src], writes=[k_junk, k_ss])
        P.op("dve", "tensor_scalar", out=ssb[:, 1:2], in0=ssb[:, 0:1], scalar1=1.0 / D, scalar2=EPS, op0=ALU.mult, op1=ALU.add, reads=[k_ss], writes=[k_ss])
        P.op("act", "activation", out=ssb[:, 2:3], in_=ssb[:, 1:2], func=AF.Sqrt, reads=[k_ss], writes=[k_ss])
        P.op("dve", "reciprocal", out=ssb[:, 3:4], in_=ssb[:, 2:3], reads=[k_ss], writes=[k_ss])
        P.op("dve", "scalar_tensor_tensor", out=dst[:], in0=src[:], scalar=ssb[:, 3:4], in1=wb[:], op0=ALU.mult, op1=ALU.mult,
             reads=[k_src, k_ss, k_wb], writes=[k_dst])

    for tb in range(NT // YB):
        yb, k_yb = ysb[tb % 2]
        for n in range(3):
            yv = ys_d[n].rearrange("(wc p) t -> p wc t", p=128)
            P.dma("sp", "dma_start", out=yb[:, n, :, :], in_=yv[:, :, tb * YB:(tb + 1) * YB], writes=[k_yb])
        def tile_gen(tt):
            ti = tb * (YB // 128) + tt
            hps_b = hps_l[tt]
            mps_b = hps_l[tt]
            k_hps = f"bank{6 + tt}"
            r0 = ti * 128
            xb_, k_x = xt[ti % 2]
            ssb, k_ss = ss[ti % 2]
            xsb, k_xs = xs[ti % 2]
            hTt, k_hT = hT[ti % 2]
            mgt, k_mg = mg[ti % 2]
            mgbt, k_mgb = mgb[ti % 2]
            mTt, k_mT = mT[ti % 2]
            xnt, k_xn = xn[ti % 2]
            xot, k_xo = xb_, k_x
            P.dma("sp", "dma_start", out=xb_[:], in_=x_d[r0:r0 + 128, :], writes=[k_x])
            rms(xb_, k_x, ssb, k_ss, nwb, k_nwb, xsb, k_xs)
            yield
            for kc in range(8):
                P.op("pe", "transpose", hps_b[:, kc * 128:(kc + 1) * 128], xsb[:, kc * 128:(kc + 1) * 128], idb[:], reads=[k_xs, k_idb], writes=[k_hps])
            P.op("act", "activation", out=hTt[:].rearrange("p k t -> p (k t)"), in_=hps_b, func=AF.Copy, reads=[k_hps], writes=[k_hT])
            yield
            for n in range(3):
                for hf in range(2):
                    c0 = n * D + hf * 512
                    g_, k_g = gt[tt * 2 + (n * 2 + hf) % 2]
                    t_, k_t = tmp[tt * 2 + (n * 2 + hf) % 2]
                    bk, k_bk = big()
                    P.op("pe", "matmul", bk[:], lhsT=sel[:, n * 2 + hf, :], rhs=bm[:], start=True, stop=False, reads=[k_sel, k_bm], writes=[k_bk])
                    for kc in range(8):
                        P.op("pe", "matmul", bk[:], lhsT=hTt[:, kc, :], rhs=wM[:, kc, c0:c0 + 512], start=False, stop=(kc == 7), reads=[k_hT, k_wM], writes=[k_bk])
                    P.op("act", "activation", out=g_[:], in_=bk[:], func=AF.Sigmoid, reads=[k_bk], writes=[k_g])
                    yield
                    bk, k_bk = big()
                    for wc in range(8):
                        P.op("pe", "matmul", bk[:], lhsT=yb[:, n, wc, tt * 128:(tt + 1) * 128], rhs=wBr[:, n, wc, hf * 512:(hf + 1) * 512],
                             start=(wc == 0), stop=(wc == 7), reads=[k_yb, k_wBr], writes=[k_bk])
                    msl = mgt[:, hf * 512:(hf + 1) * 512]
                    if n == 0:
                        P.op("dve", "tensor_tensor", out=msl, in0=g_[:], in1=bk[:], op=ALU.mult, reads=[k_g, k_bk], writes=[k_mg])
                        yield
                    else:
                        P.op("dve", "tensor_tensor", out=t_[:], in0=g_[:], in1=bk[:], op=ALU.mult, reads=[k_g, k_bk], writes=[k_t])
                        yield
                        P.op("pool", "tensor_tensor", out=msl, in0=msl, in1=t_[:], op=ALU.add, reads=[k_mg, k_t], writes=[k_mg])
                        yield
            P.op("pool", "tensor_copy", out=mgbt[:], in_=mgt[:], reads=[k_mg], writes=[k_mgb])
            yield
            for kc in range(8):
                P.op("pe", "transpose", mps_b[:, kc * 128:(kc + 1) * 128], mgbt[:, kc * 128:(kc + 1) * 128], idb[:], reads=[k_mgb, k_idb], writes=[k_hps])
            P.op("act", "activation", out=mTt[:].rearrange("p k t -> p (k t)"), in_=mps_b, func=AF.Copy, reads=[k_hps], writes=[k_mT])
            yield
            for hf in range(2):
                bk, k_bk = big()
                for kc in range(8):
                    P.op("pe", "matmul", bk[:], lhsT=mTt[:, kc, :], rhs=wO[:, kc, hf * 512:(hf + 1) * 512], start=(kc == 0), stop=(kc == 7),
                         reads=[k_mT, k_wO], writes=[k_bk])
                P.op("dve", "tensor_tensor", out=xnt[:, hf * 512:(hf + 1) * 512], in0=xb_[:, hf * 512:(hf + 1) * 512], in1=bk[:], op=ALU.add,
                     reads=[k_x, k_bk], writes=[k_xn])
                yield
            if last:
                rms(xnt, k_xn, ssb, k_ss, fnwb, k_fnwb, xot, k_xo)
                yield
                out_toks.append(P.dma("sp", "dma_start", out=out_d[r0:r0 + 128, :], in_=xot[:], reads=[k_xo]))
            else:
                out_toks.append(P.dma("sp", "dma_start", out=out_d[r0:r0 + 128, :], in_=xnt[:], reads=[k_xn]))
            return
            yield

        gens = [tile_gen(tt) for tt in range(YB // 128)]
        gidx = {id(g): i for i, g in enumerate(gens)}
        while gens:
            for g_ in list(gens):
                cur_stream[0] = gidx[id(g_)]
                try:
                    next(g_)
                except StopIteration:
                    gens.remove(g_)
    return out_toks


def _prep_B(inp, l, xcur, ys_full, b, j, NT):
    return {
        "x": np.ascontiguousarray(xcur[b, j * NT:(j + 1) * NT]),
        "ys": np.ascontiguousarray(ys_full[b][:, :, j * NT:(j + 1) * NT]),
        "nwb": np.ascontiguousarray(np.broadcast_to(inp["norm_w"][l][None, :], (128, D))),
        "fnwb": np.ascontiguousarray(np.broadcast_to(inp["final_norm_w"][None, :], (128, D))),
        "wM": np.ascontiguousarray(inp["w_in"][l][:, OFF_M:]),
        "bm": np.ascontiguousarray(inp["b_merge"][l].reshape(6, 512)),
        "sel": np.ascontiguousarray(np.broadcast_to(np.eye(6, dtype=np.float32)[:, :, None], (6, 6, 128))),
        "wBr": np.ascontiguousarray(inp["w_branch"][l]),
        "wO": np.ascontiguousarray(inp["w_out"][l]),
        "ident": np.eye(128, dtype=np.float32),
    }


def build_A(S, flags=(1, 1, 1)):
    cx = Cx()
    d = {"x": cx.dram("x", [S, D]), "wA_cols": [(0, cx.dram("wA", [D, NCOLA]))], "gw": cx.dram("gw", [2, 256, 256]),
         "w2p": cx.dram("w2p", [128, 256]), "a2p": cx.dram("a2p", [128, 256]), "pv": cx.dram("pv", [128, NPV]),
         "nwb": cx.dram("nwb", [128, D]), "cst": cx.dram("cst", [128, NCST]), "cosT": cx.dram("cosT", [128, S]),
         "sinT": cx.dram("sinT", [128, S]), "ysT": cx.dram("ysT", [3, 256, S], BF16, "ExternalOutput")}
    toks = emit_A(cx, S, d, flags)
    cx.P.finish(toks)
    return cx.nc


def build_B(NT, last):
    cx = Cx()
    d = {"x": cx.dram("x", [NT, D]), "ys": cx.dram("ys", [3, D, NT], BF16), "nwb": cx.dram("nwb", [128, D]),
         "fnwb": cx.dram("fnwb", [128, D]), "wM": cx.dram("wM", [D, 3 * D]), "bm": cx.dram("bm", [6, 512]),
         "wBr": cx.dram("wBr", [3, D, D]), "wO": cx.dram("wO", [D, D]), "ident": cx.dram("ident", [128, 128]),
         "sel": cx.dram("sel", [6, 6, 128]), "out": cx.dram("out", [NT, D], F32, "ExternalOutput")}
    toks = emit_B(cx, NT, last, d)
    cx.P.finish(toks)
    return cx.nc


_COLS = [(0, 0), (256, OFF_GA), (512, OFF_B), (768, OFF_B + 1024), (1024, OFF_B + 2048), (1280, OFF_B + 3072),
         (1664, OFF_C), (1920, OFF_C + 1024), (2176, OFF_C + 2048), (2432, OFF_C + 3072)]


def build_fused(S, groups=(0, 1, 2, 3)):
    cx = Cx()
    nc, P = cx.nc, cx.P
    x_in = cx.dram("x", [S, D])
    w_in = cx.dram("w_in", [DEPTH, D, N_IN])
    gw = cx.dram("gw", [DEPTH, 4, 2, 256, 256])
    w2p = cx.dram("w2p", [DEPTH, 4, 128, 256])
    a2p = cx.dram("a2p", [DEPTH, 4, 128, 256])
    pv = cx.dram("pv", [DEPTH, 4, 128, NPV])
    nwb = cx.dram("nwb", [DEPTH, 128, D])
    fnwb = cx.dram("fnwb", [128, D])
    cst = cx.dram("cst", [4, 128, NCST])
    cosT = cx.dram("cosT", [128, S])
    sinT = cx.dram("sinT", [128, S])
    bm = cx.dram("bm", [DEPTH, 6, 512])
    sel = cx.dram("sel", [6, 6, 128])
    wBr = cx.dram("wBr", [DEPTH, 3, D, D])
    wO = cx.dram("wO", [DEPTH, D, D])
    ident = cx.dram("ident", [128, 128])
    out = cx.dram("out", [S, D], F32, "ExternalOutput")
    ys_scr = nc.dram_tensor("ys_scr", [3, D, S], BF16, kind="Internal").ap()
    x_scr = nc.dram_tensor("x_scr", [S, D], F32, kind="Internal").ap()
    toks = []
    for l in range(DEPTH):
        x_src = x_in if l == 0 else x_scr
        for g in groups:
            cx.reset()
            cols = [(c0, w_in[l][:, off + g * 256: off + (g + 1) * 256]) for c0, off in _COLS]
            cols.append((1536, w_in[l][:, OFF_B + 4096: OFF_B + 4224]))
            d = {"x": x_src, "wA_cols": cols, "gw": gw[l, g], "w2p": w2p[l, g], "a2p": a2p[l, g], "pv": pv[l, g],
                 "nwb": nwb[l], "cst": cst[g], "cosT": cosT, "sinT": sinT, "ysT": ys_scr[:, g * 256:(g + 1) * 256, :]}
            for t in emit_A(cx, S, d):
                P.wait_tok("sp", t)
            P.barrier()
        cx.reset()
        last = (l == DEPTH - 1)
        d = {"x": x_src, "ys": ys_scr, "nwb": nwb[l], "fnwb": fnwb, "wM": w_in[l][:, OFF_M:], "bm": bm[l], "wBr": wBr[l],
             "wO": wO[l], "ident": ident, "sel": sel, "out": out if last else x_scr}
        toks = emit_B(cx, S, last, d)
        for t in toks:
            P.wait_tok("sp", t)
        P.barrier()
    P.finish(toks)
    return nc


def _prep_fused(inp, b, S):
    pvs = np.zeros((DEPTH, 4, 128, NPV), np.float32)
    csts = np.zeros((4, 128, NCST), np.float32)
    w2 = np.zeros((DEPTH, 4, 128, 256), np.float32)
    a2 = np.zeros((DEPTH, 4, 128, 256), np.float32)
    gws = np.zeros((DEPTH, 4, 2, 256, 256), np.float32)
    cosT = sinT = None
    for l in range(DEPTH):
        for g in range(4):
            m = _prep_A(inp, l, b, g, S, light=True)
            pvs[l, g] = m["pv"]
            w2[l, g] = m["w2p"]
            a2[l, g] = m["a2p"]
            gws[l, g] = m["gw"]
            csts[g] = m["cst"]
            cosT, sinT = m["cosT"], m["sinT"]
    return {
        "x": np.ascontiguousarray(inp["x"][b, :S]), "w_in": np.ascontiguousarray(inp["w_in"]), "gw": gws, "w2p": w2, "a2p": a2,
        "pv": pvs, "nwb": np.ascontiguousarray(np.broadcast_to(inp["norm_w"][:, None, :], (DEPTH, 128, D))),
        "fnwb": np.ascontiguousarray(np.broadcast_to(inp["final_norm_w"][None, :], (128, D))),
        "cst": csts, "cosT": cosT, "sinT": sinT, "bm": np.ascontiguousarray(inp["b_merge"].reshape(DEPTH, 6, 512)),
        "sel": np.ascontiguousarray(np.broadcast_to(np.eye(6, dtype=np.float32)[:, :, None], (6, 6, 128))),
        "wBr": np.ascontiguousarray(inp["w_branch"]), "wO": np.ascontiguousarray(inp["w_out"]),
        "ident": np.eye(128, dtype=np.float32),
    }


def _forward_unfused(inp, S):
    NT = S // 4
    xcur = np.ascontiguousarray(inp["x"][:, :S]).astype(np.float32)
    for l in range(DEPTH):
        ncA = build_A(S)
        inp_l = dict(inp)
        inp_l["x"] = xcur
        in_maps = [_prep_A(inp_l, l, c // 4, c % 4, S) for c in range(8)]
        res = run_bass_kernel_spmd(ncA, in_maps, core_ids=list(range(8)))
        ys_full = [np.concatenate([res.results[b * 4 + g]["ysT"] for g in range(4)], axis=1) for b in range(2)]
        ncB = build_B(NT, last=(l == DEPTH - 1))
        in_maps = [_prep_B(inp, l, xcur, ys_full, c // 4, c % 4, NT) for c in range(8)]
        res = run_bass_kernel_spmd(ncB, in_maps, core_ids=list(range(8)))
        xcur = np.stack([np.concatenate([res.results[b * 4 + j]["out"] for j in range(4)], axis=0) for b in range(2)], axis=0)
    return xcur


def _forward_fused(inp, S):
    nc = build_fused(S)
    maps = [_prep_fused(inp, b, S) for b in range(2)]
    in_maps = [maps[c // 4] for c in range(8)]
    res = run_bass_kernel_spmd(nc, in_maps, core_ids=list(range(8)))
    return np.stack([res.results[0]["out"], res.results[4]["out"]], axis=0)


def kernel(**inputs):
    inp = {k: np.asarray(v) for k, v in inputs.items()}
    return _forward_unfused(inp, inp["x"].shape[1]).astype(np.float32)
```
